# Optimizing a Trainium2 kernel written in Bass

```python
import math
import jax, jax.numpy as jnp
from jax import lax
import numpy as np

D_MODEL = 1024
BATCH = 32
SEQ = 2048
DEPTH = 4

GRID_W = 64
ROPE_THETA = 10000.0
Q_BLOCK = 128
CHUNK = 128
LN_EPS = 1e-5

A_HEADS = 4
A_QK_DIM = 32
A_V_DIM = 64
A_WIDTH = A_HEADS * A_V_DIM
B_HEADS = 8
B_KV_HEADS = 2
B_HEAD_DIM = 64
B_GROUP = B_HEADS // B_KV_HEADS
B_WIDTH = B_HEADS * B_HEAD_DIM
C_GROUPS = 4
C_GROUP_DIM = 64
C_WIDTH = C_GROUPS * C_GROUP_DIM
MIX_WIDTH = A_WIDTH + B_WIDTH + C_WIDTH

A_QK_COLS = A_HEADS * 2 * A_QK_DIM
B_KV_COLS = B_KV_HEADS * B_HEAD_DIM
IN_SPLITS = (A_QK_COLS, A_QK_COLS, A_WIDTH, B_WIDTH, B_KV_COLS, B_KV_COLS, C_WIDTH, C_WIDTH)
IN_COLS = sum(IN_SPLITS)
IN_OFFSETS = tuple(int(o) for o in np.cumsum(IN_SPLITS)[:-1])

FFN_HIDDEN = -(-8 * D_MODEL // (3 * 256)) * 256
DEEPNORM_ALPHA = (2 * DEPTH) ** 0.25
DEEPNORM_BETA = (8 * DEPTH) ** -0.25

kernel_name = 'hymba_style_diffattn_gqa_axial_gmlp_deepnorm_encoder'


def layer_norm(x, g, b):
    xf = x.astype(jnp.float32)
    mu = jnp.mean(xf, axis=-1, keepdims=True)
    var = jnp.mean(jnp.square(xf - mu), axis=-1, keepdims=True)
    return ((xf - mu) * lax.rsqrt(var + LN_EPS) * g.astype(jnp.float32) + b.astype(jnp.float32)).astype(x.dtype)


def rms_norm(x, g):
    xf = x.astype(jnp.float32)
    ms = jnp.mean(jnp.square(xf), axis=-1, keepdims=True)
    return (xf * lax.rsqrt(ms + LN_EPS) * g.astype(jnp.float32)).astype(x.dtype)


def rope_cos_sin(pos, dim):
    inv = 1.0 / (ROPE_THETA ** (jnp.arange(0, dim, 2, dtype=jnp.float32) / dim))
    ang = pos.astype(jnp.float32)[:, None] * inv[None, :]
    ang = jnp.concatenate([ang, ang], axis=-1)
    return jnp.cos(ang), jnp.sin(ang)


def rotate_half(x):
    x1, x2 = jnp.split(x, 2, axis=-1)
    return jnp.concatenate([-x2, x1], axis=-1)


def apply_rope(x, cos, sin):
    shape = (cos.shape[0],) + (1,) * (x.ndim - 3) + (cos.shape[1],)
    cos = cos.reshape(shape)
    sin = sin.reshape(shape)
    return (x * cos + rotate_half(x) * sin).astype(x.dtype)


def apply_axial_rope(x, cos_r, sin_r, cos_c, sin_c):
    x_row, x_col = jnp.split(x, 2, axis=-1)
    return jnp.concatenate([apply_rope(x_row, cos_r, sin_r), apply_rope(x_col, cos_c, sin_c)], axis=-1)


def sweep_query_blocks(fn, q):
    bsz, s = q.shape[:2]
    nb = s // Q_BLOCK
    qb = jnp.moveaxis(q.reshape((bsz, nb, Q_BLOCK) + q.shape[2:]), 1, 0)
    out = lax.map(fn, qb)
    out = jnp.moveaxis(out, 0, 1)
    return out.reshape((bsz, s) + out.shape[3:])


def diff_attention(q, k, v, lam):
    scale = A_QK_DIM ** -0.5

    def block(qb):
        s = jnp.einsum('bqhcd,bkhcd->bhcqk', qb, k).astype(jnp.float32) * scale
        p = jax.nn.softmax(s, axis=-1)
        w = (p[:, :, 0] - lam * p[:, :, 1]).astype(v.dtype)
        return jnp.einsum('bhqk,bkhd->bqhd', w, v)

    return sweep_query_blocks(block, q)


def gqa_attention(q, k, v):
    scale = B_HEAD_DIM ** -0.5

    def block(qb):
        s = jnp.einsum('bqhgd,bkhd->bhgqk', qb, k).astype(jnp.float32) * scale
        p = jax.nn.softmax(s, axis=-1).astype(v.dtype)
        return jnp.einsum('bhgqk,bkhd->bqhgd', p, v)

    return sweep_query_blocks(block, q)


def chunked_spatial_gating(u, v, w_s, b_s, g, beta):
    bsz, s, _ = v.shape
    v = layer_norm(v, g, beta)
    vc = v.reshape(bsz, s // CHUNK, CHUNK, C_GROUPS, C_GROUP_DIM)
    mixed = jnp.einsum('gpq,bnqgd->bnpgd', w_s, vc) + b_s.T[None, None, :, :, None]
    return u * mixed.reshape(bsz, s, C_WIDTH)


def setup_inputs(seed: int = 0) -> dict:
    key = jax.random.key(seed)
    ks = jax.random.split(key, 20)
    nrm = jax.random.normal
    f32 = jnp.float32

    def gain(k, shape):
        return 1.0 + 0.02 * nrm(k, shape, f32)

    def bias(k, shape):
        return 0.02 * nrm(k, shape, f32)

    return {
        'x': nrm(ks[0], (BATCH, SEQ, D_MODEL), f32),
        'w_in': nrm(ks[1], (DEPTH, D_MODEL, IN_COLS), f32) * D_MODEL ** -0.5,
        'w_out': nrm(ks[2], (DEPTH, MIX_WIDTH, D_MODEL), f32) * (MIX_WIDTH ** -0.5 * DEEPNORM_BETA),
        'lam_qk': 0.1 * nrm(ks[3], (DEPTH, 4, A_QK_DIM), f32),
        'a_subln_g': gain(ks[4], (DEPTH, A_V_DIM)),
        'b_q_norm_g': gain(ks[5], (DEPTH, B_HEAD_DIM)),
        'b_k_norm_g': gain(ks[6], (DEPTH, B_HEAD_DIM)),
        'c_ln_g': gain(ks[7], (DEPTH, C_WIDTH)),
        'c_ln_b': bias(ks[8], (DEPTH, C_WIDTH)),
        'c_w_s': nrm(ks[9], (DEPTH, C_GROUPS, CHUNK, CHUNK), f32) * CHUNK ** -0.5,
        'c_b_s': gain(ks[10], (DEPTH, C_GROUPS, CHUNK)),
        'ln1_g': gain(ks[11], (DEPTH, D_MODEL)),
        'ln1_b': bias(ks[12], (DEPTH, D_MODEL)),
        'w_gate': nrm(ks[13], (DEPTH, D_MODEL, FFN_HIDDEN), f32) * D_MODEL ** -0.5,
        'w_up': nrm(ks[14], (DEPTH, D_MODEL, FFN_HIDDEN), f32) * D_MODEL ** -0.5,
        'w_down': nrm(ks[15], (DEPTH, FFN_HIDDEN, D_MODEL), f32) * (FFN_HIDDEN ** -0.5 * DEEPNORM_BETA),
        'ln2_g': gain(ks[16], (DEPTH, D_MODEL)),
        'ln2_b': bias(ks[17], (DEPTH, D_MODEL)),
    }


def reference(x, w_in, w_out, lam_qk, a_subln_g, b_q_norm_g, b_k_norm_g, c_ln_g, c_ln_b,
              c_w_s, c_b_s, ln1_g, ln1_b, w_gate, w_up, w_down, ln2_g, ln2_b):
    bsz, s, _ = x.shape
    rows = s // GRID_W
    t = jnp.arange(s, dtype=jnp.int32)
    row_id = jnp.repeat(jnp.arange(rows, dtype=jnp.int32), GRID_W)
    col_id = jnp.tile(jnp.arange(GRID_W, dtype=jnp.int32), rows)
    cos_a, sin_a = rope_cos_sin(t, A_QK_DIM)
    cos_r, sin_r = rope_cos_sin(row_id, B_HEAD_DIM // 2)
    cos_c, sin_c = rope_cos_sin(col_id, B_HEAD_DIM // 2)

    for l in range(DEPTH):
        lam_init = 0.8 - 0.6 * math.exp(-0.3 * l)
        h = jnp.einsum('bsd,de->bse', x, w_in[l])
        a_q, a_k, a_v, b_q, b_k, b_v, c_u, c_v = jnp.split(h, IN_OFFSETS, axis=-1)

        a_q = apply_rope(a_q.reshape(bsz, s, A_HEADS, 2, A_QK_DIM), cos_a, sin_a)
        a_k = apply_rope(a_k.reshape(bsz, s, A_HEADS, 2, A_QK_DIM), cos_a, sin_a)
        a_v = a_v.reshape(bsz, s, A_HEADS, A_V_DIM)
        lq = lam_qk[l].astype(jnp.float32)
        lam = jnp.exp(jnp.sum(lq[0] * lq[1])) - jnp.exp(jnp.sum(lq[2] * lq[3])) + lam_init
        a_o = diff_attention(a_q, a_k, a_v, lam)
        a_o = (rms_norm(a_o, a_subln_g[l]) * (1.0 - lam_init)).reshape(bsz, s, A_WIDTH)

        b_q = rms_norm(b_q.reshape(bsz, s, B_KV_HEADS, B_GROUP, B_HEAD_DIM), b_q_norm_g[l])
        b_q = apply_axial_rope(b_q, cos_r, sin_r, cos_c, sin_c)
        b_k = rms_norm(b_k.reshape(bsz, s, B_KV_HEADS, B_HEAD_DIM), b_k_norm_g[l])
        b_k = apply_axial_rope(b_k, cos_r, sin_r, cos_c, sin_c)
        b_v = b_v.reshape(bsz, s, B_KV_HEADS, B_HEAD_DIM)
        b_o = gqa_attention(b_q, b_k, b_v).reshape(bsz, s, B_WIDTH)

        c_o = chunked_spatial_gating(jax.nn.gelu(c_u), jax.nn.gelu(c_v),
                                     c_w_s[l], c_b_s[l], c_ln_g[l], c_ln_b[l])

        mix = jnp.einsum('bse,ed->bsd', jnp.concatenate([a_o, b_o, c_o], axis=-1), w_out[l])
        x = layer_norm(DEEPNORM_ALPHA * x + mix, ln1_g[l], ln1_b[l])

        hid = jax.nn.silu(jnp.einsum('bsd,df->bsf', x, w_gate[l])) * jnp.einsum('bsd,df->bsf', x, w_up[l])
        ffn = jnp.einsum('bsf,fd->bsd', hid, w_down[l])
        x = layer_norm(DEEPNORM_ALPHA * x + ffn, ln2_g[l], ln2_b[l])
    return x
```

```python
import math, os
CUT = int(os.environ.get('KB_CUT', '99'))
USE_CV = False
import numpy as np
from contextlib import ExitStack
import concourse.bass as bass
import concourse.mybir as mybir

from concourse.bass_utils import run_bass_kernel_spmd


ENGS = ("pe", "act", "dve", "pool", "sp")


class T:
    __slots__ = ("name", "w", "r")

    def __init__(self, name=""):
        self.name = name
        self.w = None
        self.r = []


class Op:
    __slots__ = ("fn", "deps", "signal", "dma_sem", "dma_val")

    def __init__(self, fn, deps, dma_sem=None, dma_val=0):
        self.fn = fn
        self.deps = deps
        self.signal = False
        self.dma_sem = dma_sem
        self.dma_val = dma_val


class Prog:
    def __init__(self, nc):
        self.nc = nc
        self.ops = {e: [] for e in ENGS}
        self.dma_counts = {}
        self.n_wait = 0

    def _deps(self, eng, reads, writes):
        deps = set()
        for t in reads:
            if t.w is not None:
                deps.add(t.w)
        for t in writes:
            if t.w is not None:
                deps.add(t.w)
            for x in t.r:
                deps.add(x)
        best = {}
        for d in deps:
            if d[0] == "c" and d[1] == eng and eng == "pe":
                continue
            k = (d[0], d[1])
            if k not in best or best[k][2] < d[2]:
                best[k] = d
        return list(best.values())

    def add(self, eng, fn, reads=(), writes=()):
        deps = self._deps(eng, reads, writes)
        idx = len(self.ops[eng])
        self.ops[eng].append(Op(fn, deps))
        tok = ("c", eng, idx)
        for t in reads:
            t.r.append(tok)
        for t in writes:
            t.w = tok
            t.r = []
        return tok

    def dma(self, eng, fn, sem_key, reads=(), writes=()):
        deps = self._deps(eng, reads, writes)
        n = self.dma_counts.get(sem_key, 0) + 1
        self.dma_counts[sem_key] = n
        self.ops[eng].append(Op(fn, deps, dma_sem=sem_key, dma_val=16 * n))
        tok = ("d", sem_key, 16 * n)
        for t in reads:
            t.r.append(tok)
        for t in writes:
            t.w = tok
            t.r = []
        return tok

    def barrier(self):
        toks = []
        for e in ENGS:
            for i in range(len(self.ops[e]) - 1, -1, -1):
                op = self.ops[e][i]
                if op.fn is not None and op.dma_sem is None:
                    toks.append(("c", e, i))
                    break
        for k, n in self.dma_counts.items():
            if not str(k).startswith("cv"):
                toks.append(("d", k, 16 * n))
        for e in ENGS:
            self.wait_tokens(e, [t for t in toks if not (t[0] == "c" and t[1] == e and e == "pe")])

    def wait_tokens(self, eng, toks):
        self.ops[eng].append(Op(None, list(toks)))

    def finalize_and_emit(self, stack):
        nc = self.nc
        for e in ENGS:
            for op in self.ops[e]:
                for d in op.deps:
                    if d[0] == "c":
                        self.ops[d[1]][d[2]].signal = True
        val = {}
        for e in ENGS:
            c = 0
            for i, op in enumerate(self.ops[e]):
                if op.signal:
                    c += 1
                    val[(e, i)] = c
            assert c < 60000, (e, c)
        csem = {e: stack.enter_context(nc.semaphore("cs_" + e)) for e in ENGS}
        dsem = {k: stack.enter_context(nc.semaphore("ds_" + str(k))) for k in self.dma_counts}
        handles = {"pe": "tensor", "act": "scalar", "dve": "vector", "pool": "gpsimd", "sp": "sync"}
        block = stack.enter_context(nc.Block())
        prog = self

        def make(e):
            def body(eng):
                waited = {}
                for i, op in enumerate(prog.ops[e]):
                    need = {}
                    for d in op.deps:
                        if d[0] == "c":
                            key = ("c", d[1])
                            v = val[(d[1], d[2])]
                        else:
                            key = ("d", d[1])
                            v = d[2]
                        if waited.get(key, 0) >= v:
                            continue
                        if need.get(key, 0) < v:
                            need[key] = v
                    for key, v in need.items():
                        sem = csem[key[1]] if key[0] == "c" else dsem[key[1]]
                        eng.wait_ge(sem, v)
                        waited[key] = v
                        prog.n_wait += 1
                    if op.fn is None:
                        continue
                    ins = op.fn(eng)
                    if op.dma_sem is not None:
                        ins.then_inc(dsem[op.dma_sem], 16)
                    elif op.signal:
                        ins.then_inc(csem[e], 1)
            return body

        for e in ENGS:
            if not self.ops[e]:
                continue
            getattr(block, handles[e])(make(e))


F32 = mybir.dt.float32
BF16 = mybir.dt.bfloat16
AF = mybir.ActivationFunctionType
ALU = mybir.AluOpType
AX = mybir.AxisListType

D = 1024
S = 2048
NT = 16
HID = 2816
NFC = 22
EPS = 1e-5
ALPHA = (2 * 4) ** 0.25
KB = 1024


def host_consts():
    inv = 1.0 / (10000.0 ** (np.arange(0, 32, 2, dtype=np.float32) / 32.0))
    t = np.arange(S, dtype=np.float32)

    def cs(pos):
        ang = pos[:, None] * inv[None, :]
        ang = np.concatenate([ang, ang], -1)
        c = np.cos(ang).astype(np.float32)
        s = np.sin(ang).astype(np.float32)
        s[:, :16] *= -1.0
        return c, s
    cA, sA = cs(t)
    row = np.floor(t / 64.0).astype(np.float32)
    col = (t - 64.0 * row).astype(np.float32)
    cR, sR = cs(row)
    cC, sC = cs(col)
    cB = np.concatenate([cR, cC], -1)
    sB = np.concatenate([sR, sC], -1)

    def lay(a):
        n = a.shape[1]
        return np.ascontiguousarray(a.reshape(NT, 128, n).transpose(1, 0, 2).reshape(128, NT * n))
    tabs = np.concatenate([lay(cA), lay(sA), lay(cB), lay(sB)], 1).astype(np.float32)
    ident = np.eye(128, dtype=np.float32)
    bd = np.zeros((128, 128), np.float32)
    bd[:64, :64] = 1.0 / 64
    bd[64:, 64:] = 1.0 / 64
    return dict(tabs=tabs, ident=ident, bd=bd)


def build(nseq, depth, stop=None):
    nc = bass.Bass("TRN2", target_bir_lowering=False)
    L = depth

    def din(name, shape):
        return nc.dram_tensor(name, shape, F32, kind="ExternalInput").ap()
    x_d = din("x", [nseq, S, D])
    w_in_d = din("w_in", [L, D, 2048])
    w_out_d = din("w_out", [L, D, D])
    lam_d = din("lam_qk", [L, 4, 32])
    asg_d = din("a_subln_g", [L, 64])
    bqg_d = din("b_q_norm_g", [L, 64])
    bkg_d = din("b_k_norm_g", [L, 64])
    clg_d = din("c_ln_g", [L, 256])
    clb_d = din("c_ln_b", [L, 256])
    cws_d = din("c_w_s", [L, 4, 128, 128])
    cbs_d = din("c_b_s", [L, 4, 128])
    l1g_d = din("ln1_g", [L, D])
    l1b_d = din("ln1_b", [L, D])
    wg_d = din("w_gate", [L, D, HID])
    wu_d = din("w_up", [L, D, HID])
    wd_d = din("w_down", [L, HID, D])
    l2g_d = din("ln2_g", [L, D])
    l2b_d = din("ln2_b", [L, D])
    tabs_d = din("tabs", [128, 3072])
    ident_d = din("ident", [128, 128])
    bd_d = din("bd", [128, 128])
    out_d = nc.dram_tensor("out", [nseq, S, D], F32, kind="ExternalOutput").ap()
    if USE_CV:
        wg_bf = nc.dram_tensor("wg_bf", [L, 11, 128, 2048], BF16, kind="Internal").ap()
        wu_bf = nc.dram_tensor("wu_bf", [L, 11, 128, 2048], BF16, kind="Internal").ap()
        wd_bf = nc.dram_tensor("wd_bf", [L, 6, 128, 4096], BF16, kind="Internal").ap()

    st = ExitStack()

    def sb(name, shape, dt):
        return st.enter_context(nc.sbuf_tensor(name, shape, dt))
    X = sb("X", [128, NT, D], F32)
    REG = sb("REG", [128, 80 * KB // 4], F32)
    AR = sb("AR", [128, 57088 // 4], F32)
    ident = sb("identb", [128, 128], BF16)
    bdm = sb("bdm", [128, 128], BF16)
    Ws = sb("Ws", [128, 4, 128], BF16)
    WsT = sb("WsT", [128, 4, 128], BF16)
    gq_b = sb("gq_b", [128, 64], F32)
    gk_b = sb("gk_b", [128, 64], F32)
    cg_b = sb("cg_b", [128, 256], F32)
    cb_b = sb("cb_b", [128, 256], F32)
    bsT = sb("bsT", [128, 4], F32)
    lamq = sb("lamq", [128, 128], F32)
    gA2 = sb("gA2", [128, 1], F32)
    sm = sb("sm", [128, 128], F32)
    PR = sb("PR", [128, 64], F32)
    epsT = sb("epsT", [128, 1], F32)
    PS = [st.enter_context(nc.psum_tensor("PS%d" % i, [128, 1024], F32)) for i in range(4)]

    def view(base, off, shape, dt):
        esz = 2 if dt == BF16 else 4
        n = int(np.prod(shape[1:]))
        nb = n * esz
        assert off % 4 == 0 and nb % 4 == 0
        ap = base[:, off // 4:(off + nb) // 4]
        if dt == BF16:
            ap = ap.bitcast(BF16)
        if len(shape) == 3:
            ap = ap.rearrange("p (a b) -> p a b", a=shape[1])
        return ap

    AqT = view(REG, 0, [128, 2, S], BF16)
    AkT = view(REG, 8 * KB, [128, 2, S], BF16)
    BqT = view(REG, 16 * KB, [128, 4, S], BF16)
    BkT = view(REG, 32 * KB, [128, S], BF16)
    VX = view(REG, 36 * KB, [128, NT, 896], BF16)
    mTC = view(REG, 64 * KB, [128, 2, S], BF16)
    xTt = [view(REG, 75 * KB + i * 2 * KB, [128, 8, 128], BF16) for i in range(2)]
    hidT = view(REG, 0, [128, NFC, 512], BF16)
    xTf = view(REG, 22 * KB, [128, 8, 512], BF16)
    WG = [view(REG, 30 * KB + i * 4 * KB, [128, 8, 256], BF16) for i in range(2)]
    WU = [view(REG, 38 * KB + i * 4 * KB, [128, 8, 256], BF16) for i in range(2)]
    WD = [view(REG, 46 * KB + i * 8 * KB, [128, 4, 1024], BF16) for i in range(2)]
    XBc = view(REG, 62 * KB, [128, 1024], BF16)
    SG = [view(REG, 64 * KB + i * 2 * KB, [128, 512], F32) for i in range(2)]
    Yc = view(REG, 68 * KB, [128, 1024], F32)
    g2_b = view(REG, 72 * KB, [128, 1024], F32)
    b2_b = view(REG, 76 * KB, [128, 1024], F32)
    WIN = view(AR, 0, [128, 8, 1024], BF16)
    cosA = view(AR, 16 * KB, [128, NT, 32], F32)
    sinA = view(AR, 18 * KB, [128, NT, 32], F32)
    cosB = view(AR, 20 * KB, [128, NT, 64], F32)
    sinB = view(AR, 24 * KB, [128, NT, 64], F32)
    TA = [view(AR, 28 * KB + i * 2 * KB, [128, 512], F32) for i in range(2)]
    TB = [view(AR, 32 * KB + i * 2 * KB, [128, 512], F32) for i in range(2)]
    TC = [view(AR, 36 * KB + i * 2 * KB, [128, 512], F32) for i in range(2)]
    TD = [view(AR, 40 * KB + i * 2 * KB, [128, 512], F32) for i in range(2)]
    TE = [view(AR, 44 * KB + i * 2 * KB, [128, 512], F32) for i in range(2)]
    OBA = [view(AR, 48 * KB + i * KB, [128, 512], BF16) for i in range(2)]
    OBB = [view(AR, 50 * KB + i * 1280, [128, 640], BF16) for i in range(2)]
    XB = view(AR, 50 * KB + 2560, [128, 1024], BF16)
    VLN = [view(REG, 72 * KB + i * 512, [128, 256], BF16) for i in range(2)]
    CO = [view(REG, 73 * KB + i * 512, [128, 256], BF16) for i in range(2)]
    WOUT = view(AR, 0, [128, 8, 1024], BF16)
    mTAB = view(AR, 16 * KB, [128, 6, 1024], BF16)
    PT = [view(AR, 28 * KB + i * 2 * KB, [128, 1024], BF16) for i in range(2)]
    Rt = view(AR, 32 * KB, [128, 1024], F32)
    Tt = view(AR, 36 * KB, [128, 1024], F32)
    OP = view(AR, 40 * KB, [128, 1024], F32)
    SQ = view(AR, 44 * KB, [128, 1024], BF16)
    Yb = view(AR, 46 * KB, [128, 1024], F32)
    QM = [view(AR, 50 * KB + i * 2 * KB, [128, 1024], BF16) for i in range(2)]
    g1_b = view(AR, 44 * KB, [128, 1024], F32)
    b1_b = view(AR, 48 * KB, [128, 1024], F32)

    P = Prog(nc)
    tX = [T("x%d" % i) for i in range(NT)]
    tPS = [T("ps%d" % i) for i in range(8)]

    def psb(i):
        return PS[i // 2][:, (i % 2) * 512:(i % 2) * 512 + 512]

    def psb_bf(i):
        return PS[i // 2][:, (i % 2) * 512:(i % 2) * 512 + 512].bitcast(BF16)
    tConst = T("const")
    tPar = T("par")
    tWs = T("Ws")
    tTab = T("tab")
    tWIN = [T("win%d" % g) for g in range(2)]

    def mm(out, lhsT, rhs, start, stop, reads, writes, tp=None):
        if tp is None:
            P.add("pe", lambda e: e.matmul(out, lhsT=lhsT, rhs=rhs, start=start, stop=stop), reads, writes)
        else:
            P.add("pe", lambda e: e.matmul(out, lhsT=lhsT, rhs=rhs, start=start, stop=stop, tile_position=tp), reads, writes)

    def tr(out, in_, reads, writes):
        P.add("pe", lambda e: e.transpose(out=out, in_=in_, identity=ident[:]), list(reads) + [tConst], writes)

    def act(out, in_, func, reads, writes, scale=None, bias=None):
        kw = {}
        if scale is not None:
            kw["scale"] = scale
        if bias is not None:
            kw["bias"] = bias
        P.add("act", lambda e: e.activation(out=out, in_=in_, func=func, **kw), reads, writes)

    def tt_(eng, out, in0, in1, op, reads, writes):
        P.add(eng, lambda e: e.tensor_tensor(out=out, in0=in0, in1=in1, op=op), reads, writes)

    def ts_(eng, out, in0, s1, s2, op0, op1, reads, writes):
        if op1 is None:
            P.add(eng, lambda e: e.tensor_scalar(out=out, in0=in0, scalar1=s1, scalar2=None, op0=op0), reads, writes)
        else:
            P.add(eng, lambda e: e.tensor_scalar(out=out, in0=in0, scalar1=s1, scalar2=s2, op0=op0, op1=op1), reads, writes)

    def stt_(eng, out, in0, scalar, in1, op0, op1, reads, writes):
        P.add(eng, lambda e: e.scalar_tensor_tensor(out=out, in0=in0, scalar=scalar, in1=in1, op0=op0, op1=op1), reads, writes)

    def cp(eng, out, in_, reads, writes):
        if eng == "act":
            P.add("act", lambda e: e.copy(out=out, in_=in_), reads, writes)
        else:
            P.add(eng, lambda e: e.tensor_copy(out=out, in_=in_), reads, writes)

    def dma(eng, out, in_, key, reads, writes, slow=False):
        if slow:
            P.dma(eng, lambda e: e.dma_start(out=out, in_=in_, allow_slow_non_contiguous=True), key, reads, writes)
        else:
            P.dma(eng, lambda e: e.dma_start(out=out, in_=in_), key, reads, writes)

    def rstd_chain(dst, src, n, scale, reads_t):
        ts_("dve", dst, src, scale, EPS, ALU.mult, ALU.add, [reads_t], [reads_t])
        act(dst, dst, AF.Sqrt, [reads_t], [reads_t])
        P.add("dve", lambda e: e.reciprocal(out=dst, in_=dst), [reads_t], [reads_t])

    dma("pool", ident[:], ident_d[:, :], "c0", [], [tConst])
    dma("pool", bdm[:], bd_d[:, :], "c1", [], [tConst])
    P.add("dve", lambda e: e.memset(epsT[:], EPS), [], [tConst])

    def load_tables():
        dma("sp", cosA.rearrange("p a b -> p (a b)"), tabs_d[:, 0:512], "tab", [], [tTab])
        dma("sp", sinA.rearrange("p a b -> p (a b)"), tabs_d[:, 512:1024], "tab", [], [tTab])
        dma("sp", cosB.rearrange("p a b -> p (a b)"), tabs_d[:, 1024:2048], "tab", [], [tTab])
        dma("sp", sinB.rearrange("p a b -> p (a b)"), tabs_d[:, 2048:3072], "tab", [], [tTab])

    def load_win(l, half):
        src = w_in_d[l].rearrange("(kc p) c -> p kc c", p=128)
        if half == 0:
            lst = ((0, 512, 0, 0), (768, 1280, 512, 1))
        else:
            lst = ((1280, 1408, 0, 0), (512, 768, 128, 0), (1408, 1536, 384, 0), (1536, 2048, 512, 1))
        for (s0, s1, d0, g) in lst:
            dma("pool", WIN[:, :, d0:d0 + (s1 - s0)], src[:, :, s0:s1], "win%d" % g, [], [tWIN[g]])

    def load_params(l):
        dma("sp", gq_b[:], bqg_d[l:l + 1, :].broadcast_to([128, 64]), "par", [], [tPar])
        dma("sp", gk_b[:], bkg_d[l:l + 1, :].broadcast_to([128, 64]), "par", [], [tPar])
        dma("sp", cg_b[:], clg_d[l:l + 1, :].broadcast_to([128, 256]), "par", [], [tPar])
        dma("sp", cb_b[:], clb_d[l:l + 1, :].broadcast_to([128, 256]), "par", [], [tPar])
        dma("sp", bsT[:], cbs_d[l].rearrange("g p -> p g"), "par", [], [tPar], slow=True)
        dma("sp", lamq[:], lam_d[l:l + 1].rearrange("o a b -> o (a b)").broadcast_to([128, 128]), "par", [], [tPar])
        dma("sp", gA2[0:64, :], asg_d[l].rearrange("(d o) -> d o", o=1), "par", [], [tPar])
        dma("sp", gA2[64:128, :], asg_d[l].rearrange("(d o) -> d o", o=1), "par", [], [tPar])
        dma("pool", Ws[:], cws_d[l].rearrange("g p q -> p g q"), "ws", [], [tWs])

    tCV = [T("cv%d" % l) for l in range(L)]

    def convert_layer(l):
        wgs = wg_d[l].rearrange("(kc p) f -> p kc f", p=128)
        wus = wu_d[l].rearrange("(kc p) f -> p kc f", p=128)
        wds = wd_d[l].rearrange("(fc p) d -> p fc d", p=128)
        for fb in range(11):
            dma("pool", wg_bf[l, fb].rearrange("p (a b) -> p a b", a=8), wgs[:, :, fb * 256:(fb + 1) * 256], "cv%d" % l, [], [tCV[l]])
            dma("pool", wu_bf[l, fb].rearrange("p (a b) -> p a b", a=8), wus[:, :, fb * 256:(fb + 1) * 256], "cv%d" % l, [], [tCV[l]])
        for db in range(6):
            n = 4 if db < 5 else 2
            dma("pool", wd_bf[l, db, :, 0:n * 1024].rearrange("p (a b) -> p a b", a=n), wds[:, db * 4:db * 4 + n, :], "cv%d" % l, [], [tCV[l]])

    def phaseA(l):
        lam_init = 0.8 - 0.6 * math.exp(-0.3 * l)
        tS = [{k: T(k + str(i)) for k in ("TA", "TB", "TC", "TD", "TE", "OBA", "OBB", "VLN", "CO", "sm", "xT")} for i in range(2)]
        tXB, tsm0 = T("XB"), T("smA")
        tKVQ = T("kvq")
        lq = lamq[:].rearrange("p (a b c) -> p a b c", a=2, b=2)
        tt_("dve", PR[:].rearrange("p (a c) -> p a c", a=2), lq[:, :, 0, :], lq[:, :, 1, :], ALU.mult, [tPar], [tsm0])
        P.add("dve", lambda e: e.tensor_reduce(out=sm[:, 32:34], in_=PR[:].rearrange("p (a c) -> p a c", a=2), axis=AX.X, op=ALU.add),
              [tsm0], [tsm0])
        act(sm[:, 34:36], sm[:, 32:34], AF.Exp, [tsm0], [tsm0])
        tt_("dve", sm[:, 36:37], sm[:, 35:36], sm[:, 34:35], ALU.subtract, [tsm0], [tsm0])
        ts_("dve", sm[:, 40:41], sm[:, 36:37], -lam_init, None, ALU.add, None, [tsm0], [tsm0])
        ts_("dve", gA2[:], gA2[:], 1.0 - lam_init, None, ALU.mult, None, [tPar], [tPar])
        for g in range(4):
            tr(psb_bf(7)[:, g * 128:(g + 1) * 128], Ws[:, g, :], [tWs], [tPS[7]])
        cp("dve", WsT[:].rearrange("p a b -> p (a b)"), psb_bf(7)[:, 0:512], [tPS[7]], [tWs])
        P.add("pool", lambda e: e.memset(VX[:, 0:8, :], 1.0), [], [tKVQ])
        P.add("pool", lambda e: e.memset(VX[:, 8:16, :], 1.0), [], [tKVQ])

        def xT_for_tile(tt, slot, bank):
            cp("act", XB[:], X[:, tt, :], [tX[tt]], [tXB])
            for kc in range(8):
                tr(psb_bf(bank)[:, kc * 128:(kc + 1) * 128], XB[:, kc * 128:(kc + 1) * 128], [tXB], [tPS[bank]])
            cp("dve", xTt[slot][:].rearrange("p a b -> p (a b)"), psb_bf(bank)[:, 0:1024], [tPS[bank]], [tS[slot]["xT"]])

        def inproj(slot, g, bank):
            for kc in range(8):
                mm(psb(bank), xTt[slot][:, kc, :], WIN[:, kc, g * 512:(g + 1) * 512], kc == 0, kc == 7,
                   [tS[slot]["xT"], tWIN[g]], [tPS[bank]])

        def rms_rope_B(H, nh, so, slot, tt, g_b, out_ap_fn):
            t = tS[slot]
            n = nh * 64
            bank_t = H["t"]
            Hs = H["ap"]
            smq = sm[:, so:so + nh]
            act(TC[slot][:, 0:n], Hs, AF.Square, [bank_t], [t["TC"]])
            P.add("dve", lambda e: e.tensor_reduce(out=smq, in_=TC[slot][:, 0:n].rearrange("p (h d) -> p h d", h=nh), axis=AX.X, op=ALU.add),
                  [t["TC"]], [t["sm"]])
            rstd_chain(smq, smq, nh, 1.0 / 64, t["sm"])
            tt_("dve", TD[slot][:, 0:n].rearrange("p (h d) -> p h d", h=nh), Hs.rearrange("p (h d) -> p h d", h=nh),
                g_b[:].unsqueeze(1).broadcast_to([128, nh, 64]), ALU.mult, [bank_t, tPar], [t["TD"]])
            cBt = cosB[:, tt, :].unsqueeze(1).broadcast_to([128, nh, 64])
            tt_("pool", TE[slot][:, 0:n].rearrange("p (h d) -> p h d", h=nh), TD[slot][:, 0:n].rearrange("p (h d) -> p h d", h=nh), cBt, ALU.mult,
                [t["TD"], tTab], [t["TE"]])
            sBv = sinB[:, tt, :].rearrange("p (r h d) -> p r h d", r=2, h=2)
            for hf in range(2):
                o_ = TC[slot][:, 0:n].rearrange("p (a r h d) -> p a r h d", a=nh, r=2, h=2)[:, :, :, hf, :]
                i_ = TD[slot][:, 0:n].rearrange("p (a r h d) -> p a r h d", a=nh, r=2, h=2)[:, :, :, 1 - hf, :]
                s_ = sBv[:, :, hf, :].unsqueeze(1).broadcast_to([128, nh, 2, 16])
                tt_("dve", o_, i_, s_, ALU.mult, [t["TD"], tTab], [t["TC"]])
            tt_("pool", TE[slot][:, 0:n], TE[slot][:, 0:n], TC[slot][:, 0:n], ALU.add, [t["TE"], t["TC"]], [t["TE"]])
            out_ap_fn(TE[slot][:, 0:n], smq)

        for tt in range(NT):
            slot = tt % 2
            t = tS[slot]
            b0, b1 = (0, 1) if tt % 2 == 0 else (2, 3)
            xT_for_tile(tt, slot, 4 if slot == 0 else 7)
            inproj(slot, 0, b0)
            inproj(slot, 1, b1)
            tok = slice(tt * 128, (tt + 1) * 128)
            tb = 5 + slot
            H0 = psb(b0)
            H0v = H0.rearrange("p (v h d) -> p v h d", v=16, h=2)
            cA = cosA[:, tt, :].unsqueeze(1).broadcast_to([128, 16, 32])
            sA0 = sinA[:, tt, 0:16].unsqueeze(1).broadcast_to([128, 16, 16])
            sA1 = sinA[:, tt, 16:32].unsqueeze(1).broadcast_to([128, 16, 16])
            TAv = TA[slot][:].rearrange("p (v d) -> p v d", v=16)
            TBv = TB[slot][:].rearrange("p (v h d) -> p v h d", v=16, h=2)
            tt_("dve", TAv, H0.rearrange("p (v d) -> p v d", v=16), cA, ALU.mult, [tPS[b0], tTab], [t["TA"]])
            tt_("dve", TBv[:, :, 0, :], H0v[:, :, 1, :], sA0, ALU.mult, [tPS[b0], tTab], [t["TB"]])
            tt_("dve", TBv[:, :, 1, :], H0v[:, :, 0, :], sA1, ALU.mult, [tPS[b0], tTab], [t["TB"]])
            tt_("pool", OBA[slot][:], TA[slot][:], TB[slot][:], ALU.add, [t["TA"], t["TB"]], [t["OBA"]])
            for blk in range(4):
                tr(psb_bf(tb)[:, blk * 128:(blk + 1) * 128], OBA[slot][:, blk * 128:(blk + 1) * 128], [t["OBA"]], [tPS[tb]])

            def outq(src, smq, slot=slot, t=t):
                o_ = OBB[slot][:, 0:512].rearrange("p (g j d) -> p j g d", g=4, j=2)
                tt_("pool", o_, src.rearrange("p (j g d) -> p j g d", j=2, g=4),
                    smq.rearrange("p (j g) -> p j g", j=2).unsqueeze(3).broadcast_to([128, 2, 4, 64]), ALU.mult,
                    [t["TE"], t["sm"]], [t["OBB"]])
            rms_rope_B(dict(ap=psb(b1), t=tPS[b1]), 8, 64 * slot, slot, tt, gq_b, outq)
            for blk in range(4):
                tr(psb_bf(tb)[:, (4 + blk) * 128:(5 + blk) * 128], OBB[slot][:, blk * 128:(blk + 1) * 128], [t["OBB"]], [tPS[tb]])
            p5 = psb_bf(tb).rearrange("p (a b) -> p a b", a=8)
            cp("act", AqT[:, :, tok], p5[:, 0:2, :], [tPS[tb]], [tKVQ])
            cp("act", AkT[:, :, tok], p5[:, 2:4, :], [tPS[tb]], [tKVQ])
            cp("act", BqT[:, :, tok], p5[:, 4:8, :], [tPS[tb]], [tKVQ])

        load_win(l, 1)
        for tt in range(NT):
            slot = tt % 2
            t = tS[slot]
            b0, b1 = (0, 1) if tt % 2 == 0 else (2, 3)
            xT_for_tile(tt, slot, 4 + slot)
            inproj(slot, 0, b0)
            inproj(slot, 1, b1)
            tok = slice(tt * 128, (tt + 1) * 128)
            H2 = psb(b0)
            H3 = psb(b1)
            def outk(src, smq, slot=slot, t=t):
                tt_("pool", OBB[slot][:, 0:128].rearrange("p (h d) -> p h d", h=2), src.rearrange("p (h d) -> p h d", h=2),
                    smq.unsqueeze(2).broadcast_to([128, 2, 64]), ALU.mult, [t["TE"], t["sm"]], [t["OBB"]])
            rms_rope_B(dict(ap=H2[:, 0:128], t=tPS[b0]), 2, 64 * slot + 8, slot, tt, gk_b, outk)
            tr(psb_bf(6)[:, 0:128], OBB[slot][:, 0:128], [t["OBB"]], [tPS[6]])
            Hav = H2[:, 128:384].rearrange("p (a q d) -> p a q d", a=2, q=2)
            VXa = VX[:, tt, 0:512].rearrange("p (a c) -> p a c", a=2)
            cp("dve", VXa[:, :, 0:64], Hav[:, :, 0, :], [tPS[b0]], [tKVQ])
            cp("dve", VXa[:, :, 192:256], Hav[:, :, 1, :], [tPS[b0]], [tKVQ])
            VXb = VX[:, tt, 512:896].rearrange("p (j c) -> p j c", j=2)
            cp("dve", VXb[:, :, 64:128], H2[:, 384:512].rearrange("p (j d) -> p j d", j=2), [tPS[b0]], [tKVQ])
            UV = TA[slot]
            so = 64 * slot
            act(UV[:], H3, AF.Gelu_apprx_tanh, [tPS[b1]], [t["TA"]])
            P.add("dve", lambda e, so=so, UV=UV: e.bn_stats(out=sm[:, so + 16:so + 22], in_=UV[:, 256:512]), [t["TA"]], [t["sm"]])
            P.add("dve", lambda e, so=so: e.bn_aggr(out=sm[:, so + 22:so + 24], in_=sm[:, so + 16:so + 22].rearrange("p (a b) -> p a b", a=1)),
                  [t["sm"]], [t["sm"]])
            rstd_chain(sm[:, so + 24:so + 25], sm[:, so + 23:so + 24], 1, 1.0, t["sm"])
            ts_("dve", TB[slot][:, 0:256], UV[:, 256:512], sm[:, so + 22:so + 23], sm[:, so + 24:so + 25], ALU.subtract, ALU.mult,
                [t["TA"], t["sm"]], [t["TB"]])
            tt_("pool", TB[slot][:, 0:256], TB[slot][:, 0:256], cg_b[:], ALU.mult, [t["TB"], tPar], [t["TB"]])
            tt_("pool", VLN[slot][:], TB[slot][:, 0:256], cb_b[:], ALU.add, [t["TB"], tPar], [t["VLN"]])
            for g in range(4):
                mm(psb(7)[:, g * 64:(g + 1) * 64], WsT[:, g, :], VLN[slot][:, g * 64:(g + 1) * 64], True, True, [tWs, t["VLN"]], [tPS[7]])
            for g in range(4):
                stt_("dve", CO[slot][:, g * 64:(g + 1) * 64], psb(7)[:, g * 64:(g + 1) * 64], bsT[:, g:g + 1], UV[:, g * 64:(g + 1) * 64],
                     ALU.add, ALU.mult, [tPS[7], tPar, t["TA"]], [t["CO"]])
            for blk in range(2):
                tr(psb_bf(6)[:, (1 + blk) * 128:(2 + blk) * 128], CO[slot][:, blk * 128:(blk + 1) * 128], [t["CO"]], [tPS[6]])
            p6 = psb_bf(6).rearrange("p (a b) -> p a b", a=8)
            cp("act", BkT[:, tok], p6[:, 0, :], [tPS[6]], [tKVQ])
            cp("act", mTC[:, :, tok], p6[:, 1:3, :], [tPS[6]], [tKVQ])

    def phaseB(l):
        tWO = T("wout")
        dma("pool", WOUT[:], w_out_d[l].rearrange("(ec p) d -> p ec d", p=128), "wout", [], [tWO])
        tPT = [T("pt0"), T("pt1")]
        tR, tTt, tOP, tSQ, tY, tsm = T("R"), T("Tt"), T("OP"), T("SQ"), T("Y"), T("smB")
        tmT = [T("mT%d" % c) for c in range(6)]
        tQM = [T("qm0"), T("qm1")]
        neglam = sm[:, 40:41]
        cnt = {"s": 0, "m": 0, "tick": 0}
        pending = []

        def defer(delay, fn):
            pending.append([cnt["tick"] + delay, fn])

        def run_due(force=False):
            progressed = True
            while progressed:
                progressed = False
                for item in list(pending):
                    if force or item[0] <= cnt["tick"]:
                        pending.remove(item)
                        item[1]()
                        progressed = True

        def prep_qm(d, qs):
            rows = d["rows"]
            P.add("pool", lambda e: e.memset(QM[qs][:], 0.0), [], [tQM[qs]])
            if rows.start == 96:
                P.add("dve", lambda e: e.tensor_copy(out=QM[qs][64:128, :], in_=d["q_src64"]), [], [tQM[qs]])
                P.add("dve", lambda e: e.memset(QM[qs][64:96, :], 0.0), [], [tQM[qs]])
            else:
                P.add("dve", lambda e: e.tensor_copy(out=QM[qs][rows, :], in_=d["q_src"]), [], [tQM[qs]])

        def run_map(d, qs, acc):
            accb = (2 * acc, 2 * acc + 1)
            kT_fn, v_fn, scale = d["kT_fn"], d["v_fn"], d["scale"]
            pend = None
            for kb in range(NT):
                sp_ = cnt["s"] % 2
                cnt["s"] += 1
                sb_ = (2 * sp_, 2 * sp_ + 1)
                for j in range(2):
                    mm(psb(sb_[j]), kT_fn(kb), QM[qs][:, j * 512:(j + 1) * 512], True, True, [tQM[qs]], [tPS[sb_[j]]])
                act(PT[sp_][:], PS[sp_][:], AF.Exp, [tPS[sb_[0]], tPS[sb_[1]]], [tPT[sp_]], scale=scale)
                if pend is not None:
                    pk, ps_ = pend
                    for j in range(2):
                        mm(psb(accb[j]), v_fn(pk), PT[ps_][:, j * 512:(j + 1) * 512], pk == 0, pk == NT - 1, [tPT[ps_]], [tPS[accb[j]]])
                pend = (kb, sp_)
                cnt["tick"] += 1
                run_due()
            pk, ps_ = pend
            for j in range(2):
                mm(psb(accb[j]), v_fn(pk), PT[ps_][:, j * 512:(j + 1) * 512], pk == 0, pk == NT - 1, [tPT[ps_]], [tPS[accb[j]]])

        def tail_A(acc, hh, c, pp):
            dr = slice(64 * hh, 64 * hh + 64)
            nr = slice(64 * (1 - hh), 64 * (1 - hh) + 64)
            ta = [tPS[2 * acc], tPS[2 * acc + 1]]
            P.add("dve", lambda e: e.reciprocal(out=Rt[dr, :], in_=PS[acc][nr, :]), ta, [tR])
            if c == 0:
                tt_("dve", OP[dr, :], PS[acc][dr, :], Rt[dr, :], ALU.mult, ta + [tR], [tOP])
                return
            tt_("dve", Tt[dr, :], PS[acc][dr, :], Rt[dr, :], ALU.mult, ta + [tR], [tTt])
            stt_("dve", OP[dr, :], Tt[dr, :], neglam[dr, :], OP[dr, :], ALU.mult, ALU.add, [tTt, tOP], [tOP])
            if hh == 0:
                return
            tt_("pool", SQ[:], OP[:], OP[:], ALU.mult, [tOP], [tSQ])

            def st2():
                for j in range(2):
                    mm(psb(2 * acc + j), bdm[:], SQ[:, j * 512:(j + 1) * 512], True, True, [tSQ, tConst], [tPS[2 * acc + j]])
                defer(3, st3)

            def st3():
                act(Rt[:], PS[acc][:], AF.Sqrt, ta, [tR], bias=epsT[:, 0:1])
                defer(2, st4)

            def st4():
                P.add("dve", lambda e: e.reciprocal(out=Rt[:], in_=Rt[:]), [tR], [tR])
                stt_("dve", mTAB[:, pp, :], OP[:], gA2[:, 0:1], Rt[:], ALU.mult, ALU.mult, [tOP, tR, tPar], [tmT[pp]])
            defer(4, st2)

        def tail_B(acc, hh, cB):
            dr = slice(64 * hh, 64 * hh + 64)
            nr = slice(64 * (1 - hh), 64 * (1 - hh) + 64)
            ta = [tPS[2 * acc], tPS[2 * acc + 1]]
            P.add("dve", lambda e: e.reciprocal(out=Rt[dr, :], in_=PS[acc][nr, :]), ta, [tR])
            tt_("dve", mTAB[dr, 2 + cB, :], PS[acc][dr, :], Rt[dr, :], ALU.mult, ta + [tR], [tmT[2 + cB]])

        for qh in range(2):
            q0 = qh * 1024
            maps = []
            for pp in range(2):
                for hh in range(2):
                    h = 2 * pp + hh
                    for c in range(2):
                        r0 = (hh * 2 + c) * 32
                        maps.append(dict(
                            kT_fn=lambda kb, pp=pp: AkT[:, pp, kb * 128:(kb + 1) * 128],
                            q_src=AqT[r0:r0 + 32, pp, q0:q0 + 1024], rows=slice(r0, r0 + 32),
                            q_src64=AqT[64:128, pp, q0:q0 + 1024],
                            v_fn=lambda kb, h=h: VX[:, kb, h * 128:(h + 1) * 128],
                            scale=32 ** -0.5,
                            tail=lambda acc, hh=hh, c=c, pp=pp: tail_A(acc, hh, c, pp)))
            for cB in range(4):
                for hh in range(2):
                    hB = 2 * cB + hh
                    j_kv, g = hB // 4, hB % 4
                    rows = slice(64 * j_kv, 64 * j_kv + 64)
                    voff = 512 + j_kv * 192 + (64 if hh == 0 else 0)
                    maps.append(dict(
                        kT_fn=lambda kb: BkT[:, kb * 128:(kb + 1) * 128],
                        q_src=BqT[rows, g, q0:q0 + 1024], rows=rows, q_src64=None,
                        v_fn=lambda kb, voff=voff: VX[:, kb, voff:voff + 128],
                        scale=64 ** -0.5,
                        tail=lambda acc, hh=hh, cB=cB: tail_B(acc, hh, cB)))
            prep_qm(maps[0], cnt["m"] % 2)
            for i, d in enumerate(maps):
                qs = cnt["m"] % 2
                acc = 2 + (cnt["m"] % 2)
                cnt["m"] += 1
                if i + 1 < len(maps):
                    prep_qm(maps[i + 1], cnt["m"] % 2)
                run_map(d, qs, acc)
                defer(3, lambda d=d, acc=acc: d["tail"](acc))
            run_due(force=True)
            for tl in range(8):
                tt = qh * 8 + tl
                acc = 2 + (tl % 2)
                for dg in range(2):
                    for ec in range(8):
                        if ec < 6:
                            lhs = mTAB[:, ec, tl * 128:(tl + 1) * 128]
                            rd = [tmT[ec], tWO]
                        else:
                            lhs = mTC[:, ec - 6, tt * 128:(tt + 1) * 128]
                            rd = [tWO]
                        mm(psb(2 * acc + dg), lhs, WOUT[:, ec, dg * 512:(dg + 1) * 512], ec == 0, ec == 7, rd, [tPS[2 * acc + dg]])
                ta = [tPS[2 * acc], tPS[2 * acc + 1]]
                stt_("dve", Yb[:], X[:, tt, :], float(ALPHA), PS[acc][:], ALU.mult, ALU.add, ta + [tX[tt]], [tY])
                layernorm_tail(Yb, tY, tt, tsm)

    def layernorm_tail(Y, tY, tt, tsm):
        for j in range(2):
            P.add("dve", lambda e, j=j: e.bn_stats(out=sm[:, 44 + 6 * j:50 + 6 * j], in_=Y[:, j * 512:(j + 1) * 512]), [tY], [tsm])
        P.add("dve", lambda e: e.bn_aggr(out=sm[:, 56:58], in_=sm[:, 44:56].rearrange("p (a b) -> p a b", a=2)), [tsm], [tsm])
        rstd_chain(sm[:, 58:59], sm[:, 57:58], 1, 1.0, tsm)
        stt_("dve", sm[:, 59:60], sm[:, 56:57], -1.0, sm[:, 58:59], ALU.mult, ALU.mult, [tsm], [tsm])
        act(X[:, tt, :], Y[:], AF.Identity, [tY, tsm], [tX[tt]], scale=sm[:, 58:59], bias=sm[:, 59:60])

    def phaseC(l, s, last):
        tG1, tG2 = T("g1"), T("g2")
        dma("sp", g1_b[:], l1g_d[l:l + 1, :].broadcast_to([128, D]), "lng", [], [tG1])
        dma("sp", b1_b[:], l1b_d[l:l + 1, :].broadcast_to([128, D]), "lng", [], [tG1])
        dma("sp", g2_b[:], l2g_d[l:l + 1, :].broadcast_to([128, D]), "lng2", [], [tG2])
        dma("sp", b2_b[:], l2b_d[l:l + 1, :].broadcast_to([128, D]), "lng2", [], [tG2])
        tWG = [T("wg0"), T("wg1")]
        tWU = [T("wu0"), T("wu1")]
        tWD = [T("wd0"), T("wd1")]
        thid = [T("hid%d" % f) for f in range(NFC)]
        txTf = T("xTf")
        tXB, tSG, tY, tsm = T("XBc"), [T("sg0"), T("sg1")], T("Yc"), T("smC")
        wgs = wg_d[l].rearrange("(kc p) f -> p kc f", p=128)
        wus = wu_d[l].rearrange("(kc p) f -> p kc f", p=128)
        wds = wd_d[l].rearrange("(fc p) d -> p fc d", p=128)
        seq = {"gu": 0, "d": 0}

        def load_gu(fb):
            sl = seq["gu"] % 2
            seq["gu"] += 1
            if USE_CV:
                dma("sp", WG[sl][:].rearrange("p a b -> p (a b)"), wg_bf[l, fb], "wg%d" % sl, [tCV[l]], [tWG[sl]])
                dma("sp", WU[sl][:].rearrange("p a b -> p (a b)"), wu_bf[l, fb], "wu%d" % sl, [tCV[l]], [tWU[sl]])
            else:
                dma("pool", WG[sl][:], wgs[:, :, fb * 256:(fb + 1) * 256], "wg%d" % sl, [], [tWG[sl]])
                dma("pool", WU[sl][:], wus[:, :, fb * 256:(fb + 1) * 256], "wu%d" % sl, [], [tWU[sl]])
            return sl

        def load_d(db):
            sl = seq["d"] % 2
            seq["d"] += 1
            n = 4 if db < 5 else 2
            if USE_CV:
                dma("sp", WD[sl][:, 0:n, :].rearrange("p a b -> p (a b)"), wd_bf[l, db, :, 0:n * 1024], "wd%d" % sl, [tCV[l]], [tWD[sl]])
            else:
                dma("pool", WD[sl][:, 0:n, :], wds[:, db * 4:db * 4 + n, :], "wd%d" % sl, [], [tWD[sl]])
            return sl

        for tg in range(4):
            nxt = load_gu(0)
            for tl in range(4):
                tt = tg * 4 + tl
                tt_("pool", X[:, tt, :], X[:, tt, :], g1_b[:], ALU.mult, [tX[tt], tG1], [tX[tt]])
                tt_("pool", X[:, tt, :], X[:, tt, :], b1_b[:], ALU.add, [tX[tt], tG1], [tX[tt]])
                cp("act", XBc[:], X[:, tt, :], [tX[tt]], [tXB])
                bank = 4 + (tl % 2)
                for kc in range(8):
                    tr(psb_bf(bank)[:, kc * 128:(kc + 1) * 128], XBc[:, kc * 128:(kc + 1) * 128], [tXB], [tPS[bank]])
                cp("dve", xTf[:, :, tl * 128:(tl + 1) * 128], psb_bf(bank).rearrange("p (a b) -> p a b", a=8), [tPS[bank]], [txTf])
            for fb in range(11):
                sl = nxt
                if fb + 1 < 11:
                    nxt = load_gu(fb + 1)
                for fci in range(2):
                    fc = 2 * fb + fci
                    gb = 0 + (fc % 2)
                    ub = 2 + (fc % 2)
                    for kc in range(8):
                        mm(psb(gb), WG[sl][:, kc, fci * 128:(fci + 1) * 128], xTf[:, kc, :], kc == 0, kc == 7, [tWG[sl], txTf], [tPS[gb]])
                    for kc in range(8):
                        mm(psb(ub), WU[sl][:, kc, fci * 128:(fci + 1) * 128], xTf[:, kc, :], kc == 0, kc == 7, [tWU[sl], txTf], [tPS[ub]])
                    act(SG[fc % 2][:], psb(gb), AF.Silu, [tPS[gb]], [tSG[fc % 2]])
                    tt_("dve", hidT[:, fc, :], psb(ub), SG[fc % 2][:], ALU.mult, [tPS[ub], tSG[fc % 2]], [thid[fc]])
            nd = load_d(0)
            for db in range(6):
                sl = nd
                if db + 1 < 6:
                    nd = load_d(db + 1)
                n = 4 if db < 5 else 2
                for fci in range(n):
                    fc = db * 4 + fci
                    for tl in range(4):
                        for dg in range(2):
                            mm(psb(2 * tl + dg), hidT[:, fc, tl * 128:(tl + 1) * 128], WD[sl][:, fci, dg * 512:(dg + 1) * 512],
                               fc == 0, fc == NFC - 1, [thid[fc], tWD[sl]], [tPS[2 * tl + dg]])
            for tl in range(4):
                tt = tg * 4 + tl
                ta = [tPS[2 * tl], tPS[2 * tl + 1]]
                stt_("dve", Yc[:], X[:, tt, :], float(ALPHA), PS[tl][:], ALU.mult, ALU.add, ta + [tX[tt]], [tY])
                layernorm_tail(Yc, tY, tt, tsm)
                tt_("pool", X[:, tt, :], X[:, tt, :], g2_b[:], ALU.mult, [tX[tt], tG2], [tX[tt]])
                tt_("pool", X[:, tt, :], X[:, tt, :], b2_b[:], ALU.add, [tX[tt], tG2], [tX[tt]])
                if last:
                    dma("sp", out_d[s, tt * 128:(tt + 1) * 128, :], X[:, tt, :], "out", [tX[tt]], [])

    load_win(0, 0)
    load_tables()
    load_params(0)
    if USE_CV:
        convert_layer(0)
    for s in range(nseq):
        if s > 0:
            P.barrier()
        for q in range(4):
            dma("sp", X[:, q * 4:(q + 1) * 4, :], x_d[s, q * 512:(q + 1) * 512, :].rearrange("(t p) d -> p t d", p=128), "x%d" % q,
                [], [tX[q * 4 + i] for i in range(4)])
        def dbg_store():
            P.barrier()
            for tt in range(NT):
                dma("sp", out_d[s, tt * 128:(tt + 1) * 128, :], X[:, tt, :], "out", [tX[tt]], [])
        for l in range(L):
            P.barrier()
            if stop == "load":
                dbg_store()
                break
            phaseA(l)
            P.barrier()
            if stop == "A":
                dbg_store()
                break
            if USE_CV and s == 0 and l + 1 < L:
                convert_layer(l + 1)
            phaseB(l)
            P.barrier()
            if stop == "B":
                dbg_store()
                break
            nl = l + 1 if l + 1 < L else (0 if s + 1 < nseq else None)
            if nl is not None:
                load_win(nl, 0)
                load_tables()
                load_params(nl)
            phaseC(l, s, l == L - 1)
    P.barrier()
    P.finalize_and_emit(st)
    st.close()
    return nc, P


_CACHE = {}


def kernel(**inputs):
    n_cores = 8
    nseq = 32 // n_cores
    depth = 4
    if "nc" not in _CACHE:
        _CACHE["nc"] = build(nseq, depth)[0]
    nc = _CACHE["nc"]
    consts = host_consts()
    maps = []
    for c in range(n_cores):
        m = {k: np.ascontiguousarray(np.asarray(v, dtype=np.float32)) for k, v in inputs.items() if k != "x"}
        m["x"] = np.ascontiguousarray(np.asarray(inputs["x"], dtype=np.float32)[c * nseq:(c + 1) * nseq])
        m.update(consts)
        maps.append(m)
    res = run_bass_kernel_spmd(nc, maps, core_ids=list(range(n_cores)))
    return np.concatenate([np.asarray(r["out"]) for r in res.results], axis=0).astype(np.float32)
```

```python
import math, os
CUT = int(os.environ.get('KB_CUT', '99'))
USE_CV = True
import numpy as np
from contextlib import ExitStack
import concourse.bass as bass
import concourse.mybir as mybir

from concourse.bass_utils import run_bass_kernel_spmd


ENGS = ("pe", "act", "dve", "pool", "sp")


class T:
    __slots__ = ("name", "w", "r")

    def __init__(self, name=""):
        self.name = name
        self.w = None
        self.r = []


class Op:
    __slots__ = ("fn", "deps", "signal", "dma_sem", "dma_val")

    def __init__(self, fn, deps, dma_sem=None, dma_val=0):
        self.fn = fn
        self.deps = deps
        self.signal = False
        self.dma_sem = dma_sem
        self.dma_val = dma_val


class Prog:
    def __init__(self, nc):
        self.nc = nc
        self.ops = {e: [] for e in ENGS}
        self.dma_counts = {}
        self.n_wait = 0

    def _deps(self, eng, reads, writes):
        deps = set()
        for t in reads:
            if t.w is not None:
                deps.add(t.w)
        for t in writes:
            if t.w is not None:
                deps.add(t.w)
            for x in t.r:
                deps.add(x)
        best = {}
        for d in deps:
            if d[0] == "c" and d[1] == eng and eng == "pe":
                continue
            k = (d[0], d[1])
            if k not in best or best[k][2] < d[2]:
                best[k] = d
        return list(best.values())

    def add(self, eng, fn, reads=(), writes=()):
        deps = self._deps(eng, reads, writes)
        idx = len(self.ops[eng])
        self.ops[eng].append(Op(fn, deps))
        tok = ("c", eng, idx)
        for t in reads:
            t.r.append(tok)
        for t in writes:
            t.w = tok
            t.r = []
        return tok

    def dma(self, eng, fn, sem_key, reads=(), writes=()):
        deps = self._deps(eng, reads, writes)
        n = self.dma_counts.get(sem_key, 0) + 1
        self.dma_counts[sem_key] = n
        self.ops[eng].append(Op(fn, deps, dma_sem=sem_key, dma_val=16 * n))
        tok = ("d", sem_key, 16 * n)
        for t in reads:
            t.r.append(tok)
        for t in writes:
            t.w = tok
            t.r = []
        return tok

    def barrier(self):
        toks = []
        for e in ENGS:
            for i in range(len(self.ops[e]) - 1, -1, -1):
                op = self.ops[e][i]
                if op.fn is not None and op.dma_sem is None:
                    toks.append(("c", e, i))
                    break
        for k, n in self.dma_counts.items():
            if not str(k).startswith("cv"):
                toks.append(("d", k, 16 * n))
        for e in ENGS:
            self.wait_tokens(e, [t for t in toks if not (t[0] == "c" and t[1] == e and e == "pe")])

    def wait_tokens(self, eng, toks):
        self.ops[eng].append(Op(None, list(toks)))

    def finalize_and_emit(self, stack):
        nc = self.nc
        for e in ENGS:
            for op in self.ops[e]:
                for d in op.deps:
                    if d[0] == "c":
                        self.ops[d[1]][d[2]].signal = True
        val = {}
        for e in ENGS:
            c = 0
            for i, op in enumerate(self.ops[e]):
                if op.signal:
                    c += 1
                    val[(e, i)] = c
            assert c < 60000, (e, c)
        csem = {e: stack.enter_context(nc.semaphore("cs_" + e)) for e in ENGS}
        dsem = {k: stack.enter_context(nc.semaphore("ds_" + str(k))) for k in self.dma_counts}
        handles = {"pe": "tensor", "act": "scalar", "dve": "vector", "pool": "gpsimd", "sp": "sync"}
        block = stack.enter_context(nc.Block())
        prog = self

        def make(e):
            def body(eng):
                waited = {}
                for i, op in enumerate(prog.ops[e]):
                    need = {}
                    for d in op.deps:
                        if d[0] == "c":
                            key = ("c", d[1])
                            v = val[(d[1], d[2])]
                        else:
                            key = ("d", d[1])
                            v = d[2]
                        if waited.get(key, 0) >= v:
                            continue
                        if need.get(key, 0) < v:
                            need[key] = v
                    for key, v in need.items():
                        sem = csem[key[1]] if key[0] == "c" else dsem[key[1]]
                        eng.wait_ge(sem, v)
                        waited[key] = v
                        prog.n_wait += 1
                    if op.fn is None:
                        continue
                    ins = op.fn(eng)
                    if op.dma_sem is not None:
                        ins.then_inc(dsem[op.dma_sem], 16)
                    elif op.signal:
                        ins.then_inc(csem[e], 1)
            return body

        for e in ENGS:
            if not self.ops[e]:
                continue
            getattr(block, handles[e])(make(e))


F32 = mybir.dt.float32
BF16 = mybir.dt.bfloat16
AF = mybir.ActivationFunctionType
ALU = mybir.AluOpType
AX = mybir.AxisListType

D = 1024
S = 2048
NT = 16
HID = 2816
NFC = 22
EPS = 1e-5
ALPHA = (2 * 4) ** 0.25
KB = 1024


def host_consts():
    inv = 1.0 / (10000.0 ** (np.arange(0, 32, 2, dtype=np.float32) / 32.0))
    t = np.arange(S, dtype=np.float32)

    def cs(pos):
        ang = pos[:, None] * inv[None, :]
        ang = np.concatenate([ang, ang], -1)
        c = np.cos(ang).astype(np.float32)
        s = np.sin(ang).astype(np.float32)
        s[:, :16] *= -1.0
        return c, s
    cA, sA = cs(t)
    row = np.floor(t / 64.0).astype(np.float32)
    col = (t - 64.0 * row).astype(np.float32)
    cR, sR = cs(row)
    cC, sC = cs(col)
    cB = np.concatenate([cR, cC], -1)
    sB = np.concatenate([sR, sC], -1)

    def lay(a):
        n = a.shape[1]
        return np.ascontiguousarray(a.reshape(NT, 128, n).transpose(1, 0, 2).reshape(128, NT * n))
    tabs = np.concatenate([lay(cA), lay(sA), lay(cB), lay(sB)], 1).astype(np.float32)
    ident = np.eye(128, dtype=np.float32)
    bd = np.zeros((128, 128), np.float32)
    bd[:64, :64] = 1.0 / 64
    bd[64:, 64:] = 1.0 / 64
    return dict(tabs=tabs, ident=ident, bd=bd)


def build(nseq, depth, stop=None):
    nc = bass.Bass("TRN2", target_bir_lowering=False)
    L = depth

    def din(name, shape):
        return nc.dram_tensor(name, shape, F32, kind="ExternalInput").ap()
    x_d = din("x", [nseq, S, D])
    w_in_d = din("w_in", [L, D, 2048])
    w_out_d = din("w_out", [L, D, D])
    lam_d = din("lam_qk", [L, 4, 32])
    asg_d = din("a_subln_g", [L, 64])
    bqg_d = din("b_q_norm_g", [L, 64])
    bkg_d = din("b_k_norm_g", [L, 64])
    clg_d = din("c_ln_g", [L, 256])
    clb_d = din("c_ln_b", [L, 256])
    cws_d = din("c_w_s", [L, 4, 128, 128])
    cbs_d = din("c_b_s", [L, 4, 128])
    l1g_d = din("ln1_g", [L, D])
    l1b_d = din("ln1_b", [L, D])
    wg_d = din("w_gate", [L, D, HID])
    wu_d = din("w_up", [L, D, HID])
    wd_d = din("w_down", [L, HID, D])
    l2g_d = din("ln2_g", [L, D])
    l2b_d = din("ln2_b", [L, D])
    tabs_d = din("tabs", [128, 3072])
    ident_d = din("ident", [128, 128])
    bd_d = din("bd", [128, 128])
    out_d = nc.dram_tensor("out", [nseq, S, D], F32, kind="ExternalOutput").ap()
    if USE_CV:
        wg_bf = nc.dram_tensor("wg_bf", [L, 11, 128, 2048], BF16, kind="Internal").ap()
        wu_bf = nc.dram_tensor("wu_bf", [L, 11, 128, 2048], BF16, kind="Internal").ap()
        wd_bf = nc.dram_tensor("wd_bf", [L, 6, 128, 4096], BF16, kind="Internal").ap()

    st = ExitStack()

    def sb(name, shape, dt):
        return st.enter_context(nc.sbuf_tensor(name, shape, dt))
    X = sb("X", [128, NT, D], F32)
    REG = sb("REG", [128, 80 * KB // 4], F32)
    AR = sb("AR", [128, 57088 // 4], F32)
    ident = sb("identb", [128, 128], BF16)
    bdm = sb("bdm", [128, 128], BF16)
    Ws = sb("Ws", [128, 4, 128], BF16)
    WsT = sb("WsT", [128, 4, 128], BF16)
    gq_b = sb("gq_b", [128, 64], F32)
    gk_b = sb("gk_b", [128, 64], F32)
    cg_b = sb("cg_b", [128, 256], F32)
    cb_b = sb("cb_b", [128, 256], F32)
    bsT = sb("bsT", [128, 4], F32)
    lamq = sb("lamq", [128, 128], F32)
    gA2 = sb("gA2", [128, 1], F32)
    sm = sb("sm", [128, 128], F32)
    PR = sb("PR", [128, 64], F32)
    epsT = sb("epsT", [128, 1], F32)
    PS = [st.enter_context(nc.psum_tensor("PS%d" % i, [128, 1024], F32)) for i in range(4)]

    def view(base, off, shape, dt):
        esz = 2 if dt == BF16 else 4
        n = int(np.prod(shape[1:]))
        nb = n * esz
        assert off % 4 == 0 and nb % 4 == 0
        ap = base[:, off // 4:(off + nb) // 4]
        if dt == BF16:
            ap = ap.bitcast(BF16)
        if len(shape) == 3:
            ap = ap.rearrange("p (a b) -> p a b", a=shape[1])
        return ap

    AqT = view(REG, 0, [128, 2, S], BF16)
    AkT = view(REG, 8 * KB, [128, 2, S], BF16)
    BqT = view(REG, 16 * KB, [128, 4, S], BF16)
    BkT = view(REG, 32 * KB, [128, S], BF16)
    VX = view(REG, 36 * KB, [128, NT, 896], BF16)
    mTC = view(REG, 64 * KB, [128, 2, S], BF16)
    xTt = [view(REG, 75 * KB + i * 2 * KB, [128, 8, 128], BF16) for i in range(2)]
    hidT = view(REG, 0, [128, NFC, 512], BF16)
    xTf = view(REG, 22 * KB, [128, 8, 512], BF16)
    WG = [view(REG, 30 * KB + i * 4 * KB, [128, 8, 256], BF16) for i in range(2)]
    WU = [view(REG, 38 * KB + i * 4 * KB, [128, 8, 256], BF16) for i in range(2)]
    WD = [view(REG, 46 * KB + i * 8 * KB, [128, 4, 1024], BF16) for i in range(2)]
    XBc = view(REG, 62 * KB, [128, 1024], BF16)
    SG = [view(REG, 64 * KB + i * 2 * KB, [128, 512], F32) for i in range(2)]
    Yc = view(REG, 68 * KB, [128, 1024], F32)
    g2_b = view(REG, 72 * KB, [128, 1024], F32)
    b2_b = view(REG, 76 * KB, [128, 1024], F32)
    WIN = view(AR, 0, [128, 8, 1024], BF16)
    cosA = view(AR, 16 * KB, [128, NT, 32], F32)
    sinA = view(AR, 18 * KB, [128, NT, 32], F32)
    cosB = view(AR, 20 * KB, [128, NT, 64], F32)
    sinB = view(AR, 24 * KB, [128, NT, 64], F32)
    TA = [view(AR, 28 * KB + i * 2 * KB, [128, 512], F32) for i in range(2)]
    TB = [view(AR, 32 * KB + i * 2 * KB, [128, 512], F32) for i in range(2)]
    TC = [view(AR, 36 * KB + i * 2 * KB, [128, 512], F32) for i in range(2)]
    TD = [view(AR, 40 * KB + i * 2 * KB, [128, 512], F32) for i in range(2)]
    TE = [view(AR, 44 * KB + i * 2 * KB, [128, 512], F32) for i in range(2)]
    OBA = [view(AR, 48 * KB + i * KB, [128, 512], BF16) for i in range(2)]
    OBB = [view(AR, 50 * KB + i * 1280, [128, 640], BF16) for i in range(2)]
    XB = view(AR, 50 * KB + 2560, [128, 1024], BF16)
    VLN = [view(REG, 72 * KB + i * 512, [128, 256], BF16) for i in range(2)]
    CO = [view(REG, 73 * KB + i * 512, [128, 256], BF16) for i in range(2)]
    WOUT = view(AR, 0, [128, 8, 1024], BF16)
    mTAB = view(AR, 16 * KB, [128, 6, 1024], BF16)
    PT = [view(AR, 28 * KB + i * 2 * KB, [128, 1024], BF16) for i in range(2)]
    Rt = view(AR, 32 * KB, [128, 1024], F32)
    Tt = view(AR, 36 * KB, [128, 1024], F32)
    OP = view(AR, 40 * KB, [128, 1024], F32)
    SQ = view(AR, 44 * KB, [128, 1024], BF16)
    Yb = view(AR, 46 * KB, [128, 1024], F32)
    QM = [view(AR, 50 * KB + i * 2 * KB, [128, 1024], BF16) for i in range(2)]
    g1_b = view(AR, 44 * KB, [128, 1024], F32)
    b1_b = view(AR, 48 * KB, [128, 1024], F32)

    P = Prog(nc)
    tX = [T("x%d" % i) for i in range(NT)]
    tPS = [T("ps%d" % i) for i in range(8)]

    def psb(i):
        return PS[i // 2][:, (i % 2) * 512:(i % 2) * 512 + 512]

    def psb_bf(i):
        return PS[i // 2][:, (i % 2) * 512:(i % 2) * 512 + 512].bitcast(BF16)
    tConst = T("const")
    tPar = T("par")
    tWs = T("Ws")
    tTab = T("tab")
    tWIN = [T("win%d" % g) for g in range(2)]

    def mm(out, lhsT, rhs, start, stop, reads, writes, tp=None):
        if tp is None:
            P.add("pe", lambda e: e.matmul(out, lhsT=lhsT, rhs=rhs, start=start, stop=stop), reads, writes)
        else:
            P.add("pe", lambda e: e.matmul(out, lhsT=lhsT, rhs=rhs, start=start, stop=stop, tile_position=tp), reads, writes)

    def tr(out, in_, reads, writes):
        P.add("pe", lambda e: e.transpose(out=out, in_=in_, identity=ident[:]), list(reads) + [tConst], writes)

    def act(out, in_, func, reads, writes, scale=None, bias=None):
        kw = {}
        if scale is not None:
            kw["scale"] = scale
        if bias is not None:
            kw["bias"] = bias
        P.add("act", lambda e: e.activation(out=out, in_=in_, func=func, **kw), reads, writes)

    def tt_(eng, out, in0, in1, op, reads, writes):
        P.add(eng, lambda e: e.tensor_tensor(out=out, in0=in0, in1=in1, op=op), reads, writes)

    def ts_(eng, out, in0, s1, s2, op0, op1, reads, writes):
        if op1 is None:
            P.add(eng, lambda e: e.tensor_scalar(out=out, in0=in0, scalar1=s1, scalar2=None, op0=op0), reads, writes)
        else:
            P.add(eng, lambda e: e.tensor_scalar(out=out, in0=in0, scalar1=s1, scalar2=s2, op0=op0, op1=op1), reads, writes)

    def stt_(eng, out, in0, scalar, in1, op0, op1, reads, writes):
        P.add(eng, lambda e: e.scalar_tensor_tensor(out=out, in0=in0, scalar=scalar, in1=in1, op0=op0, op1=op1), reads, writes)

    def cp(eng, out, in_, reads, writes):
        if eng == "act":
            P.add("act", lambda e: e.copy(out=out, in_=in_), reads, writes)
        else:
            P.add(eng, lambda e: e.tensor_copy(out=out, in_=in_), reads, writes)

    def dma(eng, out, in_, key, reads, writes, slow=False):
        if slow:
            P.dma(eng, lambda e: e.dma_start(out=out, in_=in_, allow_slow_non_contiguous=True), key, reads, writes)
        else:
            P.dma(eng, lambda e: e.dma_start(out=out, in_=in_), key, reads, writes)

    def rstd_chain(dst, src, n, scale, reads_t):
        ts_("dve", dst, src, scale, EPS, ALU.mult, ALU.add, [reads_t], [reads_t])
        act(dst, dst, AF.Sqrt, [reads_t], [reads_t])
        P.add("dve", lambda e: e.reciprocal(out=dst, in_=dst), [reads_t], [reads_t])

    dma("pool", ident[:], ident_d[:, :], "c0", [], [tConst])
    dma("pool", bdm[:], bd_d[:, :], "c1", [], [tConst])
    P.add("dve", lambda e: e.memset(epsT[:], EPS), [], [tConst])

    def load_tables():
        dma("sp", cosA.rearrange("p a b -> p (a b)"), tabs_d[:, 0:512], "tab", [], [tTab])
        dma("sp", sinA.rearrange("p a b -> p (a b)"), tabs_d[:, 512:1024], "tab", [], [tTab])
        dma("sp", cosB.rearrange("p a b -> p (a b)"), tabs_d[:, 1024:2048], "tab", [], [tTab])
        dma("sp", sinB.rearrange("p a b -> p (a b)"), tabs_d[:, 2048:3072], "tab", [], [tTab])

    def load_win(l, half):
        src = w_in_d[l].rearrange("(kc p) c -> p kc c", p=128)
        if half == 0:
            lst = ((0, 512, 0, 0), (768, 1280, 512, 1))
        else:
            lst = ((1280, 1408, 0, 0), (512, 768, 128, 0), (1408, 1536, 384, 0), (1536, 2048, 512, 1))
        for (s0, s1, d0, g) in lst:
            dma("pool", WIN[:, :, d0:d0 + (s1 - s0)], src[:, :, s0:s1], "win%d" % g, [], [tWIN[g]])

    def load_params(l):
        dma("sp", gq_b[:], bqg_d[l:l + 1, :].broadcast_to([128, 64]), "par", [], [tPar])
        dma("sp", gk_b[:], bkg_d[l:l + 1, :].broadcast_to([128, 64]), "par", [], [tPar])
        dma("sp", cg_b[:], clg_d[l:l + 1, :].broadcast_to([128, 256]), "par", [], [tPar])
        dma("sp", cb_b[:], clb_d[l:l + 1, :].broadcast_to([128, 256]), "par", [], [tPar])
        dma("sp", bsT[:], cbs_d[l].rearrange("g p -> p g"), "par", [], [tPar], slow=True)
        dma("sp", lamq[:], lam_d[l:l + 1].rearrange("o a b -> o (a b)").broadcast_to([128, 128]), "par", [], [tPar])
        dma("sp", gA2[0:64, :], asg_d[l].rearrange("(d o) -> d o", o=1), "par", [], [tPar])
        dma("sp", gA2[64:128, :], asg_d[l].rearrange("(d o) -> d o", o=1), "par", [], [tPar])
        dma("pool", Ws[:], cws_d[l].rearrange("g p q -> p g q"), "ws", [], [tWs])

    tCV = [T("cv%d" % l) for l in range(L)]

    cvq = []

    def convert_layer(l):
        wgs = wg_d[l].rearrange("(kc p) f -> p kc f", p=128)
        wus = wu_d[l].rearrange("(kc p) f -> p kc f", p=128)
        wds = wd_d[l].rearrange("(fc p) d -> p fc d", p=128)
        for fb in range(11):
            cvq.append(lambda l=l, fb=fb: dma("pool", wg_bf[l, fb].rearrange("p (a b) -> p a b", a=8), wgs[:, :, fb * 256:(fb + 1) * 256],
                                               "cv%d" % l, [], [tCV[l]]))
            cvq.append(lambda l=l, fb=fb: dma("pool", wu_bf[l, fb].rearrange("p (a b) -> p a b", a=8), wus[:, :, fb * 256:(fb + 1) * 256],
                                               "cv%d" % l, [], [tCV[l]]))
        for db in range(6):
            n = 4 if db < 5 else 2
            cvq.append(lambda l=l, db=db, n=n: dma("pool", wd_bf[l, db, :, 0:n * 1024].rearrange("p (a b) -> p a b", a=n),
                                                    wds[:, db * 4:db * 4 + n, :], "cv%d" % l, [], [tCV[l]]))

    def cv_step(n=1):
        for _ in range(n):
            if cvq:
                cvq.pop(0)()

    def phaseA(l):
        lam_init = 0.8 - 0.6 * math.exp(-0.3 * l)
        tS = [{k: T(k + str(i)) for k in ("TA", "TB", "TC", "TD", "TE", "OBA", "OBB", "VLN", "CO", "sm", "xT")} for i in range(2)]
        tXB, tsm0 = T("XB"), T("smA")
        tKVQ = T("kvq")
        lq = lamq[:].rearrange("p (a b c) -> p a b c", a=2, b=2)
        tt_("dve", PR[:].rearrange("p (a c) -> p a c", a=2), lq[:, :, 0, :], lq[:, :, 1, :], ALU.mult, [tPar], [tsm0])
        P.add("dve", lambda e: e.tensor_reduce(out=sm[:, 32:34], in_=PR[:].rearrange("p (a c) -> p a c", a=2), axis=AX.X, op=ALU.add),
              [tsm0], [tsm0])
        act(sm[:, 34:36], sm[:, 32:34], AF.Exp, [tsm0], [tsm0])
        tt_("dve", sm[:, 36:37], sm[:, 35:36], sm[:, 34:35], ALU.subtract, [tsm0], [tsm0])
        ts_("dve", sm[:, 40:41], sm[:, 36:37], -lam_init, None, ALU.add, None, [tsm0], [tsm0])
        ts_("dve", gA2[:], gA2[:], 1.0 - lam_init, None, ALU.mult, None, [tPar], [tPar])
        for g in range(4):
            tr(psb_bf(7)[:, g * 128:(g + 1) * 128], Ws[:, g, :], [tWs], [tPS[7]])
        cp("dve", WsT[:].rearrange("p a b -> p (a b)"), psb_bf(7)[:, 0:512], [tPS[7]], [tWs])
        P.add("pool", lambda e: e.memset(VX[:, 0:8, :], 1.0), [], [tKVQ])
        P.add("pool", lambda e: e.memset(VX[:, 8:16, :], 1.0), [], [tKVQ])

        def xT_for_tile(tt, slot, bank):
            cp("act", XB[:], X[:, tt, :], [tX[tt]], [tXB])
            for kc in range(8):
                tr(psb_bf(bank)[:, kc * 128:(kc + 1) * 128], XB[:, kc * 128:(kc + 1) * 128], [tXB], [tPS[bank]])
            cp("dve", xTt[slot][:].rearrange("p a b -> p (a b)"), psb_bf(bank)[:, 0:1024], [tPS[bank]], [tS[slot]["xT"]])

        def inproj(slot, g, bank):
            for kc in range(8):
                mm(psb(bank), xTt[slot][:, kc, :], WIN[:, kc, g * 512:(g + 1) * 512], kc == 0, kc == 7,
                   [tS[slot]["xT"], tWIN[g]], [tPS[bank]])

        def rms_rope_B(H, nh, so, slot, tt, g_b, out_ap_fn):
            t = tS[slot]
            n = nh * 64
            bank_t = H["t"]
            Hs = H["ap"]
            smq = sm[:, so:so + nh]
            act(TC[slot][:, 0:n], Hs, AF.Square, [bank_t], [t["TC"]])
            P.add("dve", lambda e: e.tensor_reduce(out=smq, in_=TC[slot][:, 0:n].rearrange("p (h d) -> p h d", h=nh), axis=AX.X, op=ALU.add),
                  [t["TC"]], [t["sm"]])
            rstd_chain(smq, smq, nh, 1.0 / 64, t["sm"])
            tt_("dve", TD[slot][:, 0:n].rearrange("p (h d) -> p h d", h=nh), Hs.rearrange("p (h d) -> p h d", h=nh),
                g_b[:].unsqueeze(1).broadcast_to([128, nh, 64]), ALU.mult, [bank_t, tPar], [t["TD"]])
            cBt = cosB[:, tt, :].unsqueeze(1).broadcast_to([128, nh, 64])
            tt_("pool", TE[slot][:, 0:n].rearrange("p (h d) -> p h d", h=nh), TD[slot][:, 0:n].rearrange("p (h d) -> p h d", h=nh), cBt, ALU.mult,
                [t["TD"], tTab], [t["TE"]])
            sBv = sinB[:, tt, :].rearrange("p (r h d) -> p r h d", r=2, h=2)
            for hf in range(2):
                o_ = TC[slot][:, 0:n].rearrange("p (a r h d) -> p a r h d", a=nh, r=2, h=2)[:, :, :, hf, :]
                i_ = TD[slot][:, 0:n].rearrange("p (a r h d) -> p a r h d", a=nh, r=2, h=2)[:, :, :, 1 - hf, :]
                s_ = sBv[:, :, hf, :].unsqueeze(1).broadcast_to([128, nh, 2, 16])
                tt_("dve", o_, i_, s_, ALU.mult, [t["TD"], tTab], [t["TC"]])
            tt_("pool", TE[slot][:, 0:n], TE[slot][:, 0:n], TC[slot][:, 0:n], ALU.add, [t["TE"], t["TC"]], [t["TE"]])
            out_ap_fn(TE[slot][:, 0:n], smq)

        def banks(tt):
            return (0, 1) if tt % 2 == 0 else (2, 3)

        def F0(tt):
            cv_step()
            slot = tt % 2
            b0, b1 = banks(tt)
            xT_for_tile(tt, slot, 4 if slot == 0 else 7)
            inproj(slot, 0, b0)
            inproj(slot, 1, b1)

        def M0(tt):
            slot = tt % 2
            t = tS[slot]
            b0, b1 = banks(tt)
            H0 = psb(b0)
            H0v = H0.rearrange("p (v h d) -> p v h d", v=16, h=2)
            cA = cosA[:, tt, :].unsqueeze(1).broadcast_to([128, 16, 32])
            sA0 = sinA[:, tt, 0:16].unsqueeze(1).broadcast_to([128, 16, 16])
            sA1 = sinA[:, tt, 16:32].unsqueeze(1).broadcast_to([128, 16, 16])
            TAv = TA[slot][:].rearrange("p (v d) -> p v d", v=16)
            TBv = TB[slot][:].rearrange("p (v h d) -> p v h d", v=16, h=2)
            tt_("dve", TAv, H0.rearrange("p (v d) -> p v d", v=16), cA, ALU.mult, [tPS[b0], tTab], [t["TA"]])
            tt_("dve", TBv[:, :, 0, :], H0v[:, :, 1, :], sA0, ALU.mult, [tPS[b0], tTab], [t["TB"]])
            tt_("dve", TBv[:, :, 1, :], H0v[:, :, 0, :], sA1, ALU.mult, [tPS[b0], tTab], [t["TB"]])
            tt_("pool", OBA[slot][:], TA[slot][:], TB[slot][:], ALU.add, [t["TA"], t["TB"]], [t["OBA"]])

            def outq(src, smq, slot=slot, t=t):
                o_ = OBB[slot][:, 0:512].rearrange("p (g j d) -> p j g d", g=4, j=2)
                tt_("pool", o_, src.rearrange("p (j g d) -> p j g d", j=2, g=4),
                    smq.rearrange("p (j g) -> p j g", j=2).unsqueeze(3).broadcast_to([128, 2, 4, 64]), ALU.mult,
                    [t["TE"], t["sm"]], [t["OBB"]])
            rms_rope_B(dict(ap=psb(b1), t=tPS[b1]), 8, 64 * slot, slot, tt, gq_b, outq)

        def E0(tt):
            slot = tt % 2
            t = tS[slot]
            tok = slice(tt * 128, (tt + 1) * 128)
            tb = 5 + slot
            for blk in range(4):
                tr(psb_bf(tb)[:, blk * 128:(blk + 1) * 128], OBA[slot][:, blk * 128:(blk + 1) * 128], [t["OBA"]], [tPS[tb]])
            for blk in range(4):
                tr(psb_bf(tb)[:, (4 + blk) * 128:(5 + blk) * 128], OBB[slot][:, blk * 128:(blk + 1) * 128], [t["OBB"]], [tPS[tb]])
            p5 = psb_bf(tb).rearrange("p (a b) -> p a b", a=8)
            cp("act", AqT[:, :, tok], p5[:, 0:2, :], [tPS[tb]], [tKVQ])
            cp("act", AkT[:, :, tok], p5[:, 2:4, :], [tPS[tb]], [tKVQ])
            cp("act", BqT[:, :, tok], p5[:, 4:8, :], [tPS[tb]], [tKVQ])

        for i in range(NT + 2):
            if i < NT:
                F0(i)
            if 0 <= i - 1 < NT:
                M0(i - 1)
            if 0 <= i - 2 < NT:
                E0(i - 2)

        load_win(l, 1)
        def F1(tt):
            cv_step()
            slot = tt % 2
            b0, b1 = banks(tt)
            xT_for_tile(tt, slot, 4 + slot)
            inproj(slot, 0, b0)
            inproj(slot, 1, b1)

        def M1(tt):
            slot = tt % 2
            t = tS[slot]
            b0, b1 = banks(tt)
            H2 = psb(b0)
            H3 = psb(b1)

            def outk(src, smq, slot=slot, t=t):
                tt_("pool", OBB[slot][:, 0:128].rearrange("p (h d) -> p h d", h=2), src.rearrange("p (h d) -> p h d", h=2),
                    smq.unsqueeze(2).broadcast_to([128, 2, 64]), ALU.mult, [t["TE"], t["sm"]], [t["OBB"]])
            rms_rope_B(dict(ap=H2[:, 0:128], t=tPS[b0]), 2, 64 * slot + 8, slot, tt, gk_b, outk)
            Hav = H2[:, 128:384].rearrange("p (a q d) -> p a q d", a=2, q=2)
            VXa = VX[:, tt, 0:512].rearrange("p (a c) -> p a c", a=2)
            cp("dve", VXa[:, :, 0:64], Hav[:, :, 0, :], [tPS[b0]], [tKVQ])
            cp("dve", VXa[:, :, 192:256], Hav[:, :, 1, :], [tPS[b0]], [tKVQ])
            VXb = VX[:, tt, 512:896].rearrange("p (j c) -> p j c", j=2)
            cp("dve", VXb[:, :, 64:128], H2[:, 384:512].rearrange("p (j d) -> p j d", j=2), [tPS[b0]], [tKVQ])
            UV = TA[slot]
            so = 64 * slot
            act(UV[:], H3, AF.Gelu_apprx_tanh, [tPS[b1]], [t["TA"]])
            P.add("dve", lambda e, so=so, UV=UV: e.bn_stats(out=sm[:, so + 16:so + 22], in_=UV[:, 256:512]), [t["TA"]], [t["sm"]])
            P.add("dve", lambda e, so=so: e.bn_aggr(out=sm[:, so + 22:so + 24], in_=sm[:, so + 16:so + 22].rearrange("p (a b) -> p a b", a=1)),
                  [t["sm"]], [t["sm"]])
            rstd_chain(sm[:, so + 24:so + 25], sm[:, so + 23:so + 24], 1, 1.0, t["sm"])
            ts_("dve", TB[slot][:, 0:256], UV[:, 256:512], sm[:, so + 22:so + 23], sm[:, so + 24:so + 25], ALU.subtract, ALU.mult,
                [t["TA"], t["sm"]], [t["TB"]])
            tt_("pool", TB[slot][:, 0:256], TB[slot][:, 0:256], cg_b[:], ALU.mult, [t["TB"], tPar], [t["TB"]])
            tt_("pool", VLN[slot][:], TB[slot][:, 0:256], cb_b[:], ALU.add, [t["TB"], tPar], [t["VLN"]])

        def E1(tt):
            slot = tt % 2
            t = tS[slot]
            tok = slice(tt * 128, (tt + 1) * 128)
            UV = TA[slot]
            tr(psb_bf(6)[:, 0:128], OBB[slot][:, 0:128], [t["OBB"]], [tPS[6]])
            for g in range(4):
                mm(psb(7)[:, g * 64:(g + 1) * 64], WsT[:, g, :], VLN[slot][:, g * 64:(g + 1) * 64], True, True, [tWs, t["VLN"]], [tPS[7]])
            for g in range(4):
                stt_("dve", CO[slot][:, g * 64:(g + 1) * 64], psb(7)[:, g * 64:(g + 1) * 64], bsT[:, g:g + 1], UV[:, g * 64:(g + 1) * 64],
                     ALU.add, ALU.mult, [tPS[7], tPar, t["TA"]], [t["CO"]])
            for blk in range(2):
                tr(psb_bf(6)[:, (1 + blk) * 128:(2 + blk) * 128], CO[slot][:, blk * 128:(blk + 1) * 128], [t["CO"]], [tPS[6]])
            p6 = psb_bf(6).rearrange("p (a b) -> p a b", a=8)
            cp("act", BkT[:, tok], p6[:, 0, :], [tPS[6]], [tKVQ])
            cp("act", mTC[:, :, tok], p6[:, 1:3, :], [tPS[6]], [tKVQ])

        for i in range(NT + 2):
            if i < NT:
                F1(i)
            if 0 <= i - 1 < NT:
                M1(i - 1)
            if 0 <= i - 2 < NT:
                E1(i - 2)

    def phaseB(l):
        tWO = T("wout")
        dma("pool", WOUT[:], w_out_d[l].rearrange("(ec p) d -> p ec d", p=128), "wout", [], [tWO])
        tPT = [T("pt0"), T("pt1")]
        tR, tTt, tOP, tSQ, tY, tsm = T("R"), T("Tt"), T("OP"), T("SQ"), T("Y"), T("smB")
        tmT = [T("mT%d" % c) for c in range(6)]
        tQM = [T("qm0"), T("qm1")]
        neglam = sm[:, 40:41]
        cnt = {"s": 0, "m": 0, "tick": 0}
        pending = []

        def defer(delay, fn):
            pending.append([cnt["tick"] + delay, fn])

        def run_due(force=False):
            progressed = True
            while progressed:
                progressed = False
                for item in list(pending):
                    if force or item[0] <= cnt["tick"]:
                        pending.remove(item)
                        item[1]()
                        progressed = True

        def prep_qm(d, qs):
            rows = d["rows"]
            P.add("pool", lambda e: e.memset(QM[qs][:], 0.0), [], [tQM[qs]])
            if rows.start == 96:
                P.add("dve", lambda e: e.tensor_copy(out=QM[qs][64:128, :], in_=d["q_src64"]), [], [tQM[qs]])
                P.add("dve", lambda e: e.memset(QM[qs][64:96, :], 0.0), [], [tQM[qs]])
            else:
                P.add("dve", lambda e: e.tensor_copy(out=QM[qs][rows, :], in_=d["q_src"]), [], [tQM[qs]])

        def run_map(d, qs, acc):
            accb = (2 * acc, 2 * acc + 1)
            kT_fn, v_fn, scale = d["kT_fn"], d["v_fn"], d["scale"]
            pend = None
            for kb in range(NT):
                sp_ = cnt["s"] % 2
                cnt["s"] += 1
                sb_ = (2 * sp_, 2 * sp_ + 1)
                for j in range(2):
                    mm(psb(sb_[j]), kT_fn(kb), QM[qs][:, j * 512:(j + 1) * 512], True, True, [tQM[qs]], [tPS[sb_[j]]])
                act(PT[sp_][:], PS[sp_][:], AF.Exp, [tPS[sb_[0]], tPS[sb_[1]]], [tPT[sp_]], scale=scale)
                if pend is not None:
                    pk, ps_ = pend
                    for j in range(2):
                        mm(psb(accb[j]), v_fn(pk), PT[ps_][:, j * 512:(j + 1) * 512], pk == 0, pk == NT - 1, [tPT[ps_]], [tPS[accb[j]]])
                pend = (kb, sp_)
                cnt["tick"] += 1
                run_due()
            pk, ps_ = pend
            for j in range(2):
                mm(psb(accb[j]), v_fn(pk), PT[ps_][:, j * 512:(j + 1) * 512], pk == 0, pk == NT - 1, [tPT[ps_]], [tPS[accb[j]]])

        def tail_A(acc, hh, c, pp):
            dr = slice(64 * hh, 64 * hh + 64)
            nr = slice(64 * (1 - hh), 64 * (1 - hh) + 64)
            ta = [tPS[2 * acc], tPS[2 * acc + 1]]
            P.add("dve", lambda e: e.reciprocal(out=Rt[dr, :], in_=PS[acc][nr, :]), ta, [tR])
            if c == 0:
                tt_("dve", OP[dr, :], PS[acc][dr, :], Rt[dr, :], ALU.mult, ta + [tR], [tOP])
                return
            tt_("dve", Tt[dr, :], PS[acc][dr, :], Rt[dr, :], ALU.mult, ta + [tR], [tTt])
            stt_("dve", OP[dr, :], Tt[dr, :], neglam[dr, :], OP[dr, :], ALU.mult, ALU.add, [tTt, tOP], [tOP])
            if hh == 0:
                return
            tt_("pool", SQ[:], OP[:], OP[:], ALU.mult, [tOP], [tSQ])

            def st2():
                for j in range(2):
                    mm(psb(2 * acc + j), bdm[:], SQ[:, j * 512:(j + 1) * 512], True, True, [tSQ, tConst], [tPS[2 * acc + j]])
                defer(3, st3)

            def st3():
                act(Rt[:], PS[acc][:], AF.Sqrt, ta, [tR], bias=epsT[:, 0:1])
                defer(2, st4)

            def st4():
                P.add("dve", lambda e: e.reciprocal(out=Rt[:], in_=Rt[:]), [tR], [tR])
                stt_("dve", mTAB[:, pp, :], OP[:], gA2[:, 0:1], Rt[:], ALU.mult, ALU.mult, [tOP, tR, tPar], [tmT[pp]])
            defer(4, st2)

        def tail_B(acc, hh, cB):
            dr = slice(64 * hh, 64 * hh + 64)
            nr = slice(64 * (1 - hh), 64 * (1 - hh) + 64)
            ta = [tPS[2 * acc], tPS[2 * acc + 1]]
            P.add("dve", lambda e: e.reciprocal(out=Rt[dr, :], in_=PS[acc][nr, :]), ta, [tR])
            tt_("dve", mTAB[dr, 2 + cB, :], PS[acc][dr, :], Rt[dr, :], ALU.mult, ta + [tR], [tmT[2 + cB]])

        for qh in range(2):
            q0 = qh * 1024
            maps = []
            for pp in range(2):
                for hh in range(2):
                    h = 2 * pp + hh
                    for c in range(2):
                        r0 = (hh * 2 + c) * 32
                        maps.append(dict(
                            kT_fn=lambda kb, pp=pp: AkT[:, pp, kb * 128:(kb + 1) * 128],
                            q_src=AqT[r0:r0 + 32, pp, q0:q0 + 1024], rows=slice(r0, r0 + 32),
                            q_src64=AqT[64:128, pp, q0:q0 + 1024],
                            v_fn=lambda kb, h=h: VX[:, kb, h * 128:(h + 1) * 128],
                            scale=32 ** -0.5,
                            tail=lambda acc, hh=hh, c=c, pp=pp: tail_A(acc, hh, c, pp)))
            for cB in range(4):
                for hh in range(2):
                    hB = 2 * cB + hh
                    j_kv, g = hB // 4, hB % 4
                    rows = slice(64 * j_kv, 64 * j_kv + 64)
                    voff = 512 + j_kv * 192 + (64 if hh == 0 else 0)
                    maps.append(dict(
                        kT_fn=lambda kb: BkT[:, kb * 128:(kb + 1) * 128],
                        q_src=BqT[rows, g, q0:q0 + 1024], rows=rows, q_src64=None,
                        v_fn=lambda kb, voff=voff: VX[:, kb, voff:voff + 128],
                        scale=64 ** -0.5,
                        tail=lambda acc, hh=hh, cB=cB: tail_B(acc, hh, cB)))
            prep_qm(maps[0], cnt["m"] % 2)
            for i, d in enumerate(maps):
                qs = cnt["m"] % 2
                acc = 2 + (cnt["m"] % 2)
                cnt["m"] += 1
                if i + 1 < len(maps):
                    prep_qm(maps[i + 1], cnt["m"] % 2)
                cv_step()
                run_map(d, qs, acc)
                defer(3, lambda d=d, acc=acc: d["tail"](acc))
            run_due(force=True)
            for tl in range(8):
                tt = qh * 8 + tl
                acc = 2 + (tl % 2)
                for dg in range(2):
                    for ec in range(8):
                        if ec < 6:
                            lhs = mTAB[:, ec, tl * 128:(tl + 1) * 128]
                            rd = [tmT[ec], tWO]
                        else:
                            lhs = mTC[:, ec - 6, tt * 128:(tt + 1) * 128]
                            rd = [tWO]
                        mm(psb(2 * acc + dg), lhs, WOUT[:, ec, dg * 512:(dg + 1) * 512], ec == 0, ec == 7, rd, [tPS[2 * acc + dg]])
                ta = [tPS[2 * acc], tPS[2 * acc + 1]]
                stt_("dve", Yb[:], X[:, tt, :], float(ALPHA), PS[acc][:], ALU.mult, ALU.add, ta + [tX[tt]], [tY])
                layernorm_tail(Yb, tY, tt, tsm)

    def layernorm_tail(Y, tY, tt, tsm):
        for j in range(2):
            P.add("dve", lambda e, j=j: e.bn_stats(out=sm[:, 44 + 6 * j:50 + 6 * j], in_=Y[:, j * 512:(j + 1) * 512]), [tY], [tsm])
        P.add("dve", lambda e: e.bn_aggr(out=sm[:, 56:58], in_=sm[:, 44:56].rearrange("p (a b) -> p a b", a=2)), [tsm], [tsm])
        rstd_chain(sm[:, 58:59], sm[:, 57:58], 1, 1.0, tsm)
        stt_("dve", sm[:, 59:60], sm[:, 56:57], -1.0, sm[:, 58:59], ALU.mult, ALU.mult, [tsm], [tsm])
        act(X[:, tt, :], Y[:], AF.Identity, [tY, tsm], [tX[tt]], scale=sm[:, 58:59], bias=sm[:, 59:60])

    def phaseC(l, s, last):
        cv_step(1000)
        tG1, tG2 = T("g1"), T("g2")
        dma("sp", g1_b[:], l1g_d[l:l + 1, :].broadcast_to([128, D]), "lng", [], [tG1])
        dma("sp", b1_b[:], l1b_d[l:l + 1, :].broadcast_to([128, D]), "lng", [], [tG1])
        dma("sp", g2_b[:], l2g_d[l:l + 1, :].broadcast_to([128, D]), "lng2", [], [tG2])
        dma("sp", b2_b[:], l2b_d[l:l + 1, :].broadcast_to([128, D]), "lng2", [], [tG2])
        tWG = [T("wg0"), T("wg1")]
        tWU = [T("wu0"), T("wu1")]
        tWD = [T("wd0"), T("wd1")]
        thid = [T("hid%d" % f) for f in range(NFC)]
        txTf = T("xTf")
        tXB, tSG, tY, tsm = T("XBc"), [T("sg0"), T("sg1")], T("Yc"), T("smC")
        wgs = wg_d[l].rearrange("(kc p) f -> p kc f", p=128)
        wus = wu_d[l].rearrange("(kc p) f -> p kc f", p=128)
        wds = wd_d[l].rearrange("(fc p) d -> p fc d", p=128)
        seq = {"gu": 0, "d": 0}

        def load_gu(fb):
            sl = seq["gu"] % 2
            seq["gu"] += 1
            if USE_CV:
                dma("sp", WG[sl][:].rearrange("p a b -> p (a b)"), wg_bf[l, fb], "wg%d" % sl, [tCV[l]], [tWG[sl]])
                dma("sp", WU[sl][:].rearrange("p a b -> p (a b)"), wu_bf[l, fb], "wu%d" % sl, [tCV[l]], [tWU[sl]])
            else:
                dma("pool", WG[sl][:], wgs[:, :, fb * 256:(fb + 1) * 256], "wg%d" % sl, [], [tWG[sl]])
                dma("pool", WU[sl][:], wus[:, :, fb * 256:(fb + 1) * 256], "wu%d" % sl, [], [tWU[sl]])
            return sl

        def load_d(db):
            sl = seq["d"] % 2
            seq["d"] += 1
            n = 4 if db < 5 else 2
            if USE_CV:
                dma("sp", WD[sl][:, 0:n, :].rearrange("p a b -> p (a b)"), wd_bf[l, db, :, 0:n * 1024], "wd%d" % sl, [tCV[l]], [tWD[sl]])
            else:
                dma("pool", WD[sl][:, 0:n, :], wds[:, db * 4:db * 4 + n, :], "wd%d" % sl, [], [tWD[sl]])
            return sl

        for tg in range(4):
            nxt = load_gu(0)
            for tl in range(4):
                tt = tg * 4 + tl
                tt_("pool", X[:, tt, :], X[:, tt, :], g1_b[:], ALU.mult, [tX[tt], tG1], [tX[tt]])
                tt_("pool", X[:, tt, :], X[:, tt, :], b1_b[:], ALU.add, [tX[tt], tG1], [tX[tt]])
                cp("act", XBc[:], X[:, tt, :], [tX[tt]], [tXB])
                bank = 4 + (tl % 2)
                for kc in range(8):
                    tr(psb_bf(bank)[:, kc * 128:(kc + 1) * 128], XBc[:, kc * 128:(kc + 1) * 128], [tXB], [tPS[bank]])
                cp("dve", xTf[:, :, tl * 128:(tl + 1) * 128], psb_bf(bank).rearrange("p (a b) -> p a b", a=8), [tPS[bank]], [txTf])
            for fb in range(11):
                sl = nxt
                if fb + 1 < 11:
                    nxt = load_gu(fb + 1)
                for fci in range(2):
                    fc = 2 * fb + fci
                    gb = 0 + (fc % 2)
                    ub = 2 + (fc % 2)
                    for kc in range(8):
                        mm(psb(gb), WG[sl][:, kc, fci * 128:(fci + 1) * 128], xTf[:, kc, :], kc == 0, kc == 7, [tWG[sl], txTf], [tPS[gb]])
                    for kc in range(8):
                        mm(psb(ub), WU[sl][:, kc, fci * 128:(fci + 1) * 128], xTf[:, kc, :], kc == 0, kc == 7, [tWU[sl], txTf], [tPS[ub]])
                    act(SG[fc % 2][:], psb(gb), AF.Silu, [tPS[gb]], [tSG[fc % 2]])
                    tt_("dve", hidT[:, fc, :], psb(ub), SG[fc % 2][:], ALU.mult, [tPS[ub], tSG[fc % 2]], [thid[fc]])
            nd = load_d(0)
            for db in range(6):
                sl = nd
                if db + 1 < 6:
                    nd = load_d(db + 1)
                n = 4 if db < 5 else 2
                for fci in range(n):
                    fc = db * 4 + fci
                    for tl in range(4):
                        for dg in range(2):
                            mm(psb(2 * tl + dg), hidT[:, fc, tl * 128:(tl + 1) * 128], WD[sl][:, fci, dg * 512:(dg + 1) * 512],
                               fc == 0, fc == NFC - 1, [thid[fc], tWD[sl]], [tPS[2 * tl + dg]])
            for tl in range(4):
                tt = tg * 4 + tl
                ta = [tPS[2 * tl], tPS[2 * tl + 1]]
                stt_("dve", Yc[:], X[:, tt, :], float(ALPHA), PS[tl][:], ALU.mult, ALU.add, ta + [tX[tt]], [tY])
                layernorm_tail(Yc, tY, tt, tsm)
                tt_("pool", X[:, tt, :], X[:, tt, :], g2_b[:], ALU.mult, [tX[tt], tG2], [tX[tt]])
                tt_("pool", X[:, tt, :], X[:, tt, :], b2_b[:], ALU.add, [tX[tt], tG2], [tX[tt]])
                if last:
                    dma("sp", out_d[s, tt * 128:(tt + 1) * 128, :], X[:, tt, :], "out", [tX[tt]], [])

    load_win(0, 0)
    load_tables()
    load_params(0)
    if USE_CV:
        convert_layer(0)
    for s in range(nseq):
        if s > 0:
            P.barrier()
        for q in range(4):
            dma("sp", X[:, q * 4:(q + 1) * 4, :], x_d[s, q * 512:(q + 1) * 512, :].rearrange("(t p) d -> p t d", p=128), "x%d" % q,
                [], [tX[q * 4 + i] for i in range(4)])
        def dbg_store():
            P.barrier()
            for tt in range(NT):
                dma("sp", out_d[s, tt * 128:(tt + 1) * 128, :], X[:, tt, :], "out", [tX[tt]], [])
        for l in range(L):
            P.barrier()
            if stop == "load":
                dbg_store()
                break
            phaseA(l)
            P.barrier()
            if stop == "A":
                dbg_store()
                break
            if USE_CV and s == 0 and l + 1 < L:
                convert_layer(l + 1)
            phaseB(l)
            P.barrier()
            if stop == "B":
                dbg_store()
                break
            nl = l + 1 if l + 1 < L else (0 if s + 1 < nseq else None)
            if nl is not None:
                load_win(nl, 0)
                load_tables()
                load_params(nl)
            phaseC(l, s, l == L - 1)
    P.barrier()
    P.finalize_and_emit(st)
    st.close()
    return nc, P


_CACHE = {}


def kernel(**inputs):
    n_cores = 8
    nseq = 32 // n_cores
    depth = 4
    if "nc" not in _CACHE:
        _CACHE["nc"] = build(nseq, depth)[0]
    nc = _CACHE["nc"]
    consts = host_consts()
    maps = []
    for c in range(n_cores):
        m = {k: np.ascontiguousarray(np.asarray(v, dtype=np.float32)) for k, v in inputs.items() if k != "x"}
        m["x"] = np.ascontiguousarray(np.asarray(inputs["x"], dtype=np.float32)[c * nseq:(c + 1) * nseq])
        m.update(consts)
        maps.append(m)
    res = run_bass_kernel_spmd(nc, maps, core_ids=list(range(n_cores)))
    return np.concatenate([np.asarray(r["out"]) for r in res.results], axis=0).astype(np.float32)
```

```python
import math, os
CUT = int(os.environ.get('KB_CUT', '99'))
USE_CV = True
import numpy as np
from contextlib import ExitStack
import concourse.bass as bass
import concourse.mybir as mybir

from concourse.bass_utils import run_bass_kernel_spmd


ENGS = ("pe", "act", "dve", "pool", "sp")


class T:
    __slots__ = ("name", "w", "r")

    def __init__(self, name=""):
        self.name = name
        self.w = None
        self.r = []


class Op:
    __slots__ = ("fn", "deps", "signal", "dma_sem", "dma_val")

    def __init__(self, fn, deps, dma_sem=None, dma_val=0):
        self.fn = fn
        self.deps = deps
        self.signal = False
        self.dma_sem = dma_sem
        self.dma_val = dma_val


class Prog:
    def __init__(self, nc):
        self.nc = nc
        self.ops = {e: [] for e in ENGS}
        self.dma_counts = {}
        self.n_wait = 0

    def _deps(self, eng, reads, writes):
        deps = set()
        for t in reads:
            if t.w is not None:
                deps.add(t.w)
        for t in writes:
            if t.w is not None:
                deps.add(t.w)
            for x in t.r:
                deps.add(x)
        best = {}
        for d in deps:
            if d[0] == "c" and d[1] == eng and eng == "pe":
                continue
            k = (d[0], d[1])
            if k not in best or best[k][2] < d[2]:
                best[k] = d
        return list(best.values())

    def add(self, eng, fn, reads=(), writes=()):
        deps = self._deps(eng, reads, writes)
        idx = len(self.ops[eng])
        self.ops[eng].append(Op(fn, deps))
        tok = ("c", eng, idx)
        for t in reads:
            t.r.append(tok)
        for t in writes:
            t.w = tok
            t.r = []
        return tok

    def dma(self, eng, fn, sem_key, reads=(), writes=()):
        deps = self._deps(eng, reads, writes)
        n = self.dma_counts.get(sem_key, 0) + 1
        self.dma_counts[sem_key] = n
        self.ops[eng].append(Op(fn, deps, dma_sem=sem_key, dma_val=16 * n))
        tok = ("d", sem_key, 16 * n)
        for t in reads:
            t.r.append(tok)
        for t in writes:
            t.w = tok
            t.r = []
        return tok

    def barrier(self):
        toks = []
        for e in ENGS:
            for i in range(len(self.ops[e]) - 1, -1, -1):
                op = self.ops[e][i]
                if op.fn is not None and op.dma_sem is None:
                    toks.append(("c", e, i))
                    break
        for k, n in self.dma_counts.items():
            if not str(k).startswith("cv"):
                toks.append(("d", k, 16 * n))
        for e in ENGS:
            self.wait_tokens(e, [t for t in toks if not (t[0] == "c" and t[1] == e and e == "pe")])

    def wait_tokens(self, eng, toks):
        self.ops[eng].append(Op(None, list(toks)))

    def finalize_and_emit(self, stack):
        nc = self.nc
        for e in ENGS:
            for op in self.ops[e]:
                for d in op.deps:
                    if d[0] == "c":
                        self.ops[d[1]][d[2]].signal = True
        val = {}
        for e in ENGS:
            c = 0
            for i, op in enumerate(self.ops[e]):
                if op.signal:
                    c += 1
                    val[(e, i)] = c
            assert c < 60000, (e, c)
        csem = {e: stack.enter_context(nc.semaphore("cs_" + e)) for e in ENGS}
        dsem = {k: stack.enter_context(nc.semaphore("ds_" + str(k))) for k in self.dma_counts}
        handles = {"pe": "tensor", "act": "scalar", "dve": "vector", "pool": "gpsimd", "sp": "sync"}
        block = stack.enter_context(nc.Block())
        prog = self

        def make(e):
            def body(eng):
                waited = {}
                for i, op in enumerate(prog.ops[e]):
                    need = {}
                    for d in op.deps:
                        if d[0] == "c":
                            key = ("c", d[1])
                            v = val[(d[1], d[2])]
                        else:
                            key = ("d", d[1])
                            v = d[2]
                        if waited.get(key, 0) >= v:
                            continue
                        if need.get(key, 0) < v:
                            need[key] = v
                    for key, v in need.items():
                        sem = csem[key[1]] if key[0] == "c" else dsem[key[1]]
                        eng.wait_ge(sem, v)
                        waited[key] = v
                        prog.n_wait += 1
                    if op.fn is None:
                        continue
                    ins = op.fn(eng)
                    if op.dma_sem is not None:
                        ins.then_inc(dsem[op.dma_sem], 16)
                    elif op.signal:
                        ins.then_inc(csem[e], 1)
            return body

        for e in ENGS:
            if not self.ops[e]:
                continue
            getattr(block, handles[e])(make(e))


F32 = mybir.dt.float32
BF16 = mybir.dt.bfloat16
AF = mybir.ActivationFunctionType
ALU = mybir.AluOpType
AX = mybir.AxisListType

D = 1024
S = 2048
NT = 16
HID = 2816
NFC = 22
EPS = 1e-5
ALPHA = (2 * 4) ** 0.25
KB = 1024


def host_consts():
    inv = 1.0 / (10000.0 ** (np.arange(0, 32, 2, dtype=np.float32) / 32.0))
    t = np.arange(S, dtype=np.float32)

    def cs(pos):
        ang = pos[:, None] * inv[None, :]
        ang = np.concatenate([ang, ang], -1)
        c = np.cos(ang).astype(np.float32)
        s = np.sin(ang).astype(np.float32)
        s[:, :16] *= -1.0
        return c, s
    cA, sA = cs(t)
    row = np.floor(t / 64.0).astype(np.float32)
    col = (t - 64.0 * row).astype(np.float32)
    cR, sR = cs(row)
    cC, sC = cs(col)
    cB = np.concatenate([cR, cC], -1)
    sB = np.concatenate([sR, sC], -1)

    def lay(a):
        n = a.shape[1]
        return np.ascontiguousarray(a.reshape(NT, 128, n).transpose(1, 0, 2).reshape(128, NT * n))
    tabs = np.concatenate([lay(cA), lay(sA), lay(cB), lay(sB)], 1).astype(np.float32)
    ident = np.eye(128, dtype=np.float32)
    bd = np.zeros((128, 128), np.float32)
    bd[:64, :64] = 1.0 / 64
    bd[64:, 64:] = 1.0 / 64
    return dict(tabs=tabs, ident=ident, bd=bd)


def build(nseq, depth, stop=None):
    nc = bass.Bass("TRN2", target_bir_lowering=False)
    L = depth

    def din(name, shape):
        return nc.dram_tensor(name, shape, F32, kind="ExternalInput").ap()
    x_d = din("x", [nseq, S, D])
    w_in_d = din("w_in", [L, D, 2048])
    w_out_d = din("w_out", [L, D, D])
    lam_d = din("lam_qk", [L, 4, 32])
    asg_d = din("a_subln_g", [L, 64])
    bqg_d = din("b_q_norm_g", [L, 64])
    bkg_d = din("b_k_norm_g", [L, 64])
    clg_d = din("c_ln_g", [L, 256])
    clb_d = din("c_ln_b", [L, 256])
    cws_d = din("c_w_s", [L, 4, 128, 128])
    cbs_d = din("c_b_s", [L, 4, 128])
    l1g_d = din("ln1_g", [L, D])
    l1b_d = din("ln1_b", [L, D])
    wg_d = din("w_gate", [L, D, HID])
    wu_d = din("w_up", [L, D, HID])
    wd_d = din("w_down", [L, HID, D])
    l2g_d = din("ln2_g", [L, D])
    l2b_d = din("ln2_b", [L, D])
    tabs_d = din("tabs", [128, 3072])
    ident_d = din("ident", [128, 128])
    bd_d = din("bd", [128, 128])
    out_d = nc.dram_tensor("out", [nseq, S, D], F32, kind="ExternalOutput").ap()
    if USE_CV:
        wg_bf = nc.dram_tensor("wg_bf", [L, 11, 128, 2048], BF16, kind="Internal").ap()
        wu_bf = nc.dram_tensor("wu_bf", [L, 11, 128, 2048], BF16, kind="Internal").ap()
        wd_bf = nc.dram_tensor("wd_bf", [L, 6, 128, 4096], BF16, kind="Internal").ap()

    st = ExitStack()

    def sb(name, shape, dt):
        return st.enter_context(nc.sbuf_tensor(name, shape, dt))
    X = sb("X", [128, NT, D], F32)
    REG = sb("REG", [128, 80 * KB // 4], F32)
    AR = sb("AR", [128, 57088 // 4], F32)
    ident = sb("identb", [128, 128], BF16)
    bdm = sb("bdm", [128, 128], BF16)
    Ws = sb("Ws", [128, 4, 128], BF16)
    WsT = sb("WsT", [128, 4, 128], BF16)
    gq_b = sb("gq_b", [128, 64], F32)
    gk_b = sb("gk_b", [128, 64], F32)
    cg_b = sb("cg_b", [128, 256], F32)
    cb_b = sb("cb_b", [128, 256], F32)
    bsT = sb("bsT", [128, 4], F32)
    lamq = sb("lamq", [128, 128], F32)
    gA2 = sb("gA2", [128, 1], F32)
    sm = sb("sm", [128, 128], F32)
    PR = sb("PR", [128, 64], F32)
    epsT = sb("epsT", [128, 1], F32)
    g1T = sb("g1T", [128, 8], F32)
    b1T = sb("b1T", [128, 8], F32)
    PS = [st.enter_context(nc.psum_tensor("PS%d" % i, [128, 1024], F32)) for i in range(4)]

    def view(base, off, shape, dt):
        esz = 2 if dt == BF16 else 4
        n = int(np.prod(shape[1:]))
        nb = n * esz
        assert off % 4 == 0 and nb % 4 == 0
        ap = base[:, off // 4:(off + nb) // 4]
        if dt == BF16:
            ap = ap.bitcast(BF16)
        if len(shape) == 3:
            ap = ap.rearrange("p (a b) -> p a b", a=shape[1])
        return ap

    AqT = view(REG, 0, [128, 2, S], BF16)
    AkT = view(REG, 8 * KB, [128, 2, S], BF16)
    BqT = view(REG, 16 * KB, [128, 4, S], BF16)
    BkT = view(REG, 32 * KB, [128, S], BF16)
    VX = view(REG, 36 * KB, [128, NT, 896], BF16)
    mTC = view(REG, 64 * KB, [128, 2, S], BF16)
    xTt = [view(REG, 75 * KB + i * 2 * KB, [128, 8, 128], BF16) for i in range(2)]
    hidT = view(REG, 0, [128, NFC, 512], BF16)
    xTf = view(REG, 22 * KB, [128, 8, 512], BF16)
    WG = [view(REG, 30 * KB + i * 4 * KB, [128, 8, 256], BF16) for i in range(2)]
    WU = [view(REG, 38 * KB + i * 4 * KB, [128, 8, 256], BF16) for i in range(2)]
    WD = [view(REG, 46 * KB + i * 8 * KB, [128, 4, 1024], BF16) for i in range(2)]
    XBc = view(REG, 62 * KB, [128, 1024], BF16)
    SG = [view(REG, 64 * KB + i * 2 * KB, [128, 512], F32) for i in range(2)]
    Yc = view(REG, 68 * KB, [128, 1024], F32)
    g2_b = view(REG, 72 * KB, [128, 1024], F32)
    b2_b = view(REG, 76 * KB, [128, 1024], F32)
    WIN = view(AR, 0, [128, 8, 1024], BF16)
    cosA = view(AR, 16 * KB, [128, NT, 32], F32)
    sinA = view(AR, 18 * KB, [128, NT, 32], F32)
    cosB = view(AR, 20 * KB, [128, NT, 64], F32)
    sinB = view(AR, 24 * KB, [128, NT, 64], F32)
    TA = [view(AR, 28 * KB + i * 2 * KB, [128, 512], F32) for i in range(2)]
    TB = [view(AR, 32 * KB + i * 2 * KB, [128, 512], F32) for i in range(2)]
    TC = [view(AR, 36 * KB + i * 2 * KB, [128, 512], F32) for i in range(2)]
    TD = [view(AR, 40 * KB + i * 2 * KB, [128, 512], F32) for i in range(2)]
    TE = [view(AR, 44 * KB + i * 2 * KB, [128, 512], F32) for i in range(2)]
    OBA = [view(AR, 48 * KB + i * KB, [128, 512], BF16) for i in range(2)]
    OBB = [view(AR, 50 * KB + i * 1280, [128, 640], BF16) for i in range(2)]
    XB = view(AR, 50 * KB + 2560, [128, 1024], BF16)
    VLN = [view(REG, 72 * KB + i * 512, [128, 256], BF16) for i in range(2)]
    CO = [view(REG, 73 * KB + i * 512, [128, 256], BF16) for i in range(2)]
    WOUT = view(AR, 0, [128, 8, 1024], BF16)
    mTAB = view(AR, 16 * KB, [128, 6, 1024], BF16)
    PT = [view(AR, 28 * KB + i * 2 * KB, [128, 1024], BF16) for i in range(2)] + [view(AR, 46 * KB, [128, 1024], BF16)]
    Rt = view(AR, 32 * KB, [128, 1024], F32)
    Tt = view(AR, 36 * KB, [128, 1024], F32)
    OP = view(AR, 40 * KB, [128, 1024], F32)
    SQ = view(AR, 44 * KB, [128, 1024], BF16)
    Yb = view(AR, 46 * KB, [128, 1024], F32)
    QM = [view(AR, 50 * KB + i * 2 * KB, [128, 1024], BF16) for i in range(2)]
    g1_b = view(AR, 44 * KB, [128, 1024], F32)
    b1_b = view(AR, 48 * KB, [128, 1024], F32)
    xTf2 = [xTf, view(AR, 28 * KB, [128, 8, 512], BF16)]
    Ys = [Yc, view(AR, 36 * KB, [128, 1024], F32), view(AR, 40 * KB, [128, 1024], F32)]

    P = Prog(nc)
    tX = [T("x%d" % i) for i in range(NT)]
    tPS = [T("ps%d" % i) for i in range(8)]

    def psb(i):
        return PS[i // 2][:, (i % 2) * 512:(i % 2) * 512 + 512]

    def psb_bf(i):
        return PS[i // 2][:, (i % 2) * 512:(i % 2) * 512 + 512].bitcast(BF16)
    tConst = T("const")
    tPar = T("par")
    tWs = T("Ws")
    tTab = T("tab")
    tWIN = [T("win%d" % g) for g in range(2)]

    def mm(out, lhsT, rhs, start, stop, reads, writes, tp=None):
        if tp is None:
            P.add("pe", lambda e: e.matmul(out, lhsT=lhsT, rhs=rhs, start=start, stop=stop), reads, writes)
        else:
            P.add("pe", lambda e: e.matmul(out, lhsT=lhsT, rhs=rhs, start=start, stop=stop, tile_position=tp), reads, writes)

    def tr(out, in_, reads, writes):
        P.add("pe", lambda e: e.transpose(out=out, in_=in_, identity=ident[:]), list(reads) + [tConst], writes)

    def act(out, in_, func, reads, writes, scale=None, bias=None):
        kw = {}
        if scale is not None:
            kw["scale"] = scale
        if bias is not None:
            kw["bias"] = bias
        P.add("act", lambda e: e.activation(out=out, in_=in_, func=func, **kw), reads, writes)

    def tt_(eng, out, in0, in1, op, reads, writes):
        P.add(eng, lambda e: e.tensor_tensor(out=out, in0=in0, in1=in1, op=op), reads, writes)

    def ts_(eng, out, in0, s1, s2, op0, op1, reads, writes):
        if op1 is None:
            P.add(eng, lambda e: e.tensor_scalar(out=out, in0=in0, scalar1=s1, scalar2=None, op0=op0), reads, writes)
        else:
            P.add(eng, lambda e: e.tensor_scalar(out=out, in0=in0, scalar1=s1, scalar2=s2, op0=op0, op1=op1), reads, writes)

    def stt_(eng, out, in0, scalar, in1, op0, op1, reads, writes):
        P.add(eng, lambda e: e.scalar_tensor_tensor(out=out, in0=in0, scalar=scalar, in1=in1, op0=op0, op1=op1), reads, writes)

    def cp(eng, out, in_, reads, writes):
        if eng == "act":
            P.add("act", lambda e: e.copy(out=out, in_=in_), reads, writes)
        else:
            P.add(eng, lambda e: e.tensor_copy(out=out, in_=in_), reads, writes)

    def dma(eng, out, in_, key, reads, writes, slow=False):
        if slow:
            P.dma(eng, lambda e: e.dma_start(out=out, in_=in_, allow_slow_non_contiguous=True), key, reads, writes)
        else:
            P.dma(eng, lambda e: e.dma_start(out=out, in_=in_), key, reads, writes)

    def rstd_chain(dst, src, n, scale, reads_t):
        ts_("dve", dst, src, scale, EPS, ALU.mult, ALU.add, [reads_t], [reads_t])
        act(dst, dst, AF.Sqrt, [reads_t], [reads_t])
        P.add("dve", lambda e: e.reciprocal(out=dst, in_=dst), [reads_t], [reads_t])

    dma("pool", ident[:], ident_d[:, :], "c0", [], [tConst])
    dma("pool", bdm[:], bd_d[:, :], "c1", [], [tConst])
    P.add("dve", lambda e: e.memset(epsT[:], EPS), [], [tConst])

    def load_tables():
        dma("sp", cosA.rearrange("p a b -> p (a b)"), tabs_d[:, 0:512], "tab", [], [tTab])
        dma("sp", sinA.rearrange("p a b -> p (a b)"), tabs_d[:, 512:1024], "tab", [], [tTab])
        dma("sp", cosB.rearrange("p a b -> p (a b)"), tabs_d[:, 1024:2048], "tab", [], [tTab])
        dma("sp", sinB.rearrange("p a b -> p (a b)"), tabs_d[:, 2048:3072], "tab", [], [tTab])

    def load_win(l, half):
        src = w_in_d[l].rearrange("(kc p) c -> p kc c", p=128)
        if half == 0:
            lst = ((0, 512, 0, 0), (768, 1280, 512, 1))
        else:
            lst = ((1280, 1408, 0, 0), (512, 768, 128, 0), (1408, 1536, 384, 0), (1536, 2048, 512, 1))
        for (s0, s1, d0, g) in lst:
            dma("pool", WIN[:, :, d0:d0 + (s1 - s0)], src[:, :, s0:s1], "win%d" % g, [], [tWIN[g]])

    def load_params(l):
        dma("sp", gq_b[:], bqg_d[l:l + 1, :].broadcast_to([128, 64]), "par", [], [tPar])
        dma("sp", gk_b[:], bkg_d[l:l + 1, :].broadcast_to([128, 64]), "par", [], [tPar])
        dma("sp", cg_b[:], clg_d[l:l + 1, :].broadcast_to([128, 256]), "par", [], [tPar])
        dma("sp", cb_b[:], clb_d[l:l + 1, :].broadcast_to([128, 256]), "par", [], [tPar])
        dma("sp", bsT[:], cbs_d[l].rearrange("g p -> p g"), "par", [], [tPar], slow=True)
        dma("sp", lamq[:], lam_d[l:l + 1].rearrange("o a b -> o (a b)").broadcast_to([128, 128]), "par", [], [tPar])
        dma("sp", gA2[0:64, :], asg_d[l].rearrange("(d o) -> d o", o=1), "par", [], [tPar])
        dma("sp", gA2[64:128, :], asg_d[l].rearrange("(d o) -> d o", o=1), "par", [], [tPar])
        dma("pool", Ws[:], cws_d[l].rearrange("g p q -> p g q"), "ws", [], [tWs])

    tCV = [T("cv%d" % l) for l in range(L)]

    cvq = []

    def convert_layer(l):
        wgs = wg_d[l].rearrange("(kc p) f -> p kc f", p=128)
        wus = wu_d[l].rearrange("(kc p) f -> p kc f", p=128)
        wds = wd_d[l].rearrange("(fc p) d -> p fc d", p=128)
        for fb in range(11):
            cvq.append(lambda l=l, fb=fb: dma("pool", wg_bf[l, fb].rearrange("p (a b) -> p a b", a=8), wgs[:, :, fb * 256:(fb + 1) * 256],
                                               "cv%d" % l, [], [tCV[l]]))
            cvq.append(lambda l=l, fb=fb: dma("pool", wu_bf[l, fb].rearrange("p (a b) -> p a b", a=8), wus[:, :, fb * 256:(fb + 1) * 256],
                                               "cv%d" % l, [], [tCV[l]]))
        for db in range(6):
            n = 4 if db < 5 else 2
            cvq.append(lambda l=l, db=db, n=n: dma("pool", wd_bf[l, db, :, 0:n * 1024].rearrange("p (a b) -> p a b", a=n),
                                                    wds[:, db * 4:db * 4 + n, :], "cv%d" % l, [], [tCV[l]]))

    def cv_step(n=1):
        for _ in range(n):
            if cvq:
                cvq.pop(0)()

    def phaseA(l):
        lam_init = 0.8 - 0.6 * math.exp(-0.3 * l)
        tS = [{k: T(k + str(i)) for k in ("TA", "TB", "TC", "TD", "TE", "OBA", "OBB", "VLN", "CO", "sm", "xT")} for i in range(2)]
        tXB, tsm0 = T("XB"), T("smA")
        tKVQ = T("kvq")
        lq = lamq[:].rearrange("p (a b c) -> p a b c", a=2, b=2)
        tt_("dve", PR[:].rearrange("p (a c) -> p a c", a=2), lq[:, :, 0, :], lq[:, :, 1, :], ALU.mult, [tPar], [tsm0])
        P.add("dve", lambda e: e.tensor_reduce(out=sm[:, 32:34], in_=PR[:].rearrange("p (a c) -> p a c", a=2), axis=AX.X, op=ALU.add),
              [tsm0], [tsm0])
        act(sm[:, 34:36], sm[:, 32:34], AF.Exp, [tsm0], [tsm0])
        tt_("dve", sm[:, 36:37], sm[:, 35:36], sm[:, 34:35], ALU.subtract, [tsm0], [tsm0])
        ts_("dve", sm[:, 40:41], sm[:, 36:37], -lam_init, None, ALU.add, None, [tsm0], [tsm0])
        ts_("dve", gA2[:], gA2[:], 1.0 - lam_init, None, ALU.mult, None, [tPar], [tPar])
        for g in range(4):
            tr(psb_bf(7)[:, g * 128:(g + 1) * 128], Ws[:, g, :], [tWs], [tPS[7]])
        cp("dve", WsT[:].rearrange("p a b -> p (a b)"), psb_bf(7)[:, 0:512], [tPS[7]], [tWs])
        P.add("pool", lambda e: e.memset(VX[:, 0:8, :], 1.0), [], [tKVQ])
        P.add("pool", lambda e: e.memset(VX[:, 8:16, :], 1.0), [], [tKVQ])

        def xT_for_tile(tt, slot, bank):
            cp("act", XB[:], X[:, tt, :], [tX[tt]], [tXB])
            for kc in range(8):
                tr(psb_bf(bank)[:, kc * 128:(kc + 1) * 128], XB[:, kc * 128:(kc + 1) * 128], [tXB], [tPS[bank]])
            cp("dve", xTt[slot][:].rearrange("p a b -> p (a b)"), psb_bf(bank)[:, 0:1024], [tPS[bank]], [tS[slot]["xT"]])

        def inproj(slot, g, bank):
            for kc in range(8):
                mm(psb(bank), xTt[slot][:, kc, :], WIN[:, kc, g * 512:(g + 1) * 512], kc == 0, kc == 7,
                   [tS[slot]["xT"], tWIN[g]], [tPS[bank]])

        def rms_rope_B(H, nh, so, slot, tt, g_b, out_ap_fn):
            t = tS[slot]
            n = nh * 64
            bank_t = H["t"]
            Hs = H["ap"]
            smq = sm[:, so:so + nh]
            act(TC[slot][:, 0:n], Hs, AF.Square, [bank_t], [t["TC"]])
            P.add("dve", lambda e: e.tensor_reduce(out=smq, in_=TC[slot][:, 0:n].rearrange("p (h d) -> p h d", h=nh), axis=AX.X, op=ALU.add),
                  [t["TC"]], [t["sm"]])
            rstd_chain(smq, smq, nh, 1.0 / 64, t["sm"])
            tt_("dve", TD[slot][:, 0:n].rearrange("p (h d) -> p h d", h=nh), Hs.rearrange("p (h d) -> p h d", h=nh),
                g_b[:].unsqueeze(1).broadcast_to([128, nh, 64]), ALU.mult, [bank_t, tPar], [t["TD"]])
            cBt = cosB[:, tt, :].unsqueeze(1).broadcast_to([128, nh, 64])
            tt_("pool", TE[slot][:, 0:n].rearrange("p (h d) -> p h d", h=nh), TD[slot][:, 0:n].rearrange("p (h d) -> p h d", h=nh), cBt, ALU.mult,
                [t["TD"], tTab], [t["TE"]])
            sBv = sinB[:, tt, :].rearrange("p (r h d) -> p r h d", r=2, h=2)
            for hf in range(2):
                o_ = TC[slot][:, 0:n].rearrange("p (a r h d) -> p a r h d", a=nh, r=2, h=2)[:, :, :, hf, :]
                i_ = TD[slot][:, 0:n].rearrange("p (a r h d) -> p a r h d", a=nh, r=2, h=2)[:, :, :, 1 - hf, :]
                s_ = sBv[:, :, hf, :].unsqueeze(1).broadcast_to([128, nh, 2, 16])
                tt_("dve", o_, i_, s_, ALU.mult, [t["TD"], tTab], [t["TC"]])
            tt_("pool", TE[slot][:, 0:n], TE[slot][:, 0:n], TC[slot][:, 0:n], ALU.add, [t["TE"], t["TC"]], [t["TE"]])
            out_ap_fn(TE[slot][:, 0:n], smq)

        def banks(tt):
            return (0, 1) if tt % 2 == 0 else (2, 3)

        def F0(tt):
            cv_step()
            slot = tt % 2
            b0, b1 = banks(tt)
            xT_for_tile(tt, slot, 4 if slot == 0 else 7)
            inproj(slot, 0, b0)
            inproj(slot, 1, b1)

        def M0(tt):
            slot = tt % 2
            t = tS[slot]
            b0, b1 = banks(tt)
            H0 = psb(b0)
            H0v = H0.rearrange("p (v h d) -> p v h d", v=16, h=2)
            cA = cosA[:, tt, :].unsqueeze(1).broadcast_to([128, 16, 32])
            sA0 = sinA[:, tt, 0:16].unsqueeze(1).broadcast_to([128, 16, 16])
            sA1 = sinA[:, tt, 16:32].unsqueeze(1).broadcast_to([128, 16, 16])
            TAv = TA[slot][:].rearrange("p (v d) -> p v d", v=16)
            TBv = TB[slot][:].rearrange("p (v h d) -> p v h d", v=16, h=2)
            tt_("dve", TAv, H0.rearrange("p (v d) -> p v d", v=16), cA, ALU.mult, [tPS[b0], tTab], [t["TA"]])
            tt_("dve", TBv[:, :, 0, :], H0v[:, :, 1, :], sA0, ALU.mult, [tPS[b0], tTab], [t["TB"]])
            tt_("dve", TBv[:, :, 1, :], H0v[:, :, 0, :], sA1, ALU.mult, [tPS[b0], tTab], [t["TB"]])
            tt_("pool", OBA[slot][:], TA[slot][:], TB[slot][:], ALU.add, [t["TA"], t["TB"]], [t["OBA"]])

            def outq(src, smq, slot=slot, t=t):
                o_ = OBB[slot][:, 0:512].rearrange("p (g j d) -> p j g d", g=4, j=2)
                tt_("pool", o_, src.rearrange("p (j g d) -> p j g d", j=2, g=4),
                    smq.rearrange("p (j g) -> p j g", j=2).unsqueeze(3).broadcast_to([128, 2, 4, 64]), ALU.mult,
                    [t["TE"], t["sm"]], [t["OBB"]])
            rms_rope_B(dict(ap=psb(b1), t=tPS[b1]), 8, 64 * slot, slot, tt, gq_b, outq)

        def E0(tt):
            slot = tt % 2
            t = tS[slot]
            tok = slice(tt * 128, (tt + 1) * 128)
            tb = 5 + slot
            for blk in range(4):
                tr(psb_bf(tb)[:, blk * 128:(blk + 1) * 128], OBA[slot][:, blk * 128:(blk + 1) * 128], [t["OBA"]], [tPS[tb]])
            for blk in range(4):
                tr(psb_bf(tb)[:, (4 + blk) * 128:(5 + blk) * 128], OBB[slot][:, blk * 128:(blk + 1) * 128], [t["OBB"]], [tPS[tb]])
            p5 = psb_bf(tb).rearrange("p (a b) -> p a b", a=8)
            cp("act", AqT[:, :, tok], p5[:, 0:2, :], [tPS[tb]], [tKVQ])
            cp("act", AkT[:, :, tok], p5[:, 2:4, :], [tPS[tb]], [tKVQ])
            cp("act", BqT[:, :, tok], p5[:, 4:8, :], [tPS[tb]], [tKVQ])

        for i in range(NT + 2):
            if i < NT:
                F0(i)
            if 0 <= i - 1 < NT:
                M0(i - 1)
            if 0 <= i - 2 < NT:
                E0(i - 2)

        load_win(l, 1)
        def F1(tt):
            cv_step()
            slot = tt % 2
            b0, b1 = banks(tt)
            xT_for_tile(tt, slot, 4 + slot)
            inproj(slot, 0, b0)
            inproj(slot, 1, b1)

        def M1(tt):
            slot = tt % 2
            t = tS[slot]
            b0, b1 = banks(tt)
            H2 = psb(b0)
            H3 = psb(b1)

            def outk(src, smq, slot=slot, t=t):
                tt_("pool", OBB[slot][:, 0:128].rearrange("p (h d) -> p h d", h=2), src.rearrange("p (h d) -> p h d", h=2),
                    smq.unsqueeze(2).broadcast_to([128, 2, 64]), ALU.mult, [t["TE"], t["sm"]], [t["OBB"]])
            rms_rope_B(dict(ap=H2[:, 0:128], t=tPS[b0]), 2, 64 * slot + 8, slot, tt, gk_b, outk)
            Hav = H2[:, 128:384].rearrange("p (a q d) -> p a q d", a=2, q=2)
            VXa = VX[:, tt, 0:512].rearrange("p (a c) -> p a c", a=2)
            cp("dve", VXa[:, :, 0:64], Hav[:, :, 0, :], [tPS[b0]], [tKVQ])
            cp("dve", VXa[:, :, 192:256], Hav[:, :, 1, :], [tPS[b0]], [tKVQ])
            VXb = VX[:, tt, 512:896].rearrange("p (j c) -> p j c", j=2)
            cp("dve", VXb[:, :, 64:128], H2[:, 384:512].rearrange("p (j d) -> p j d", j=2), [tPS[b0]], [tKVQ])
            UV = TA[slot]
            so = 64 * slot
            act(UV[:], H3, AF.Gelu_apprx_tanh, [tPS[b1]], [t["TA"]])
            P.add("dve", lambda e, so=so, UV=UV: e.bn_stats(out=sm[:, so + 16:so + 22], in_=UV[:, 256:512]), [t["TA"]], [t["sm"]])
            P.add("dve", lambda e, so=so: e.bn_aggr(out=sm[:, so + 22:so + 24], in_=sm[:, so + 16:so + 22].rearrange("p (a b) -> p a b", a=1)),
                  [t["sm"]], [t["sm"]])
            rstd_chain(sm[:, so + 24:so + 25], sm[:, so + 23:so + 24], 1, 1.0, t["sm"])
            ts_("dve", TB[slot][:, 0:256], UV[:, 256:512], sm[:, so + 22:so + 23], sm[:, so + 24:so + 25], ALU.subtract, ALU.mult,
                [t["TA"], t["sm"]], [t["TB"]])
            tt_("pool", TB[slot][:, 0:256], TB[slot][:, 0:256], cg_b[:], ALU.mult, [t["TB"], tPar], [t["TB"]])
            tt_("pool", VLN[slot][:], TB[slot][:, 0:256], cb_b[:], ALU.add, [t["TB"], tPar], [t["VLN"]])

        def E1(tt):
            slot = tt % 2
            t = tS[slot]
            tok = slice(tt * 128, (tt + 1) * 128)
            UV = TA[slot]
            tr(psb_bf(6)[:, 0:128], OBB[slot][:, 0:128], [t["OBB"]], [tPS[6]])
            for g in range(4):
                mm(psb(7)[:, g * 64:(g + 1) * 64], WsT[:, g, :], VLN[slot][:, g * 64:(g + 1) * 64], True, True, [tWs, t["VLN"]], [tPS[7]])
            for g in range(4):
                stt_("dve", CO[slot][:, g * 64:(g + 1) * 64], psb(7)[:, g * 64:(g + 1) * 64], bsT[:, g:g + 1], UV[:, g * 64:(g + 1) * 64],
                     ALU.add, ALU.mult, [tPS[7], tPar, t["TA"]], [t["CO"]])
            for blk in range(2):
                tr(psb_bf(6)[:, (1 + blk) * 128:(2 + blk) * 128], CO[slot][:, blk * 128:(blk + 1) * 128], [t["CO"]], [tPS[6]])
            p6 = psb_bf(6).rearrange("p (a b) -> p a b", a=8)
            cp("act", BkT[:, tok], p6[:, 0, :], [tPS[6]], [tKVQ])
            cp("act", mTC[:, :, tok], p6[:, 1:3, :], [tPS[6]], [tKVQ])

        for i in range(NT + 2):
            if i < NT:
                F1(i)
            if 0 <= i - 1 < NT:
                M1(i - 1)
            if 0 <= i - 2 < NT:
                E1(i - 2)

    def phaseB(l):
        tWO = T("wout")
        dma("pool", WOUT[:], w_out_d[l].rearrange("(ec p) d -> p ec d", p=128), "wout", [], [tWO])
        tR, tTt, tOP, tSQ, tY, tsm = T("R"), T("Tt"), T("OP"), T("SQ"), T("Y"), T("smB")
        tPT = [T("pt0"), T("pt1"), tY]
        tmT = [T("mT%d" % c) for c in range(6)]
        tQM = [T("qm0"), T("qm1")]
        neglam = sm[:, 40:41]
        cnt = {"s": 0, "m": 0, "tick": 0}
        pending = []

        def defer(delay, fn):
            pending.append([cnt["tick"] + delay, fn])

        def run_due(force=False):
            progressed = True
            while progressed:
                progressed = False
                for item in list(pending):
                    if force or item[0] <= cnt["tick"]:
                        pending.remove(item)
                        item[1]()
                        progressed = True

        def prep_qm(d, qs):
            rows = d["rows"]
            P.add("pool", lambda e: e.memset(QM[qs][:], 0.0), [], [tQM[qs]])
            if rows.start == 96:
                P.add("dve", lambda e: e.tensor_copy(out=QM[qs][64:128, :], in_=d["q_src64"]), [], [tQM[qs]])
                P.add("dve", lambda e: e.memset(QM[qs][64:96, :], 0.0), [], [tQM[qs]])
            else:
                P.add("dve", lambda e: e.tensor_copy(out=QM[qs][rows, :], in_=d["q_src"]), [], [tQM[qs]])

        def run_map(d, qs, acc):
            accb = (2 * acc, 2 * acc + 1)
            kT_fn, v_fn, scale = d["kT_fn"], d["v_fn"], d["scale"]
            pend = []
            for kb in range(NT):
                sp_ = cnt["s"] % 2
                pt_ = cnt["s"] % 3
                cnt["s"] += 1
                sb_ = (2 * sp_, 2 * sp_ + 1)
                for j in range(2):
                    mm(psb(sb_[j]), kT_fn(kb), QM[qs][:, j * 512:(j + 1) * 512], True, True, [tQM[qs]], [tPS[sb_[j]]])
                act(PT[pt_][:], PS[sp_][:], AF.Exp, [tPS[sb_[0]], tPS[sb_[1]]], [tPT[pt_]], scale=scale)
                pend.append((kb, pt_))
                if len(pend) > 2:
                    pk, ps_ = pend.pop(0)
                    for j in range(2):
                        mm(psb(accb[j]), v_fn(pk), PT[ps_][:, j * 512:(j + 1) * 512], pk == 0, pk == NT - 1, [tPT[ps_]], [tPS[accb[j]]])
                cnt["tick"] += 1
                run_due()
            for pk, ps_ in pend:
                for j in range(2):
                    mm(psb(accb[j]), v_fn(pk), PT[ps_][:, j * 512:(j + 1) * 512], pk == 0, pk == NT - 1, [tPT[ps_]], [tPS[accb[j]]])

        def tail_A(acc, hh, c, pp):
            dr = slice(64 * hh, 64 * hh + 64)
            nr = slice(64 * (1 - hh), 64 * (1 - hh) + 64)
            ta = [tPS[2 * acc], tPS[2 * acc + 1]]
            P.add("dve", lambda e: e.reciprocal(out=Rt[dr, :], in_=PS[acc][nr, :]), ta, [tR])
            if c == 0:
                tt_("dve", OP[dr, :], PS[acc][dr, :], Rt[dr, :], ALU.mult, ta + [tR], [tOP])
                return
            tt_("dve", Tt[dr, :], PS[acc][dr, :], Rt[dr, :], ALU.mult, ta + [tR], [tTt])
            stt_("dve", OP[dr, :], Tt[dr, :], neglam[dr, :], OP[dr, :], ALU.mult, ALU.add, [tTt, tOP], [tOP])
            if hh == 0:
                return
            tt_("pool", SQ[:], OP[:], OP[:], ALU.mult, [tOP], [tSQ])

            def st2():
                for j in range(2):
                    mm(psb(2 * acc + j), bdm[:], SQ[:, j * 512:(j + 1) * 512], True, True, [tSQ, tConst], [tPS[2 * acc + j]])
                defer(3, st3)

            def st3():
                act(Rt[:], PS[acc][:], AF.Sqrt, ta, [tR], bias=epsT[:, 0:1])
                defer(2, st4)

            def st4():
                P.add("dve", lambda e: e.reciprocal(out=Rt[:], in_=Rt[:]), [tR], [tR])
                stt_("dve", mTAB[:, pp, :], OP[:], gA2[:, 0:1], Rt[:], ALU.mult, ALU.mult, [tOP, tR, tPar], [tmT[pp]])
            defer(4, st2)

        def tail_B(acc, hh, cB):
            dr = slice(64 * hh, 64 * hh + 64)
            nr = slice(64 * (1 - hh), 64 * (1 - hh) + 64)
            ta = [tPS[2 * acc], tPS[2 * acc + 1]]
            P.add("dve", lambda e: e.reciprocal(out=Rt[dr, :], in_=PS[acc][nr, :]), ta, [tR])
            tt_("dve", mTAB[dr, 2 + cB, :], PS[acc][dr, :], Rt[dr, :], ALU.mult, ta + [tR], [tmT[2 + cB]])

        for qh in range(2):
            q0 = qh * 1024
            maps = []
            for pp in range(2):
                for hh in range(2):
                    h = 2 * pp + hh
                    for c in range(2):
                        r0 = (hh * 2 + c) * 32
                        maps.append(dict(
                            kT_fn=lambda kb, pp=pp: AkT[:, pp, kb * 128:(kb + 1) * 128],
                            q_src=AqT[r0:r0 + 32, pp, q0:q0 + 1024], rows=slice(r0, r0 + 32),
                            q_src64=AqT[64:128, pp, q0:q0 + 1024],
                            v_fn=lambda kb, h=h: VX[:, kb, h * 128:(h + 1) * 128],
                            scale=32 ** -0.5,
                            tail=lambda acc, hh=hh, c=c, pp=pp: tail_A(acc, hh, c, pp)))
            for cB in range(4):
                for hh in range(2):
                    hB = 2 * cB + hh
                    j_kv, g = hB // 4, hB % 4
                    rows = slice(64 * j_kv, 64 * j_kv + 64)
                    voff = 512 + j_kv * 192 + (64 if hh == 0 else 0)
                    maps.append(dict(
                        kT_fn=lambda kb: BkT[:, kb * 128:(kb + 1) * 128],
                        q_src=BqT[rows, g, q0:q0 + 1024], rows=rows, q_src64=None,
                        v_fn=lambda kb, voff=voff: VX[:, kb, voff:voff + 128],
                        scale=64 ** -0.5,
                        tail=lambda acc, hh=hh, cB=cB: tail_B(acc, hh, cB)))
            prep_qm(maps[0], cnt["m"] % 2)
            for i, d in enumerate(maps):
                qs = cnt["m"] % 2
                acc = 2 + (cnt["m"] % 2)
                cnt["m"] += 1
                if i + 1 < len(maps):
                    prep_qm(maps[i + 1], cnt["m"] % 2)
                cv_step()
                run_map(d, qs, acc)
                defer(3, lambda d=d, acc=acc: d["tail"](acc))
            run_due(force=True)
            for tl in range(8):
                tt = qh * 8 + tl
                acc = 2 + (tl % 2)
                for dg in range(2):
                    for ec in range(8):
                        if ec < 6:
                            lhs = mTAB[:, ec, tl * 128:(tl + 1) * 128]
                            rd = [tmT[ec], tWO]
                        else:
                            lhs = mTC[:, ec - 6, tt * 128:(tt + 1) * 128]
                            rd = [tWO]
                        mm(psb(2 * acc + dg), lhs, WOUT[:, ec, dg * 512:(dg + 1) * 512], ec == 0, ec == 7, rd, [tPS[2 * acc + dg]])
                ta = [tPS[2 * acc], tPS[2 * acc + 1]]
                stt_("dve", Yb[:], X[:, tt, :], float(ALPHA), PS[acc][:], ALU.mult, ALU.add, ta + [tX[tt]], [tY])
                layernorm_tail(Yb, tY, tt, tsm)

    def layernorm_tail(Y, tY, tt, tsm):
        for j in range(2):
            P.add("dve", lambda e, j=j: e.bn_stats(out=sm[:, 44 + 6 * j:50 + 6 * j], in_=Y[:, j * 512:(j + 1) * 512]), [tY], [tsm])
        P.add("dve", lambda e: e.bn_aggr(out=sm[:, 56:58], in_=sm[:, 44:56].rearrange("p (a b) -> p a b", a=2)), [tsm], [tsm])
        rstd_chain(sm[:, 58:59], sm[:, 57:58], 1, 1.0, tsm)
        stt_("dve", sm[:, 59:60], sm[:, 56:57], -1.0, sm[:, 58:59], ALU.mult, ALU.mult, [tsm], [tsm])
        act(X[:, tt, :], Y[:], AF.Identity, [tY, tsm], [tX[tt]], scale=sm[:, 58:59], bias=sm[:, 59:60])

    tG1T = T("g1T")

    def load_g1T(l):
        dma("sp", g1T[:], l1g_d[l].rearrange("(kc p) -> p kc", p=128), "g1t", [], [tG1T], slow=True)
        dma("sp", b1T[:], l1b_d[l].rearrange("(kc p) -> p kc", p=128), "g1t", [], [tG1T], slow=True)

    def phaseC(l, s, last, prefetch):
        cv_step(1000)
        tG1, tG2 = T("g1"), T("g2")
        dma("sp", g1_b[:], l1g_d[l:l + 1, :].broadcast_to([128, D]), "lng", [], [tG1])
        dma("sp", b1_b[:], l1b_d[l:l + 1, :].broadcast_to([128, D]), "lng", [], [tG1])
        dma("sp", g2_b[:], l2g_d[l:l + 1, :].broadcast_to([128, D]), "lng2", [], [tG2])
        dma("sp", b2_b[:], l2b_d[l:l + 1, :].broadcast_to([128, D]), "lng2", [], [tG2])
        prefetch()
        tWG = [T("wg0"), T("wg1")]
        tWU = [T("wu0"), T("wu1")]
        tWD = [T("wd0"), T("wd1")]
        thid = [T("hid%d" % f) for f in range(NFC)]
        txTf = [T("xTf0"), T("xTf1")]
        tXB, tSG, tsm = T("XBc"), [T("sg0"), T("sg1")], T("smC")
        tYs = [T("Y0"), T("Y1"), T("Y2")]
        wgs = wg_d[l].rearrange("(kc p) f -> p kc f", p=128)
        wus = wu_d[l].rearrange("(kc p) f -> p kc f", p=128)
        wds = wd_d[l].rearrange("(fc p) d -> p fc d", p=128)
        seq = {"gu": 0, "d": 0}

        def load_gu(fb):
            sl = seq["gu"] % 2
            seq["gu"] += 1
            if USE_CV:
                dma("sp", WG[sl][:].rearrange("p a b -> p (a b)"), wg_bf[l, fb], "wg%d" % sl, [tCV[l]], [tWG[sl]])
                dma("sp", WU[sl][:].rearrange("p a b -> p (a b)"), wu_bf[l, fb], "wu%d" % sl, [tCV[l]], [tWU[sl]])
            else:
                dma("pool", WG[sl][:], wgs[:, :, fb * 256:(fb + 1) * 256], "wg%d" % sl, [], [tWG[sl]])
                dma("pool", WU[sl][:], wus[:, :, fb * 256:(fb + 1) * 256], "wu%d" % sl, [], [tWU[sl]])
            return sl

        def load_d(db):
            sl = seq["d"] % 2
            seq["d"] += 1
            n = 4 if db < 5 else 2
            if USE_CV:
                dma("sp", WD[sl][:, 0:n, :].rearrange("p a b -> p (a b)"), wd_bf[l, db, :, 0:n * 1024], "wd%d" % sl, [tCV[l]], [tWD[sl]])
            else:
                dma("pool", WD[sl][:, 0:n, :], wds[:, db * 4:db * 4 + n, :], "wd%d" % sl, [], [tWD[sl]])
            return sl

        def front(tg):
            xs = tg % 2
            for tl in range(4):
                tt = tg * 4 + tl
                cp("act", XBc[:], X[:, tt, :], [tX[tt]], [tXB])
                bank = 4 + tl
                for kc in range(8):
                    tr(psb_bf(bank)[:, kc * 128:(kc + 1) * 128], XBc[:, kc * 128:(kc + 1) * 128], [tXB], [tPS[bank]])
                for kc in range(8):
                    ts_("dve", xTf2[xs][:, kc, tl * 128:(tl + 1) * 128], psb_bf(bank)[:, kc * 128:(kc + 1) * 128],
                        g1T[:, kc:kc + 1], b1T[:, kc:kc + 1], ALU.mult, ALU.add, [tPS[bank], tG1T], [txTf[xs]])
                tt_("pool", X[:, tt, :], X[:, tt, :], g1_b[:], ALU.mult, [tX[tt], tG1], [tX[tt]])
                tt_("pool", X[:, tt, :], X[:, tt, :], b1_b[:], ALU.add, [tX[tt], tG1], [tX[tt]])

        def C1(tg, nxt):
            xs = tg % 2
            for fb in range(11):
                sl = nxt
                if fb + 1 < 11:
                    nxt = load_gu(fb + 1)
                for fci in range(2):
                    fc = 2 * fb + fci
                    gb = 0 + (fc % 2)
                    ub = 2 + (fc % 2)
                    for kc in range(8):
                        mm(psb(gb), WG[sl][:, kc, fci * 128:(fci + 1) * 128], xTf2[xs][:, kc, :], kc == 0, kc == 7, [tWG[sl], txTf[xs]], [tPS[gb]])
                    for kc in range(8):
                        mm(psb(ub), WU[sl][:, kc, fci * 128:(fci + 1) * 128], xTf2[xs][:, kc, :], kc == 0, kc == 7, [tWU[sl], txTf[xs]], [tPS[ub]])
                    act(SG[fc % 2][:], psb(gb), AF.Silu, [tPS[gb]], [tSG[fc % 2]])
                    tt_("dve", hidT[:, fc, :], psb(ub), SG[fc % 2][:], ALU.mult, [tPS[ub], tSG[fc % 2]], [thid[fc]])

        def C2(tg):
            nd = load_d(0)
            for db in range(6):
                sl = nd
                if db + 1 < 6:
                    nd = load_d(db + 1)
                n = 4 if db < 5 else 2
                for fci in range(n):
                    fc = db * 4 + fci
                    for tl in range(4):
                        for dg in range(2):
                            mm(psb(2 * tl + dg), hidT[:, fc, tl * 128:(tl + 1) * 128], WD[sl][:, fci, dg * 512:(dg + 1) * 512],
                               fc == 0, fc == NFC - 1, [thid[fc], tWD[sl]], [tPS[2 * tl + dg]])

        def LN2(tg):
            def evac(tl):
                tt = tg * 4 + tl
                ta = [tPS[2 * tl], tPS[2 * tl + 1]]
                stt_("dve", Ys[tl % 3][:], X[:, tt, :], float(ALPHA), PS[tl][:], ALU.mult, ALU.add, ta + [tX[tt]], [tYs[tl % 3]])

            def tail(tl):
                tt = tg * 4 + tl
                layernorm_tail(Ys[tl % 3], tYs[tl % 3], tt, tsm)
                tt_("pool", X[:, tt, :], X[:, tt, :], g2_b[:], ALU.mult, [tX[tt], tG2], [tX[tt]])
                tt_("pool", X[:, tt, :], X[:, tt, :], b2_b[:], ALU.add, [tX[tt], tG2], [tX[tt]])
                if last:
                    dma("sp", out_d[s, tt * 128:(tt + 1) * 128, :], X[:, tt, :], "out", [tX[tt]], [])
            evac(0)
            evac(1)
            evac(2)
            tail(0)
            evac(3)
            tail(1)
            tail(2)
            tail(3)

        nxt = load_gu(0)
        front(0)
        for tg in range(4):
            C1(tg, nxt)
            if tg + 1 < 4:
                nxt = load_gu(0)
                front(tg + 1)
            C2(tg)
            LN2(tg)

    load_win(0, 0)
    load_tables()
    load_params(0)
    if USE_CV:
        convert_layer(0)
    for s in range(nseq):
        if s > 0:
            P.barrier()
        for q in range(4):
            dma("sp", X[:, q * 4:(q + 1) * 4, :], x_d[s, q * 512:(q + 1) * 512, :].rearrange("(t p) d -> p t d", p=128), "x%d" % q,
                [], [tX[q * 4 + i] for i in range(4)])
        def dbg_store():
            P.barrier()
            for tt in range(NT):
                dma("sp", out_d[s, tt * 128:(tt + 1) * 128, :], X[:, tt, :], "out", [tX[tt]], [])
        for l in range(L):
            P.barrier()
            if stop == "load":
                dbg_store()
                break
            phaseA(l)
            P.barrier()
            if stop == "A":
                dbg_store()
                break
            if USE_CV and s == 0 and l + 1 < L:
                convert_layer(l + 1)
            load_g1T(l)
            phaseB(l)
            P.barrier()
            if stop == "B":
                dbg_store()
                break
            nl = l + 1 if l + 1 < L else (0 if s + 1 < nseq else None)

            def prefetch(nl=nl):
                if nl is not None:
                    load_win(nl, 0)
                    load_tables()
                    load_params(nl)
            phaseC(l, s, l == L - 1, prefetch)
    P.barrier()
    P.finalize_and_emit(st)
    st.close()
    return nc, P


_CACHE = {}


def kernel(**inputs):
    n_cores = 8
    nseq = 32 // n_cores
    depth = 4
    if "nc" not in _CACHE:
        _CACHE["nc"] = build(nseq, depth)[0]
    nc = _CACHE["nc"]
    consts = host_consts()
    maps = []
    for c in range(n_cores):
        m = {k: np.ascontiguousarray(np.asarray(v, dtype=np.float32)) for k, v in inputs.items() if k != "x"}
        m["x"] = np.ascontiguousarray(np.asarray(inputs["x"], dtype=np.float32)[c * nseq:(c + 1) * nseq])
        m.update(consts)
        maps.append(m)
    res = run_bass_kernel_spmd(nc, maps, core_ids=list(range(n_cores)))
    return np.concatenate([np.asarray(r["out"]) for r in res.results], axis=0).astype(np.float32)
```

```python
import math, os
CUT = int(os.environ.get('KB_CUT', '99'))
USE_CV = True
import numpy as np
from contextlib import ExitStack
import concourse.bass as bass
import concourse.mybir as mybir

from concourse.bass_utils import run_bass_kernel_spmd


ENGS = ("pe", "act", "dve", "pool", "sp")


class T:
    __slots__ = ("name", "w", "r")

    def __init__(self, name=""):
        self.name = name
        self.w = None
        self.r = []


class Op:
    __slots__ = ("fn", "deps", "signal", "dma_sem", "dma_val")

    def __init__(self, fn, deps, dma_sem=None, dma_val=0):
        self.fn = fn
        self.deps = deps
        self.signal = False
        self.dma_sem = dma_sem
        self.dma_val = dma_val


class Prog:
    def __init__(self, nc):
        self.nc = nc
        self.ops = {e: [] for e in ENGS}
        self.dma_counts = {}
        self.n_wait = 0

    def _deps(self, eng, reads, writes):
        deps = set()
        for t in reads:
            if t.w is not None:
                deps.add(t.w)
        for t in writes:
            if t.w is not None:
                deps.add(t.w)
            for x in t.r:
                deps.add(x)
        best = {}
        for d in deps:
            if d[0] == "c" and d[1] == eng and eng == "pe":
                continue
            k = (d[0], d[1])
            if k not in best or best[k][2] < d[2]:
                best[k] = d
        return list(best.values())

    def add(self, eng, fn, reads=(), writes=()):
        deps = self._deps(eng, reads, writes)
        idx = len(self.ops[eng])
        self.ops[eng].append(Op(fn, deps))
        tok = ("c", eng, idx)
        for t in reads:
            t.r.append(tok)
        for t in writes:
            t.w = tok
            t.r = []
        return tok

    def dma(self, eng, fn, sem_key, reads=(), writes=()):
        deps = self._deps(eng, reads, writes)
        n = self.dma_counts.get(sem_key, 0) + 1
        self.dma_counts[sem_key] = n
        self.ops[eng].append(Op(fn, deps, dma_sem=sem_key, dma_val=16 * n))
        tok = ("d", sem_key, 16 * n)
        for t in reads:
            t.r.append(tok)
        for t in writes:
            t.w = tok
            t.r = []
        return tok

    def barrier(self):
        toks = []
        for e in ENGS:
            for i in range(len(self.ops[e]) - 1, -1, -1):
                op = self.ops[e][i]
                if op.fn is not None and op.dma_sem is None:
                    toks.append(("c", e, i))
                    break
        for k, n in self.dma_counts.items():
            if not str(k).startswith("cv"):
                toks.append(("d", k, 16 * n))
        for e in ENGS:
            self.wait_tokens(e, [t for t in toks if not (t[0] == "c" and t[1] == e and e == "pe")])

    def wait_tokens(self, eng, toks):
        self.ops[eng].append(Op(None, list(toks)))

    def finalize_and_emit(self, stack):
        nc = self.nc
        for e in ENGS:
            for op in self.ops[e]:
                for d in op.deps:
                    if d[0] == "c":
                        self.ops[d[1]][d[2]].signal = True
        val = {}
        for e in ENGS:
            c = 0
            for i, op in enumerate(self.ops[e]):
                if op.signal:
                    c += 1
                    val[(e, i)] = c
            assert c < 60000, (e, c)
        csem = {e: stack.enter_context(nc.semaphore("cs_" + e)) for e in ENGS}
        dsem = {k: stack.enter_context(nc.semaphore("ds_" + str(k))) for k in self.dma_counts}
        handles = {"pe": "tensor", "act": "scalar", "dve": "vector", "pool": "gpsimd", "sp": "sync"}
        block = stack.enter_context(nc.Block())
        prog = self

        def make(e):
            def body(eng):
                waited = {}
                for i, op in enumerate(prog.ops[e]):
                    need = {}
                    for d in op.deps:
                        if d[0] == "c":
                            key = ("c", d[1])
                            v = val[(d[1], d[2])]
                        else:
                            key = ("d", d[1])
                            v = d[2]
                        if waited.get(key, 0) >= v:
                            continue
                        if need.get(key, 0) < v:
                            need[key] = v
                    for key, v in need.items():
                        sem = csem[key[1]] if key[0] == "c" else dsem[key[1]]
                        eng.wait_ge(sem, v)
                        waited[key] = v
                        prog.n_wait += 1
                    if op.fn is None:
                        continue
                    ins = op.fn(eng)
                    if op.dma_sem is not None:
                        ins.then_inc(dsem[op.dma_sem], 16)
                    elif op.signal:
                        ins.then_inc(csem[e], 1)
            return body

        for e in ENGS:
            if not self.ops[e]:
                continue
            getattr(block, handles[e])(make(e))


F32 = mybir.dt.float32
BF16 = mybir.dt.bfloat16
AF = mybir.ActivationFunctionType
ALU = mybir.AluOpType
AX = mybir.AxisListType

D = 1024
S = 2048
NT = 16
HID = 2816
NFC = 22
EPS = 1e-5
ALPHA = (2 * 4) ** 0.25
KB = 1024


def host_consts():
    inv = 1.0 / (10000.0 ** (np.arange(0, 32, 2, dtype=np.float32) / 32.0))
    t = np.arange(S, dtype=np.float32)

    def cs(pos):
        ang = pos[:, None] * inv[None, :]
        ang = np.concatenate([ang, ang], -1)
        c = np.cos(ang).astype(np.float32)
        s = np.sin(ang).astype(np.float32)
        s[:, :16] *= -1.0
        return c, s
    cA, sA = cs(t)
    row = np.floor(t / 64.0).astype(np.float32)
    col = (t - 64.0 * row).astype(np.float32)
    cR, sR = cs(row)
    cC, sC = cs(col)
    cB = np.concatenate([cR, cC], -1)
    sB = np.concatenate([sR, sC], -1)

    def lay(a):
        n = a.shape[1]
        return np.ascontiguousarray(a.reshape(NT, 128, n).transpose(1, 0, 2).reshape(128, NT * n))
    tabs = np.concatenate([lay(cA), lay(sA), lay(cB), lay(sB)], 1).astype(np.float32)
    ident = np.eye(128, dtype=np.float32)
    bd = np.zeros((128, 128), np.float32)
    bd[:64, :64] = 1.0 / 64
    bd[64:, 64:] = 1.0 / 64
    return dict(tabs=tabs, ident=ident, bd=bd)


def build(nseq, depth, stop=None):
    nc = bass.Bass("TRN2", target_bir_lowering=False)
    L = depth

    def din(name, shape):
        return nc.dram_tensor(name, shape, F32, kind="ExternalInput").ap()
    x_d = din("x", [nseq, S, D])
    w_in_d = din("w_in", [L, D, 2048])
    w_out_d = din("w_out", [L, D, D])
    lam_d = din("lam_qk", [L, 4, 32])
    asg_d = din("a_subln_g", [L, 64])
    bqg_d = din("b_q_norm_g", [L, 64])
    bkg_d = din("b_k_norm_g", [L, 64])
    clg_d = din("c_ln_g", [L, 256])
    clb_d = din("c_ln_b", [L, 256])
    cws_d = din("c_w_s", [L, 4, 128, 128])
    cbs_d = din("c_b_s", [L, 4, 128])
    l1g_d = din("ln1_g", [L, D])
    l1b_d = din("ln1_b", [L, D])
    wg_d = din("w_gate", [L, D, HID])
    wu_d = din("w_up", [L, D, HID])
    wd_d = din("w_down", [L, HID, D])
    l2g_d = din("ln2_g", [L, D])
    l2b_d = din("ln2_b", [L, D])
    tabs_d = din("tabs", [128, 3072])
    ident_d = din("ident", [128, 128])
    bd_d = din("bd", [128, 128])
    out_d = nc.dram_tensor("out", [nseq, S, D], F32, kind="ExternalOutput").ap()
    if USE_CV:
        wg_bf = nc.dram_tensor("wg_bf", [L, 11, 128, 2048], BF16, kind="Internal").ap()
        wu_bf = nc.dram_tensor("wu_bf", [L, 11, 128, 2048], BF16, kind="Internal").ap()
        wd_bf = nc.dram_tensor("wd_bf", [L, 6, 128, 4096], BF16, kind="Internal").ap()

    st = ExitStack()

    def sb(name, shape, dt):
        return st.enter_context(nc.sbuf_tensor(name, shape, dt))
    X = sb("X", [128, NT, D], F32)
    REG = sb("REG", [128, 80 * KB // 4], F32)
    AR = sb("AR", [128, 57088 // 4], F32)
    ident = sb("identb", [128, 128], BF16)
    bdm = sb("bdm", [128, 128], BF16)
    Ws = sb("Ws", [128, 4, 128], BF16)
    WsT = sb("WsT", [128, 4, 128], BF16)
    gq_b = sb("gq_b", [128, 64], F32)
    gk_b = sb("gk_b", [128, 64], F32)
    cg_b = sb("cg_b", [128, 256], F32)
    cb_b = sb("cb_b", [128, 256], F32)
    bsT = sb("bsT", [128, 4], F32)
    lamq = sb("lamq", [128, 128], F32)
    gA2 = sb("gA2", [128, 1], F32)
    sm = sb("sm", [128, 128], F32)
    PR = sb("PR", [128, 64], F32)
    epsT = sb("epsT", [128, 1], F32)
    g1T = sb("g1T", [128, 8], F32)
    b1T = sb("b1T", [128, 8], F32)
    PS = [st.enter_context(nc.psum_tensor("PS%d" % i, [128, 1024], F32)) for i in range(4)]

    def view(base, off, shape, dt):
        esz = 2 if dt == BF16 else 4
        n = int(np.prod(shape[1:]))
        nb = n * esz
        assert off % 4 == 0 and nb % 4 == 0
        ap = base[:, off // 4:(off + nb) // 4]
        if dt == BF16:
            ap = ap.bitcast(BF16)
        if len(shape) == 3:
            ap = ap.rearrange("p (a b) -> p a b", a=shape[1])
        return ap

    AqT = view(REG, 0, [128, 2, S], BF16)
    AkT = view(REG, 8 * KB, [128, 2, S], BF16)
    BqT = view(REG, 16 * KB, [128, 4, S], BF16)
    BkT = view(REG, 32 * KB, [128, S], BF16)
    VX = view(REG, 36 * KB, [128, NT, 896], BF16)
    mTC = view(REG, 64 * KB, [128, 2, S], BF16)
    xTt = [view(REG, 75 * KB + i * 2 * KB, [128, 8, 128], BF16) for i in range(2)]
    hidT = view(REG, 0, [128, NFC, 512], BF16)
    xTf = view(REG, 22 * KB, [128, 8, 512], BF16)
    WG = [view(REG, 30 * KB + i * 4 * KB, [128, 8, 256], BF16) for i in range(2)]
    WU = [view(REG, 38 * KB + i * 4 * KB, [128, 8, 256], BF16) for i in range(2)]
    WD = [view(REG, 46 * KB + i * 8 * KB, [128, 4, 1024], BF16) for i in range(2)]
    XBc = view(REG, 62 * KB, [128, 1024], BF16)
    SG = [view(REG, 64 * KB + i * 2 * KB, [128, 512], F32) for i in range(2)]
    Yc = view(REG, 68 * KB, [128, 1024], F32)
    g2_b = view(REG, 72 * KB, [128, 1024], F32)
    b2_b = view(REG, 76 * KB, [128, 1024], F32)
    WIN = view(AR, 0, [128, 8, 1024], BF16)
    cosA = view(AR, 16 * KB, [128, NT, 32], F32)
    sinA = view(AR, 18 * KB, [128, NT, 32], F32)
    cosB = view(AR, 20 * KB, [128, NT, 64], F32)
    sinB = view(AR, 24 * KB, [128, NT, 64], F32)
    TA = [view(AR, 28 * KB + i * 2 * KB, [128, 512], F32) for i in range(2)]
    TB = [view(AR, 32 * KB + i * 2 * KB, [128, 512], F32) for i in range(2)]
    TC = [view(AR, 36 * KB + i * 2 * KB, [128, 512], F32) for i in range(2)]
    TD = [view(AR, 40 * KB + i * 2 * KB, [128, 512], F32) for i in range(2)]
    TE = [view(AR, 44 * KB + i * 2 * KB, [128, 512], F32) for i in range(2)]
    OBA = [view(AR, 48 * KB + i * KB, [128, 512], BF16) for i in range(2)]
    OBB = [view(AR, 50 * KB + i * 1280, [128, 640], BF16) for i in range(2)]
    XB = view(AR, 50 * KB + 2560, [128, 1024], BF16)
    VLN = [view(REG, 72 * KB + i * 512, [128, 256], BF16) for i in range(2)]
    CO = [view(REG, 73 * KB + i * 512, [128, 256], BF16) for i in range(2)]
    WOUT = view(AR, 0, [128, 8, 1024], BF16)
    mTAB = view(AR, 16 * KB, [128, 6, 1024], BF16)
    PT = [view(AR, 28 * KB + i * 2 * KB, [128, 1024], BF16) for i in range(2)] + [view(AR, 46 * KB, [128, 1024], BF16)]
    Rt = view(AR, 32 * KB, [128, 1024], F32)
    Tt = view(AR, 36 * KB, [128, 1024], F32)
    OP = view(AR, 40 * KB, [128, 1024], F32)
    SQ = view(AR, 44 * KB, [128, 1024], BF16)
    Yb = view(AR, 46 * KB, [128, 1024], F32)
    QM = [view(AR, 50 * KB + i * 2 * KB, [128, 1024], BF16) for i in range(2)]
    g1_b = view(AR, 44 * KB, [128, 1024], F32)
    b1_b = view(AR, 48 * KB, [128, 1024], F32)
    xTf2 = [xTf, view(AR, 28 * KB, [128, 8, 512], BF16)]
    Ys = [Yc, view(AR, 36 * KB, [128, 1024], F32), view(AR, 40 * KB, [128, 1024], F32)]

    P = Prog(nc)
    tX = [T("x%d" % i) for i in range(NT)]
    tPS = [T("ps%d" % i) for i in range(8)]

    def psb(i):
        return PS[i // 2][:, (i % 2) * 512:(i % 2) * 512 + 512]

    def psb_bf(i):
        return PS[i // 2][:, (i % 2) * 512:(i % 2) * 512 + 512].bitcast(BF16)
    tConst = T("const")
    tPar = T("par")
    tWs = T("Ws")
    tTab = T("tab")
    tWIN = [T("win%d" % g) for g in range(2)]

    def mm(out, lhsT, rhs, start, stop, reads, writes, tp=None):
        if tp is None:
            P.add("pe", lambda e: e.matmul(out, lhsT=lhsT, rhs=rhs, start=start, stop=stop), reads, writes)
        else:
            P.add("pe", lambda e: e.matmul(out, lhsT=lhsT, rhs=rhs, start=start, stop=stop, tile_position=tp), reads, writes)

    def tr(out, in_, reads, writes):
        P.add("pe", lambda e: e.transpose(out=out, in_=in_, identity=ident[:]), list(reads) + [tConst], writes)

    def act(out, in_, func, reads, writes, scale=None, bias=None):
        kw = {}
        if scale is not None:
            kw["scale"] = scale
        if bias is not None:
            kw["bias"] = bias
        P.add("act", lambda e: e.activation(out=out, in_=in_, func=func, **kw), reads, writes)

    def tt_(eng, out, in0, in1, op, reads, writes):
        P.add(eng, lambda e: e.tensor_tensor(out=out, in0=in0, in1=in1, op=op), reads, writes)

    def ts_(eng, out, in0, s1, s2, op0, op1, reads, writes):
        if op1 is None:
            P.add(eng, lambda e: e.tensor_scalar(out=out, in0=in0, scalar1=s1, scalar2=None, op0=op0), reads, writes)
        else:
            P.add(eng, lambda e: e.tensor_scalar(out=out, in0=in0, scalar1=s1, scalar2=s2, op0=op0, op1=op1), reads, writes)

    def stt_(eng, out, in0, scalar, in1, op0, op1, reads, writes):
        P.add(eng, lambda e: e.scalar_tensor_tensor(out=out, in0=in0, scalar=scalar, in1=in1, op0=op0, op1=op1), reads, writes)

    def cp(eng, out, in_, reads, writes):
        if eng == "act":
            P.add("act", lambda e: e.copy(out=out, in_=in_), reads, writes)
        else:
            P.add(eng, lambda e: e.tensor_copy(out=out, in_=in_), reads, writes)

    def dma(eng, out, in_, key, reads, writes, slow=False):
        if slow:
            P.dma(eng, lambda e: e.dma_start(out=out, in_=in_, allow_slow_non_contiguous=True), key, reads, writes)
        else:
            P.dma(eng, lambda e: e.dma_start(out=out, in_=in_), key, reads, writes)

    def rstd_chain(dst, src, n, scale, reads_t):
        ts_("dve", dst, src, scale, EPS, ALU.mult, ALU.add, [reads_t], [reads_t])
        act(dst, dst, AF.Sqrt, [reads_t], [reads_t])
        P.add("dve", lambda e: e.reciprocal(out=dst, in_=dst), [reads_t], [reads_t])

    dma("pool", ident[:], ident_d[:, :], "c0", [], [tConst])
    dma("pool", bdm[:], bd_d[:, :], "c1", [], [tConst])
    P.add("dve", lambda e: e.memset(epsT[:], EPS), [], [tConst])

    def load_tables():
        dma("sp", cosA.rearrange("p a b -> p (a b)"), tabs_d[:, 0:512], "tab", [], [tTab])
        dma("sp", sinA.rearrange("p a b -> p (a b)"), tabs_d[:, 512:1024], "tab", [], [tTab])
        dma("sp", cosB.rearrange("p a b -> p (a b)"), tabs_d[:, 1024:2048], "tab", [], [tTab])
        dma("sp", sinB.rearrange("p a b -> p (a b)"), tabs_d[:, 2048:3072], "tab", [], [tTab])

    def load_win(l, half):
        src = w_in_d[l].rearrange("(kc p) c -> p kc c", p=128)
        if half == 0:
            lst = ((0, 512, 0, 0), (768, 1280, 512, 1))
        else:
            lst = ((1280, 1408, 0, 0), (512, 768, 128, 0), (1408, 1536, 384, 0), (1536, 2048, 512, 1))
        for (s0, s1, d0, g) in lst:
            dma("pool", WIN[:, :, d0:d0 + (s1 - s0)], src[:, :, s0:s1], "win%d" % g, [], [tWIN[g]])

    def load_params(l):
        dma("sp", gq_b[:], bqg_d[l:l + 1, :].broadcast_to([128, 64]), "par", [], [tPar])
        dma("sp", gk_b[:], bkg_d[l:l + 1, :].broadcast_to([128, 64]), "par", [], [tPar])
        dma("sp", cg_b[:], clg_d[l:l + 1, :].broadcast_to([128, 256]), "par", [], [tPar])
        dma("sp", cb_b[:], clb_d[l:l + 1, :].broadcast_to([128, 256]), "par", [], [tPar])
        dma("sp", bsT[:], cbs_d[l].rearrange("g p -> p g"), "par", [], [tPar], slow=True)
        dma("sp", lamq[:], lam_d[l:l + 1].rearrange("o a b -> o (a b)").broadcast_to([128, 128]), "par", [], [tPar])
        dma("sp", gA2[0:64, :], asg_d[l].rearrange("(d o) -> d o", o=1), "par", [], [tPar])
        dma("sp", gA2[64:128, :], asg_d[l].rearrange("(d o) -> d o", o=1), "par", [], [tPar])
        dma("pool", Ws[:], cws_d[l].rearrange("g p q -> p g q"), "ws", [], [tWs])

    tCV = [T("cv%d" % l) for l in range(L)]

    cvq = []

    def convert_layer(l):
        wgs = wg_d[l].rearrange("(kc p) f -> p kc f", p=128)
        wus = wu_d[l].rearrange("(kc p) f -> p kc f", p=128)
        wds = wd_d[l].rearrange("(fc p) d -> p fc d", p=128)
        for fb in range(11):
            cvq.append(lambda l=l, fb=fb: dma("pool", wg_bf[l, fb].rearrange("p (a b) -> p a b", a=8), wgs[:, :, fb * 256:(fb + 1) * 256],
                                               "cv%d" % l, [], [tCV[l]]))
            cvq.append(lambda l=l, fb=fb: dma("pool", wu_bf[l, fb].rearrange("p (a b) -> p a b", a=8), wus[:, :, fb * 256:(fb + 1) * 256],
                                               "cv%d" % l, [], [tCV[l]]))
        for db in range(6):
            n = 4 if db < 5 else 2
            cvq.append(lambda l=l, db=db, n=n: dma("pool", wd_bf[l, db, :, 0:n * 1024].rearrange("p (a b) -> p a b", a=n),
                                                    wds[:, db * 4:db * 4 + n, :], "cv%d" % l, [], [tCV[l]]))

    def cv_step(n=1):
        for _ in range(n):
            if cvq:
                cvq.pop(0)()

    def phaseA(l):
        lam_init = 0.8 - 0.6 * math.exp(-0.3 * l)
        tS = [{k: T(k + str(i)) for k in ("TA", "TB", "TC", "TD", "TE", "OBA", "OBB", "VLN", "CO", "sm", "xT")} for i in range(2)]
        tXB, tsm0 = T("XB"), T("smA")
        tKVQ = T("kvq")
        lq = lamq[:].rearrange("p (a b c) -> p a b c", a=2, b=2)
        tt_("dve", PR[:].rearrange("p (a c) -> p a c", a=2), lq[:, :, 0, :], lq[:, :, 1, :], ALU.mult, [tPar], [tsm0])
        P.add("dve", lambda e: e.tensor_reduce(out=sm[:, 32:34], in_=PR[:].rearrange("p (a c) -> p a c", a=2), axis=AX.X, op=ALU.add),
              [tsm0], [tsm0])
        act(sm[:, 34:36], sm[:, 32:34], AF.Exp, [tsm0], [tsm0])
        tt_("dve", sm[:, 36:37], sm[:, 35:36], sm[:, 34:35], ALU.subtract, [tsm0], [tsm0])
        ts_("dve", sm[:, 40:41], sm[:, 36:37], -lam_init, None, ALU.add, None, [tsm0], [tsm0])
        ts_("dve", gA2[:], gA2[:], 1.0 - lam_init, None, ALU.mult, None, [tPar], [tPar])
        for g in range(4):
            tr(psb_bf(7)[:, g * 128:(g + 1) * 128], Ws[:, g, :], [tWs], [tPS[7]])
        cp("dve", WsT[:].rearrange("p a b -> p (a b)"), psb_bf(7)[:, 0:512], [tPS[7]], [tWs])
        P.add("pool", lambda e: e.memset(VX[:, 0:8, :], 1.0), [], [tKVQ])
        P.add("pool", lambda e: e.memset(VX[:, 8:16, :], 1.0), [], [tKVQ])

        def xT_for_tile(tt, slot, bank):
            cp("act", XB[:], X[:, tt, :], [tX[tt]], [tXB])
            for kc in range(8):
                tr(psb_bf(bank)[:, kc * 128:(kc + 1) * 128], XB[:, kc * 128:(kc + 1) * 128], [tXB], [tPS[bank]])
            cp("dve", xTt[slot][:].rearrange("p a b -> p (a b)"), psb_bf(bank)[:, 0:1024], [tPS[bank]], [tS[slot]["xT"]])

        def inproj(slot, g, bank):
            for kc in range(8):
                mm(psb(bank), xTt[slot][:, kc, :], WIN[:, kc, g * 512:(g + 1) * 512], kc == 0, kc == 7,
                   [tS[slot]["xT"], tWIN[g]], [tPS[bank]])

        def rms_rope_B(H, nh, so, slot, tt, g_b, out_ap_fn):
            t = tS[slot]
            n = nh * 64
            bank_t = H["t"]
            Hs = H["ap"]
            smq = sm[:, so:so + nh]
            act(TC[slot][:, 0:n], Hs, AF.Square, [bank_t], [t["TC"]])
            P.add("dve", lambda e: e.tensor_reduce(out=smq, in_=TC[slot][:, 0:n].rearrange("p (h d) -> p h d", h=nh), axis=AX.X, op=ALU.add),
                  [t["TC"]], [t["sm"]])
            rstd_chain(smq, smq, nh, 1.0 / 64, t["sm"])
            tt_("dve", TD[slot][:, 0:n].rearrange("p (h d) -> p h d", h=nh), Hs.rearrange("p (h d) -> p h d", h=nh),
                g_b[:].unsqueeze(1).broadcast_to([128, nh, 64]), ALU.mult, [bank_t, tPar], [t["TD"]])
            cBt = cosB[:, tt, :].unsqueeze(1).broadcast_to([128, nh, 64])
            tt_("pool", TE[slot][:, 0:n].rearrange("p (h d) -> p h d", h=nh), TD[slot][:, 0:n].rearrange("p (h d) -> p h d", h=nh), cBt, ALU.mult,
                [t["TD"], tTab], [t["TE"]])
            sBv = sinB[:, tt, :].rearrange("p (r h d) -> p r h d", r=2, h=2)
            for hf in range(2):
                o_ = TC[slot][:, 0:n].rearrange("p (a r h d) -> p a r h d", a=nh, r=2, h=2)[:, :, :, hf, :]
                i_ = TD[slot][:, 0:n].rearrange("p (a r h d) -> p a r h d", a=nh, r=2, h=2)[:, :, :, 1 - hf, :]
                s_ = sBv[:, :, hf, :].unsqueeze(1).broadcast_to([128, nh, 2, 16])
                tt_("dve", o_, i_, s_, ALU.mult, [t["TD"], tTab], [t["TC"]])
            tt_("pool", TE[slot][:, 0:n], TE[slot][:, 0:n], TC[slot][:, 0:n], ALU.add, [t["TE"], t["TC"]], [t["TE"]])
            out_ap_fn(TE[slot][:, 0:n], smq)

        def banks(tt):
            return (0, 1) if tt % 2 == 0 else (2, 3)

        def F0(tt):
            cv_step()
            slot = tt % 2
            b0, b1 = banks(tt)
            xT_for_tile(tt, slot, 4 if slot == 0 else 7)
            inproj(slot, 0, b0)
            inproj(slot, 1, b1)

        def M0(tt):
            slot = tt % 2
            t = tS[slot]
            b0, b1 = banks(tt)
            H0 = psb(b0)
            H0v = H0.rearrange("p (v h d) -> p v h d", v=16, h=2)
            cA = cosA[:, tt, :].unsqueeze(1).broadcast_to([128, 16, 32])
            sA0 = sinA[:, tt, 0:16].unsqueeze(1).broadcast_to([128, 16, 16])
            sA1 = sinA[:, tt, 16:32].unsqueeze(1).broadcast_to([128, 16, 16])
            TAv = TA[slot][:].rearrange("p (v d) -> p v d", v=16)
            TBv = TB[slot][:].rearrange("p (v h d) -> p v h d", v=16, h=2)
            tt_("dve", TAv, H0.rearrange("p (v d) -> p v d", v=16), cA, ALU.mult, [tPS[b0], tTab], [t["TA"]])
            tt_("dve", TBv[:, :, 0, :], H0v[:, :, 1, :], sA0, ALU.mult, [tPS[b0], tTab], [t["TB"]])
            tt_("dve", TBv[:, :, 1, :], H0v[:, :, 0, :], sA1, ALU.mult, [tPS[b0], tTab], [t["TB"]])
            tt_("pool", OBA[slot][:], TA[slot][:], TB[slot][:], ALU.add, [t["TA"], t["TB"]], [t["OBA"]])

            def outq(src, smq, slot=slot, t=t):
                o_ = OBB[slot][:, 0:512].rearrange("p (g j d) -> p j g d", g=4, j=2)
                tt_("pool", o_, src.rearrange("p (j g d) -> p j g d", j=2, g=4),
                    smq.rearrange("p (j g) -> p j g", j=2).unsqueeze(3).broadcast_to([128, 2, 4, 64]), ALU.mult,
                    [t["TE"], t["sm"]], [t["OBB"]])
            rms_rope_B(dict(ap=psb(b1), t=tPS[b1]), 8, 64 * slot, slot, tt, gq_b, outq)

        def E0(tt):
            slot = tt % 2
            t = tS[slot]
            tok = slice(tt * 128, (tt + 1) * 128)
            tb = 5 + slot
            for blk in range(4):
                tr(psb_bf(tb)[:, blk * 128:(blk + 1) * 128], OBA[slot][:, blk * 128:(blk + 1) * 128], [t["OBA"]], [tPS[tb]])
            for blk in range(4):
                tr(psb_bf(tb)[:, (4 + blk) * 128:(5 + blk) * 128], OBB[slot][:, blk * 128:(blk + 1) * 128], [t["OBB"]], [tPS[tb]])
            p5 = psb_bf(tb).rearrange("p (a b) -> p a b", a=8)
            cp("act", AqT[:, :, tok], p5[:, 0:2, :], [tPS[tb]], [tKVQ])
            cp("act", AkT[:, :, tok], p5[:, 2:4, :], [tPS[tb]], [tKVQ])
            cp("act", BqT[:, :, tok], p5[:, 4:8, :], [tPS[tb]], [tKVQ])

        for i in range(NT + 2):
            if i < NT:
                F0(i)
            if 0 <= i - 1 < NT:
                M0(i - 1)
            if 0 <= i - 2 < NT:
                E0(i - 2)

        load_win(l, 1)
        def F1(tt):
            cv_step()
            slot = tt % 2
            b0, b1 = banks(tt)
            xT_for_tile(tt, slot, 4 + slot)
            inproj(slot, 0, b0)
            inproj(slot, 1, b1)

        def M1(tt):
            slot = tt % 2
            t = tS[slot]
            b0, b1 = banks(tt)
            H2 = psb(b0)
            H3 = psb(b1)

            def outk(src, smq, slot=slot, t=t):
                tt_("pool", OBB[slot][:, 0:128].rearrange("p (h d) -> p h d", h=2), src.rearrange("p (h d) -> p h d", h=2),
                    smq.unsqueeze(2).broadcast_to([128, 2, 64]), ALU.mult, [t["TE"], t["sm"]], [t["OBB"]])
            rms_rope_B(dict(ap=H2[:, 0:128], t=tPS[b0]), 2, 64 * slot + 8, slot, tt, gk_b, outk)
            Hav = H2[:, 128:384].rearrange("p (a q d) -> p a q d", a=2, q=2)
            VXa = VX[:, tt, 0:512].rearrange("p (a c) -> p a c", a=2)
            cp("dve", VXa[:, :, 0:64], Hav[:, :, 0, :], [tPS[b0]], [tKVQ])
            cp("dve", VXa[:, :, 192:256], Hav[:, :, 1, :], [tPS[b0]], [tKVQ])
            VXb = VX[:, tt, 512:896].rearrange("p (j c) -> p j c", j=2)
            cp("dve", VXb[:, :, 64:128], H2[:, 384:512].rearrange("p (j d) -> p j d", j=2), [tPS[b0]], [tKVQ])
            UV = TA[slot]
            so = 64 * slot
            act(UV[:], H3, AF.Gelu_apprx_tanh, [tPS[b1]], [t["TA"]])
            P.add("dve", lambda e, so=so, UV=UV: e.bn_stats(out=sm[:, so + 16:so + 22], in_=UV[:, 256:512]), [t["TA"]], [t["sm"]])
            P.add("dve", lambda e, so=so: e.bn_aggr(out=sm[:, so + 22:so + 24], in_=sm[:, so + 16:so + 22].rearrange("p (a b) -> p a b", a=1)),
                  [t["sm"]], [t["sm"]])
            rstd_chain(sm[:, so + 24:so + 25], sm[:, so + 23:so + 24], 1, 1.0, t["sm"])
            ts_("dve", TB[slot][:, 0:256], UV[:, 256:512], sm[:, so + 22:so + 23], sm[:, so + 24:so + 25], ALU.subtract, ALU.mult,
                [t["TA"], t["sm"]], [t["TB"]])
            tt_("pool", TB[slot][:, 0:256], TB[slot][:, 0:256], cg_b[:], ALU.mult, [t["TB"], tPar], [t["TB"]])
            tt_("pool", VLN[slot][:], TB[slot][:, 0:256], cb_b[:], ALU.add, [t["TB"], tPar], [t["VLN"]])

        def E1(tt):
            slot = tt % 2
            t = tS[slot]
            tok = slice(tt * 128, (tt + 1) * 128)
            UV = TA[slot]
            tr(psb_bf(6)[:, 0:128], OBB[slot][:, 0:128], [t["OBB"]], [tPS[6]])
            for g in range(4):
                mm(psb(7)[:, g * 64:(g + 1) * 64], WsT[:, g, :], VLN[slot][:, g * 64:(g + 1) * 64], True, True, [tWs, t["VLN"]], [tPS[7]])
            for g in range(4):
                stt_("dve", CO[slot][:, g * 64:(g + 1) * 64], psb(7)[:, g * 64:(g + 1) * 64], bsT[:, g:g + 1], UV[:, g * 64:(g + 1) * 64],
                     ALU.add, ALU.mult, [tPS[7], tPar, t["TA"]], [t["CO"]])
            for blk in range(2):
                tr(psb_bf(6)[:, (1 + blk) * 128:(2 + blk) * 128], CO[slot][:, blk * 128:(blk + 1) * 128], [t["CO"]], [tPS[6]])
            p6 = psb_bf(6).rearrange("p (a b) -> p a b", a=8)
            cp("act", BkT[:, tok], p6[:, 0, :], [tPS[6]], [tKVQ])
            cp("act", mTC[:, :, tok], p6[:, 1:3, :], [tPS[6]], [tKVQ])

        for i in range(NT + 2):
            if i < NT:
                F1(i)
            if 0 <= i - 1 < NT:
                M1(i - 1)
            if 0 <= i - 2 < NT:
                E1(i - 2)

    def phaseB(l):
        tWO = T("wout")
        dma("pool", WOUT[:], w_out_d[l].rearrange("(ec p) d -> p ec d", p=128), "wout", [], [tWO])
        tR, tTt, tOP, tSQ, tY, tsm = T("R"), T("Tt"), T("OP"), T("SQ"), T("Y"), T("smB")
        tPT = [T("pt0"), T("pt1"), tY]
        tmT = [T("mT%d" % c) for c in range(6)]
        tQM = [T("qm0"), T("qm1")]
        neglam = sm[:, 40:41]
        cnt = {"s": 0, "m": 0, "tick": 0}
        pending = []

        def defer(delay, fn):
            pending.append([cnt["tick"] + delay, fn])

        def run_due(force=False):
            progressed = True
            while progressed:
                progressed = False
                for item in list(pending):
                    if force or item[0] <= cnt["tick"]:
                        pending.remove(item)
                        item[1]()
                        progressed = True

        def prep_qm(d, qs):
            rows = d["rows"]
            P.add("pool", lambda e: e.memset(QM[qs][:], 0.0), [], [tQM[qs]])
            if rows.start == 96:
                P.add("dve", lambda e: e.tensor_copy(out=QM[qs][64:128, :], in_=d["q_src64"]), [], [tQM[qs]])
                P.add("dve", lambda e: e.memset(QM[qs][64:96, :], 0.0), [], [tQM[qs]])
            else:
                P.add("dve", lambda e: e.tensor_copy(out=QM[qs][rows, :], in_=d["q_src"]), [], [tQM[qs]])

        def run_map(d, qs, acc):
            accb = (2 * acc, 2 * acc + 1)
            kT_fn, v_fn, scale = d["kT_fn"], d["v_fn"], d["scale"]
            pend = []
            for kb in range(NT):
                sp_ = cnt["s"] % 2
                pt_ = cnt["s"] % 3
                cnt["s"] += 1
                sb_ = (2 * sp_, 2 * sp_ + 1)
                for j in range(2):
                    mm(psb(sb_[j]), kT_fn(kb), QM[qs][:, j * 512:(j + 1) * 512], True, True, [tQM[qs]], [tPS[sb_[j]]])
                act(PT[pt_][:], PS[sp_][:], AF.Exp, [tPS[sb_[0]], tPS[sb_[1]]], [tPT[pt_]], scale=scale)
                pend.append((kb, pt_))
                if len(pend) > 2:
                    pk, ps_ = pend.pop(0)
                    for j in range(2):
                        mm(psb(accb[j]), v_fn(pk), PT[ps_][:, j * 512:(j + 1) * 512], pk == 0, pk == NT - 1, [tPT[ps_]], [tPS[accb[j]]])
                cnt["tick"] += 1
                run_due()
            for pk, ps_ in pend:
                for j in range(2):
                    mm(psb(accb[j]), v_fn(pk), PT[ps_][:, j * 512:(j + 1) * 512], pk == 0, pk == NT - 1, [tPT[ps_]], [tPS[accb[j]]])

        def tail_A(acc, hh, c, pp):
            dr = slice(64 * hh, 64 * hh + 64)
            nr = slice(64 * (1 - hh), 64 * (1 - hh) + 64)
            ta = [tPS[2 * acc], tPS[2 * acc + 1]]
            P.add("dve", lambda e: e.reciprocal(out=Rt[dr, :], in_=PS[acc][nr, :]), ta, [tR])
            if c == 0:
                tt_("dve", OP[dr, :], PS[acc][dr, :], Rt[dr, :], ALU.mult, ta + [tR], [tOP])
                return
            tt_("dve", Tt[dr, :], PS[acc][dr, :], Rt[dr, :], ALU.mult, ta + [tR], [tTt])
            stt_("dve", OP[dr, :], Tt[dr, :], neglam[dr, :], OP[dr, :], ALU.mult, ALU.add, [tTt, tOP], [tOP])
            if hh == 0:
                return
            tt_("pool", SQ[:], OP[:], OP[:], ALU.mult, [tOP], [tSQ])

            def st2():
                for j in range(2):
                    mm(psb(2 * acc + j), bdm[:], SQ[:, j * 512:(j + 1) * 512], True, True, [tSQ, tConst], [tPS[2 * acc + j]])
                defer(3, st3)

            def st3():
                act(Rt[:], PS[acc][:], AF.Sqrt, ta, [tR], bias=epsT[:, 0:1])
                defer(2, st4)

            def st4():
                P.add("dve", lambda e: e.reciprocal(out=Rt[:], in_=Rt[:]), [tR], [tR])
                stt_("dve", mTAB[:, pp, :], OP[:], gA2[:, 0:1], Rt[:], ALU.mult, ALU.mult, [tOP, tR, tPar], [tmT[pp]])
            defer(4, st2)

        def tail_B(acc, hh, cB):
            dr = slice(64 * hh, 64 * hh + 64)
            nr = slice(64 * (1 - hh), 64 * (1 - hh) + 64)
            ta = [tPS[2 * acc], tPS[2 * acc + 1]]
            P.add("dve", lambda e: e.reciprocal(out=Rt[dr, :], in_=PS[acc][nr, :]), ta, [tR])
            tt_("dve", mTAB[dr, 2 + cB, :], PS[acc][dr, :], Rt[dr, :], ALU.mult, ta + [tR], [tmT[2 + cB]])

        for qh in range(2):
            q0 = qh * 1024
            maps = []
            for pp in range(2):
                for hh in range(2):
                    h = 2 * pp + hh
                    for c in range(2):
                        r0 = (hh * 2 + c) * 32
                        maps.append(dict(
                            kT_fn=lambda kb, pp=pp: AkT[:, pp, kb * 128:(kb + 1) * 128],
                            q_src=AqT[r0:r0 + 32, pp, q0:q0 + 1024], rows=slice(r0, r0 + 32),
                            q_src64=AqT[64:128, pp, q0:q0 + 1024],
                            v_fn=lambda kb, h=h: VX[:, kb, h * 128:(h + 1) * 128],
                            scale=32 ** -0.5,
                            tail=lambda acc, hh=hh, c=c, pp=pp: tail_A(acc, hh, c, pp)))
            for cB in range(4):
                for hh in range(2):
                    hB = 2 * cB + hh
                    j_kv, g = hB // 4, hB % 4
                    rows = slice(64 * j_kv, 64 * j_kv + 64)
                    voff = 512 + j_kv * 192 + (64 if hh == 0 else 0)
                    maps.append(dict(
                        kT_fn=lambda kb: BkT[:, kb * 128:(kb + 1) * 128],
                        q_src=BqT[rows, g, q0:q0 + 1024], rows=rows, q_src64=None,
                        v_fn=lambda kb, voff=voff: VX[:, kb, voff:voff + 128],
                        scale=64 ** -0.5,
                        tail=lambda acc, hh=hh, cB=cB: tail_B(acc, hh, cB)))
            prep_qm(maps[0], cnt["m"] % 2)
            for i, d in enumerate(maps):
                qs = cnt["m"] % 2
                acc = 2 + (cnt["m"] % 2)
                cnt["m"] += 1
                if i + 1 < len(maps):
                    prep_qm(maps[i + 1], cnt["m"] % 2)
                cv_step()
                run_map(d, qs, acc)
                defer(3, lambda d=d, acc=acc: d["tail"](acc))
            run_due(force=True)
            for tl in range(8):
                tt = qh * 8 + tl
                acc = 2 + (tl % 2)
                for dg in range(2):
                    for ec in range(8):
                        if ec < 6:
                            lhs = mTAB[:, ec, tl * 128:(tl + 1) * 128]
                            rd = [tmT[ec], tWO]
                        else:
                            lhs = mTC[:, ec - 6, tt * 128:(tt + 1) * 128]
                            rd = [tWO]
                        mm(psb(2 * acc + dg), lhs, WOUT[:, ec, dg * 512:(dg + 1) * 512], ec == 0, ec == 7, rd, [tPS[2 * acc + dg]])
                ta = [tPS[2 * acc], tPS[2 * acc + 1]]
                Yx, tYx = (Yb, tY) if tl % 2 == 0 else (Tt, tTt)
                stt_("dve", Yx[:], X[:, tt, :], float(ALPHA), PS[acc][:], ALU.mult, ALU.add, ta + [tX[tt]], [tYx])
                layernorm_tail(Yx, tYx, tt, tsm, so=44 + 20 * (tl % 2))

    def layernorm_tail(Y, tY, tt, tsm, so=44):
        for j in range(2):
            P.add("dve", lambda e, j=j: e.bn_stats(out=sm[:, so + 6 * j:so + 6 + 6 * j], in_=Y[:, j * 512:(j + 1) * 512]), [tY], [tsm])
        P.add("dve", lambda e: e.bn_aggr(out=sm[:, so + 12:so + 14], in_=sm[:, so:so + 12].rearrange("p (a b) -> p a b", a=2)), [tsm], [tsm])
        rstd_chain(sm[:, so + 14:so + 15], sm[:, so + 13:so + 14], 1, 1.0, tsm)
        stt_("dve", sm[:, so + 15:so + 16], sm[:, so + 12:so + 13], -1.0, sm[:, so + 14:so + 15], ALU.mult, ALU.mult, [tsm], [tsm])
        act(X[:, tt, :], Y[:], AF.Identity, [tY, tsm], [tX[tt]], scale=sm[:, so + 14:so + 15], bias=sm[:, so + 15:so + 16])

    tG1T = T("g1T")

    def load_g1T(l):
        dma("sp", g1T[:], l1g_d[l].rearrange("(kc p) -> p kc", p=128), "g1t", [], [tG1T], slow=True)
        dma("sp", b1T[:], l1b_d[l].rearrange("(kc p) -> p kc", p=128), "g1t", [], [tG1T], slow=True)

    def phaseC(l, s, last, prefetch):
        cv_step(1000)
        tG1, tG2 = T("g1"), T("g2")
        tWG = [T("wg0"), T("wg1")]
        tWU = [T("wu0"), T("wu1")]
        tWD = [T("wd0"), T("wd1")]
        thid = [T("hid%d" % f) for f in range(NFC)]
        txTf = [T("xTf0"), T("xTf1")]
        tXB, tSG, tsm = T("XBc"), [T("sg0"), T("sg1")], T("smC")
        tYs = [T("Y0"), T("Y1"), T("Y2")]
        wgs = wg_d[l].rearrange("(kc p) f -> p kc f", p=128)
        wus = wu_d[l].rearrange("(kc p) f -> p kc f", p=128)
        wds = wd_d[l].rearrange("(fc p) d -> p fc d", p=128)
        seq = {"gu": 0, "d": 0}

        def load_gu(fb):
            sl = seq["gu"] % 2
            seq["gu"] += 1
            if USE_CV:
                dma("sp", WG[sl][:].rearrange("p a b -> p (a b)"), wg_bf[l, fb], "wg%d" % sl, [tCV[l]], [tWG[sl]])
                dma("sp", WU[sl][:].rearrange("p a b -> p (a b)"), wu_bf[l, fb], "wu%d" % sl, [tCV[l]], [tWU[sl]])
            else:
                dma("pool", WG[sl][:], wgs[:, :, fb * 256:(fb + 1) * 256], "wg%d" % sl, [], [tWG[sl]])
                dma("pool", WU[sl][:], wus[:, :, fb * 256:(fb + 1) * 256], "wu%d" % sl, [], [tWU[sl]])
            return sl

        def load_d(db):
            sl = seq["d"] % 2
            seq["d"] += 1
            n = 4 if db < 5 else 2
            if USE_CV:
                dma("sp", WD[sl][:, 0:n, :].rearrange("p a b -> p (a b)"), wd_bf[l, db, :, 0:n * 1024], "wd%d" % sl, [tCV[l]], [tWD[sl]])
            else:
                dma("pool", WD[sl][:, 0:n, :], wds[:, db * 4:db * 4 + n, :], "wd%d" % sl, [], [tWD[sl]])
            return sl

        def front(tg):
            xs = tg % 2
            for tl in range(4):
                tt = tg * 4 + tl
                cp("act", XBc[:], X[:, tt, :], [tX[tt]], [tXB])
                bank = 4 + tl
                for kc in range(8):
                    tr(psb_bf(bank)[:, kc * 128:(kc + 1) * 128], XBc[:, kc * 128:(kc + 1) * 128], [tXB], [tPS[bank]])
                for kc in range(8):
                    ts_("dve", xTf2[xs][:, kc, tl * 128:(tl + 1) * 128], psb_bf(bank)[:, kc * 128:(kc + 1) * 128],
                        g1T[:, kc:kc + 1], b1T[:, kc:kc + 1], ALU.mult, ALU.add, [tPS[bank], tG1T], [txTf[xs]])
                tt_("pool", X[:, tt, :], X[:, tt, :], g1_b[:], ALU.mult, [tX[tt], tG1], [tX[tt]])
                tt_("pool", X[:, tt, :], X[:, tt, :], b1_b[:], ALU.add, [tX[tt], tG1], [tX[tt]])

        def C1(tg, nxt):
            xs = tg % 2
            for fb in range(11):
                sl = nxt
                if fb + 1 < 11:
                    nxt = load_gu(fb + 1)
                for fci in range(2):
                    fc = 2 * fb + fci
                    gb = 0 + (fc % 2)
                    ub = 2 + (fc % 2)
                    for kc in range(8):
                        mm(psb(gb), WG[sl][:, kc, fci * 128:(fci + 1) * 128], xTf2[xs][:, kc, :], kc == 0, kc == 7, [tWG[sl], txTf[xs]], [tPS[gb]])
                    for kc in range(8):
                        mm(psb(ub), WU[sl][:, kc, fci * 128:(fci + 1) * 128], xTf2[xs][:, kc, :], kc == 0, kc == 7, [tWU[sl], txTf[xs]], [tPS[ub]])
                    act(SG[fc % 2][:], psb(gb), AF.Silu, [tPS[gb]], [tSG[fc % 2]])
                    tt_("dve", hidT[:, fc, :], psb(ub), SG[fc % 2][:], ALU.mult, [tPS[ub], tSG[fc % 2]], [thid[fc]])

        def C2(tg):
            nd = load_d(0)
            for db in range(6):
                sl = nd
                if db + 1 < 6:
                    nd = load_d(db + 1)
                n = 4 if db < 5 else 2
                for fci in range(n):
                    fc = db * 4 + fci
                    for tl in range(4):
                        for dg in range(2):
                            mm(psb(2 * tl + dg), hidT[:, fc, tl * 128:(tl + 1) * 128], WD[sl][:, fci, dg * 512:(dg + 1) * 512],
                               fc == 0, fc == NFC - 1, [thid[fc], tWD[sl]], [tPS[2 * tl + dg]])

        def LN2(tg):
            def evac(tl):
                tt = tg * 4 + tl
                ta = [tPS[2 * tl], tPS[2 * tl + 1]]
                stt_("dve", Ys[tl % 3][:], X[:, tt, :], float(ALPHA), PS[tl][:], ALU.mult, ALU.add, ta + [tX[tt]], [tYs[tl % 3]])

            def tail(tl):
                tt = tg * 4 + tl
                layernorm_tail(Ys[tl % 3], tYs[tl % 3], tt, tsm)
                tt_("pool", X[:, tt, :], X[:, tt, :], g2_b[:], ALU.mult, [tX[tt], tG2], [tX[tt]])
                tt_("pool", X[:, tt, :], X[:, tt, :], b2_b[:], ALU.add, [tX[tt], tG2], [tX[tt]])
                if last:
                    dma("sp", out_d[s, tt * 128:(tt + 1) * 128, :], X[:, tt, :], "out", [tX[tt]], [])
            evac(0)
            evac(1)
            evac(2)
            tail(0)
            evac(3)
            tail(1)
            tail(2)
            tail(3)

        nxt = load_gu(0)
        dma("sp", g1_b[:], l1g_d[l:l + 1, :].broadcast_to([128, D]), "lng", [], [tG1])
        dma("sp", b1_b[:], l1b_d[l:l + 1, :].broadcast_to([128, D]), "lng", [], [tG1])
        dma("sp", g2_b[:], l2g_d[l:l + 1, :].broadcast_to([128, D]), "lng2", [], [tG2])
        dma("sp", b2_b[:], l2b_d[l:l + 1, :].broadcast_to([128, D]), "lng2", [], [tG2])
        front(0)
        for tg in range(4):
            C1(tg, nxt)
            if tg == 0:
                prefetch()
            if tg + 1 < 4:
                nxt = load_gu(0)
                front(tg + 1)
            C2(tg)
            LN2(tg)

    load_win(0, 0)
    load_tables()
    load_params(0)
    if USE_CV:
        convert_layer(0)
    for s in range(nseq):
        if s > 0:
            P.barrier()
        for q in range(4):
            dma("sp", X[:, q * 4:(q + 1) * 4, :], x_d[s, q * 512:(q + 1) * 512, :].rearrange("(t p) d -> p t d", p=128), "x%d" % q,
                [], [tX[q * 4 + i] for i in range(4)])
        def dbg_store():
            P.barrier()
            for tt in range(NT):
                dma("sp", out_d[s, tt * 128:(tt + 1) * 128, :], X[:, tt, :], "out", [tX[tt]], [])
        for l in range(L):
            P.barrier()
            if stop == "load":
                dbg_store()
                break
            phaseA(l)
            P.barrier()
            if stop == "A":
                dbg_store()
                break
            if USE_CV and s == 0 and l + 1 < L:
                convert_layer(l + 1)
            load_g1T(l)
            phaseB(l)
            P.barrier()
            if stop == "B":
                dbg_store()
                break
            nl = l + 1 if l + 1 < L else (0 if s + 1 < nseq else None)

            def prefetch(nl=nl):
                if nl is not None:
                    load_win(nl, 0)
                    load_tables()
                    load_params(nl)
            phaseC(l, s, l == L - 1, prefetch)
    P.barrier()
    P.finalize_and_emit(st)
    st.close()
    return nc, P


_CACHE = {}


def kernel(**inputs):
    n_cores = 8
    nseq = 32 // n_cores
    depth = 4
    if "nc" not in _CACHE:
        _CACHE["nc"] = build(nseq, depth)[0]
    nc = _CACHE["nc"]
    consts = host_consts()
    maps = []
    for c in range(n_cores):
        m = {k: np.ascontiguousarray(np.asarray(v, dtype=np.float32)) for k, v in inputs.items() if k != "x"}
        m["x"] = np.ascontiguousarray(np.asarray(inputs["x"], dtype=np.float32)[c * nseq:(c + 1) * nseq])
        m.update(consts)
        maps.append(m)
    res = run_bass_kernel_spmd(nc, maps, core_ids=list(range(n_cores)))
    return np.concatenate([np.asarray(r["out"]) for r in res.results], axis=0).astype(np.float32)
```

```python
import math, os
CUT = int(os.environ.get('KB_CUT', '99'))
USE_CV = True
import numpy as np
from contextlib import ExitStack
import concourse.bass as bass
import concourse.mybir as mybir

from concourse.bass_utils import run_bass_kernel_spmd


ENGS = ("pe", "act", "dve", "pool", "sp")


class T:
    __slots__ = ("name", "w", "r")

    def __init__(self, name=""):
        self.name = name
        self.w = None
        self.r = []


class Op:
    __slots__ = ("fn", "deps", "signal", "dma_sem", "dma_val")

    def __init__(self, fn, deps, dma_sem=None, dma_val=0):
        self.fn = fn
        self.deps = deps
        self.signal = False
        self.dma_sem = dma_sem
        self.dma_val = dma_val


class Prog:
    def __init__(self, nc):
        self.nc = nc
        self.ops = {e: [] for e in ENGS}
        self.dma_counts = {}
        self.n_wait = 0

    def _deps(self, eng, reads, writes):
        deps = set()
        for t in reads:
            if t.w is not None:
                deps.add(t.w)
        for t in writes:
            if t.w is not None:
                deps.add(t.w)
            for x in t.r:
                deps.add(x)
        best = {}
        for d in deps:
            if d[0] == "c" and d[1] == eng and eng == "pe":
                continue
            k = (d[0], d[1])
            if k not in best or best[k][2] < d[2]:
                best[k] = d
        return list(best.values())

    def add(self, eng, fn, reads=(), writes=()):
        deps = self._deps(eng, reads, writes)
        idx = len(self.ops[eng])
        self.ops[eng].append(Op(fn, deps))
        tok = ("c", eng, idx)
        for t in reads:
            t.r.append(tok)
        for t in writes:
            t.w = tok
            t.r = []
        return tok

    def dma(self, eng, fn, sem_key, reads=(), writes=()):
        deps = self._deps(eng, reads, writes)
        n = self.dma_counts.get(sem_key, 0) + 1
        self.dma_counts[sem_key] = n
        self.ops[eng].append(Op(fn, deps, dma_sem=sem_key, dma_val=16 * n))
        tok = ("d", sem_key, 16 * n)
        for t in reads:
            t.r.append(tok)
        for t in writes:
            t.w = tok
            t.r = []
        return tok

    def barrier(self):
        toks = []
        for e in ENGS:
            for i in range(len(self.ops[e]) - 1, -1, -1):
                op = self.ops[e][i]
                if op.fn is not None and op.dma_sem is None:
                    toks.append(("c", e, i))
                    break
        for k, n in self.dma_counts.items():
            if not str(k).startswith("cv"):
                toks.append(("d", k, 16 * n))
        for e in ENGS:
            self.wait_tokens(e, [t for t in toks if not (t[0] == "c" and t[1] == e and e == "pe")])

    def wait_tokens(self, eng, toks):
        self.ops[eng].append(Op(None, list(toks)))

    def finalize_and_emit(self, stack):
        nc = self.nc
        for e in ENGS:
            for op in self.ops[e]:
                for d in op.deps:
                    if d[0] == "c":
                        self.ops[d[1]][d[2]].signal = True
        val = {}
        for e in ENGS:
            c = 0
            for i, op in enumerate(self.ops[e]):
                if op.signal:
                    c += 1
                    val[(e, i)] = c
            assert c < 60000, (e, c)
        csem = {e: stack.enter_context(nc.semaphore("cs_" + e)) for e in ENGS}
        dsem = {k: stack.enter_context(nc.semaphore("ds_" + str(k))) for k in self.dma_counts}
        handles = {"pe": "tensor", "act": "scalar", "dve": "vector", "pool": "gpsimd", "sp": "sync"}
        block = stack.enter_context(nc.Block())
        prog = self

        def make(e):
            def body(eng):
                waited = {}
                for i, op in enumerate(prog.ops[e]):
                    need = {}
                    for d in op.deps:
                        if d[0] == "c":
                            key = ("c", d[1])
                            v = val[(d[1], d[2])]
                        else:
                            key = ("d", d[1])
                            v = d[2]
                        if waited.get(key, 0) >= v:
                            continue
                        if need.get(key, 0) < v:
                            need[key] = v
                    for key, v in need.items():
                        sem = csem[key[1]] if key[0] == "c" else dsem[key[1]]
                        eng.wait_ge(sem, v)
                        waited[key] = v
                        prog.n_wait += 1
                    if op.fn is None:
                        continue
                    ins = op.fn(eng)
                    if op.dma_sem is not None:
                        ins.then_inc(dsem[op.dma_sem], 16)
                    elif op.signal:
                        ins.then_inc(csem[e], 1)
            return body

        for e in ENGS:
            if not self.ops[e]:
                continue
            getattr(block, handles[e])(make(e))


F32 = mybir.dt.float32
BF16 = mybir.dt.bfloat16
AF = mybir.ActivationFunctionType
ALU = mybir.AluOpType
AX = mybir.AxisListType

D = 1024
S = 2048
NT = 16
HID = 2816
NFC = 22
EPS = 1e-5
ALPHA = (2 * 4) ** 0.25
KB = 1024


def host_consts():
    inv = 1.0 / (10000.0 ** (np.arange(0, 32, 2, dtype=np.float32) / 32.0))
    t = np.arange(S, dtype=np.float32)

    def cs(pos):
        ang = pos[:, None] * inv[None, :]
        ang = np.concatenate([ang, ang], -1)
        c = np.cos(ang).astype(np.float32)
        s = np.sin(ang).astype(np.float32)
        s[:, :16] *= -1.0
        return c, s
    cA, sA = cs(t)
    row = np.floor(t / 64.0).astype(np.float32)
    col = (t - 64.0 * row).astype(np.float32)
    cR, sR = cs(row)
    cC, sC = cs(col)
    cB = np.concatenate([cR, cC], -1)
    sB = np.concatenate([sR, sC], -1)

    def lay(a):
        n = a.shape[1]
        return np.ascontiguousarray(a.reshape(NT, 128, n).transpose(1, 0, 2).reshape(128, NT * n))
    tabs = np.concatenate([lay(cA), lay(sA), lay(cB), lay(sB)], 1).astype(np.float32)
    ident = np.eye(128, dtype=np.float32)
    bd = np.zeros((128, 128), np.float32)
    bd[:64, :64] = 1.0 / 64
    bd[64:, 64:] = 1.0 / 64
    return dict(tabs=tabs, ident=ident, bd=bd)


def build(nseq, depth, stop=None):
    nc = bass.Bass("TRN2", target_bir_lowering=False)
    L = depth

    def din(name, shape):
        return nc.dram_tensor(name, shape, F32, kind="ExternalInput").ap()
    x_d = din("x", [nseq, S, D])
    w_in_d = din("w_in", [L, D, 2048])
    w_out_d = din("w_out", [L, D, D])
    lam_d = din("lam_qk", [L, 4, 32])
    asg_d = din("a_subln_g", [L, 64])
    bqg_d = din("b_q_norm_g", [L, 64])
    bkg_d = din("b_k_norm_g", [L, 64])
    clg_d = din("c_ln_g", [L, 256])
    clb_d = din("c_ln_b", [L, 256])
    cws_d = din("c_w_s", [L, 4, 128, 128])
    cbs_d = din("c_b_s", [L, 4, 128])
    l1g_d = din("ln1_g", [L, D])
    l1b_d = din("ln1_b", [L, D])
    wg_d = din("w_gate", [L, D, HID])
    wu_d = din("w_up", [L, D, HID])
    wd_d = din("w_down", [L, HID, D])
    l2g_d = din("ln2_g", [L, D])
    l2b_d = din("ln2_b", [L, D])
    tabs_d = din("tabs", [128, 3072])
    ident_d = din("ident", [128, 128])
    bd_d = din("bd", [128, 128])
    out_d = nc.dram_tensor("out", [nseq, S, D], F32, kind="ExternalOutput").ap()
    if USE_CV:
        wg_bf = nc.dram_tensor("wg_bf", [L, 11, 128, 2048], BF16, kind="Internal").ap()
        wu_bf = nc.dram_tensor("wu_bf", [L, 11, 128, 2048], BF16, kind="Internal").ap()
        wd_bf = nc.dram_tensor("wd_bf", [L, 6, 128, 4096], BF16, kind="Internal").ap()

    st = ExitStack()

    def sb(name, shape, dt):
        return st.enter_context(nc.sbuf_tensor(name, shape, dt))
    X = sb("X", [128, NT, D], F32)
    REG = sb("REG", [128, 80 * KB // 4], F32)
    AR = sb("AR", [128, 57088 // 4], F32)
    ident = sb("identb", [128, 128], BF16)
    bdm = sb("bdm", [128, 128], BF16)
    Ws = sb("Ws", [128, 4, 128], BF16)
    WsT = sb("WsT", [128, 4, 128], BF16)
    gq_b = sb("gq_b", [128, 64], F32)
    gk_b = sb("gk_b", [128, 64], F32)
    cg_b = sb("cg_b", [128, 256], F32)
    cb_b = sb("cb_b", [128, 256], F32)
    bsT = sb("bsT", [128, 4], F32)
    lamq = sb("lamq", [128, 128], F32)
    gA2 = sb("gA2", [128, 1], F32)
    sm = sb("sm", [128, 128], F32)
    PR = sb("PR", [128, 64], F32)
    epsT = sb("epsT", [128, 1], F32)
    g1T = sb("g1T", [128, 8], F32)
    b1T = sb("b1T", [128, 8], F32)
    PS = [st.enter_context(nc.psum_tensor("PS%d" % i, [128, 1024], F32)) for i in range(4)]

    def view(base, off, shape, dt):
        esz = 2 if dt == BF16 else 4
        n = int(np.prod(shape[1:]))
        nb = n * esz
        assert off % 4 == 0 and nb % 4 == 0
        ap = base[:, off // 4:(off + nb) // 4]
        if dt == BF16:
            ap = ap.bitcast(BF16)
        if len(shape) == 3:
            ap = ap.rearrange("p (a b) -> p a b", a=shape[1])
        return ap

    AqT = view(REG, 0, [128, 2, S], BF16)
    AkT = view(REG, 8 * KB, [128, 2, S], BF16)
    BqT = view(REG, 16 * KB, [128, 4, S], BF16)
    BkT = view(REG, 32 * KB, [128, S], BF16)
    VX = view(REG, 36 * KB, [128, NT, 896], BF16)
    mTC = view(REG, 64 * KB, [128, 2, S], BF16)
    xTt = [view(REG, 75 * KB + i * 2 * KB, [128, 8, 128], BF16) for i in range(2)]
    hidT = view(REG, 0, [128, NFC, 512], BF16)
    xTf = view(REG, 22 * KB, [128, 8, 512], BF16)
    WG = [view(REG, 30 * KB + i * 4 * KB, [128, 8, 256], BF16) for i in range(2)]
    WU = [view(REG, 38 * KB + i * 4 * KB, [128, 8, 256], BF16) for i in range(2)]
    WD = [view(REG, 46 * KB + i * 8 * KB, [128, 4, 1024], BF16) for i in range(2)]
    XBc = view(REG, 62 * KB, [128, 1024], BF16)
    SG = [view(REG, 64 * KB + i * 2 * KB, [128, 512], F32) for i in range(2)]
    Yc = view(REG, 68 * KB, [128, 1024], F32)
    g2_b = view(REG, 72 * KB, [128, 1024], F32)
    b2_b = view(REG, 76 * KB, [128, 1024], F32)
    WIN = view(AR, 0, [128, 8, 1024], BF16)
    cosA = view(AR, 16 * KB, [128, NT, 32], F32)
    sinA = view(AR, 18 * KB, [128, NT, 32], F32)
    cosB = view(AR, 20 * KB, [128, NT, 64], F32)
    sinB = view(AR, 24 * KB, [128, NT, 64], F32)
    TA = [view(AR, 28 * KB + i * 2 * KB, [128, 512], F32) for i in range(2)]
    TB = [view(AR, 32 * KB + i * 2 * KB, [128, 512], F32) for i in range(2)]
    TC = [view(AR, 36 * KB + i * 2 * KB, [128, 512], F32) for i in range(2)]
    TD = [view(AR, 40 * KB + i * 2 * KB, [128, 512], F32) for i in range(2)]
    TE = [view(AR, 44 * KB + i * 2 * KB, [128, 512], F32) for i in range(2)]
    OBA = [view(AR, 48 * KB + i * KB, [128, 512], BF16) for i in range(2)]
    OBB = [view(AR, 50 * KB + i * 1280, [128, 640], BF16) for i in range(2)]
    XB = view(AR, 50 * KB + 2560, [128, 1024], BF16)
    VLN = [view(REG, 72 * KB + i * 512, [128, 256], BF16) for i in range(2)]
    CO = [view(REG, 73 * KB + i * 512, [128, 256], BF16) for i in range(2)]
    WOUT = view(AR, 0, [128, 8, 1024], BF16)
    mTAB = view(AR, 16 * KB, [128, 6, 1024], BF16)
    PT = [view(AR, 28 * KB + i * 2 * KB, [128, 1024], BF16) for i in range(2)] + [view(AR, 46 * KB, [128, 1024], BF16)]
    Rt = view(AR, 32 * KB, [128, 1024], F32)
    Tt = view(AR, 36 * KB, [128, 1024], F32)
    OP = view(AR, 40 * KB, [128, 1024], F32)
    SQ = view(AR, 44 * KB, [128, 1024], BF16)
    Yb = view(AR, 46 * KB, [128, 1024], F32)
    QM = [view(AR, 50 * KB + i * 2 * KB, [128, 1024], BF16) for i in range(2)]
    g1_b = view(AR, 44 * KB, [128, 1024], F32)
    b1_b = view(AR, 48 * KB, [128, 1024], F32)
    xTf2 = [xTf, view(AR, 28 * KB, [128, 8, 512], BF16)]
    Ys = [Yc, view(AR, 36 * KB, [128, 1024], F32), view(AR, 40 * KB, [128, 1024], F32)]

    P = Prog(nc)
    tX = [T("x%d" % i) for i in range(NT)]
    tPS = [T("ps%d" % i) for i in range(8)]

    def psb(i):
        return PS[i // 2][:, (i % 2) * 512:(i % 2) * 512 + 512]

    def psb_bf(i):
        return PS[i // 2][:, (i % 2) * 512:(i % 2) * 512 + 512].bitcast(BF16)
    tConst = T("const")
    tPar = T("par")
    tWs = T("Ws")
    tTab = T("tab")
    tWIN = [T("win%d" % g) for g in range(2)]

    def mm(out, lhsT, rhs, start, stop, reads, writes, tp=None):
        if tp is None:
            P.add("pe", lambda e: e.matmul(out, lhsT=lhsT, rhs=rhs, start=start, stop=stop), reads, writes)
        else:
            P.add("pe", lambda e: e.matmul(out, lhsT=lhsT, rhs=rhs, start=start, stop=stop, tile_position=tp), reads, writes)

    def tr(out, in_, reads, writes):
        P.add("pe", lambda e: e.transpose(out=out, in_=in_, identity=ident[:]), list(reads) + [tConst], writes)

    def act(out, in_, func, reads, writes, scale=None, bias=None):
        kw = {}
        if scale is not None:
            kw["scale"] = scale
        if bias is not None:
            kw["bias"] = bias
        P.add("act", lambda e: e.activation(out=out, in_=in_, func=func, **kw), reads, writes)

    def tt_(eng, out, in0, in1, op, reads, writes):
        P.add(eng, lambda e: e.tensor_tensor(out=out, in0=in0, in1=in1, op=op), reads, writes)

    def ts_(eng, out, in0, s1, s2, op0, op1, reads, writes):
        if op1 is None:
            P.add(eng, lambda e: e.tensor_scalar(out=out, in0=in0, scalar1=s1, scalar2=None, op0=op0), reads, writes)
        else:
            P.add(eng, lambda e: e.tensor_scalar(out=out, in0=in0, scalar1=s1, scalar2=s2, op0=op0, op1=op1), reads, writes)

    def stt_(eng, out, in0, scalar, in1, op0, op1, reads, writes):
        P.add(eng, lambda e: e.scalar_tensor_tensor(out=out, in0=in0, scalar=scalar, in1=in1, op0=op0, op1=op1), reads, writes)

    def cp(eng, out, in_, reads, writes):
        if eng == "act":
            P.add("act", lambda e: e.copy(out=out, in_=in_), reads, writes)
        else:
            P.add(eng, lambda e: e.tensor_copy(out=out, in_=in_), reads, writes)

    def dma(eng, out, in_, key, reads, writes, slow=False):
        if slow:
            P.dma(eng, lambda e: e.dma_start(out=out, in_=in_, allow_slow_non_contiguous=True), key, reads, writes)
        else:
            P.dma(eng, lambda e: e.dma_start(out=out, in_=in_), key, reads, writes)

    def rstd_chain(dst, src, n, scale, reads_t):
        ts_("dve", dst, src, scale, EPS, ALU.mult, ALU.add, [reads_t], [reads_t])
        act(dst, dst, AF.Sqrt, [reads_t], [reads_t])
        P.add("dve", lambda e: e.reciprocal(out=dst, in_=dst), [reads_t], [reads_t])

    dma("pool", ident[:], ident_d[:, :], "c0", [], [tConst])
    dma("pool", bdm[:], bd_d[:, :], "c1", [], [tConst])
    P.add("dve", lambda e: e.memset(epsT[:], EPS), [], [tConst])

    def load_tables(q="sp"):
        dma(q, cosA.rearrange("p a b -> p (a b)"), tabs_d[:, 0:512], "tab", [], [tTab])
        dma(q, sinA.rearrange("p a b -> p (a b)"), tabs_d[:, 512:1024], "tab", [], [tTab])
        dma(q, cosB.rearrange("p a b -> p (a b)"), tabs_d[:, 1024:2048], "tab", [], [tTab])
        dma(q, sinB.rearrange("p a b -> p (a b)"), tabs_d[:, 2048:3072], "tab", [], [tTab])

    def load_win(l, half):
        src = w_in_d[l].rearrange("(kc p) c -> p kc c", p=128)
        if half == 0:
            lst = ((0, 512, 0, 0), (768, 1280, 512, 1))
        else:
            lst = ((1280, 1408, 0, 0), (512, 768, 128, 0), (1408, 1536, 384, 0), (1536, 2048, 512, 1))
        for (s0, s1, d0, g) in lst:
            dma("pool", WIN[:, :, d0:d0 + (s1 - s0)], src[:, :, s0:s1], "win%d" % g, [], [tWIN[g]])

    def load_params(l, q="sp"):
        dma(q, gq_b[:], bqg_d[l:l + 1, :].broadcast_to([128, 64]), "par", [], [tPar])
        dma(q, gk_b[:], bkg_d[l:l + 1, :].broadcast_to([128, 64]), "par", [], [tPar])
        dma(q, cg_b[:], clg_d[l:l + 1, :].broadcast_to([128, 256]), "par", [], [tPar])
        dma(q, cb_b[:], clb_d[l:l + 1, :].broadcast_to([128, 256]), "par", [], [tPar])
        dma(q, bsT[:], cbs_d[l].rearrange("g p -> p g"), "par", [], [tPar], slow=True)
        dma(q, lamq[:], lam_d[l:l + 1].rearrange("o a b -> o (a b)").broadcast_to([128, 128]), "par", [], [tPar])
        dma(q, gA2[0:64, :], asg_d[l].rearrange("(d o) -> d o", o=1), "par", [], [tPar])
        dma(q, gA2[64:128, :], asg_d[l].rearrange("(d o) -> d o", o=1), "par", [], [tPar])
        dma("pool", Ws[:], cws_d[l].rearrange("g p q -> p g q"), "ws", [], [tWs])

    tCV = [T("cv%d" % l) for l in range(L)]

    cvq = []

    def convert_layer(l):
        wgs = wg_d[l].rearrange("(kc p) f -> p kc f", p=128)
        wus = wu_d[l].rearrange("(kc p) f -> p kc f", p=128)
        wds = wd_d[l].rearrange("(fc p) d -> p fc d", p=128)
        for fb in range(11):
            cvq.append(lambda l=l, fb=fb: dma("pool", wg_bf[l, fb].rearrange("p (a b) -> p a b", a=8), wgs[:, :, fb * 256:(fb + 1) * 256],
                                               "cv%d" % l, [], [tCV[l]]))
            cvq.append(lambda l=l, fb=fb: dma("pool", wu_bf[l, fb].rearrange("p (a b) -> p a b", a=8), wus[:, :, fb * 256:(fb + 1) * 256],
                                               "cv%d" % l, [], [tCV[l]]))
        for db in range(6):
            n = 4 if db < 5 else 2
            cvq.append(lambda l=l, db=db, n=n: dma("pool", wd_bf[l, db, :, 0:n * 1024].rearrange("p (a b) -> p a b", a=n),
                                                    wds[:, db * 4:db * 4 + n, :], "cv%d" % l, [], [tCV[l]]))

    def cv_step(n=1):
        for _ in range(n):
            if cvq:
                cvq.pop(0)()

    def phaseA(l):
        lam_init = 0.8 - 0.6 * math.exp(-0.3 * l)
        tS = [{k: T(k + str(i)) for k in ("TA", "TB", "TC", "TD", "TE", "OBA", "OBB", "VLN", "CO", "sm", "xT")} for i in range(2)]
        tXB, tsm0 = T("XB"), T("smA")
        tKVQ = T("kvq")
        lq = lamq[:].rearrange("p (a b c) -> p a b c", a=2, b=2)
        tt_("dve", PR[:].rearrange("p (a c) -> p a c", a=2), lq[:, :, 0, :], lq[:, :, 1, :], ALU.mult, [tPar], [tsm0])
        P.add("dve", lambda e: e.tensor_reduce(out=sm[:, 32:34], in_=PR[:].rearrange("p (a c) -> p a c", a=2), axis=AX.X, op=ALU.add),
              [tsm0], [tsm0])
        act(sm[:, 34:36], sm[:, 32:34], AF.Exp, [tsm0], [tsm0])
        tt_("dve", sm[:, 36:37], sm[:, 35:36], sm[:, 34:35], ALU.subtract, [tsm0], [tsm0])
        ts_("dve", sm[:, 40:41], sm[:, 36:37], -lam_init, None, ALU.add, None, [tsm0], [tsm0])
        ts_("dve", gA2[:], gA2[:], 1.0 - lam_init, None, ALU.mult, None, [tPar], [tPar])
        for g in range(4):
            tr(psb_bf(7)[:, g * 128:(g + 1) * 128], Ws[:, g, :], [tWs], [tPS[7]])
        cp("dve", WsT[:].rearrange("p a b -> p (a b)"), psb_bf(7)[:, 0:512], [tPS[7]], [tWs])
        P.add("pool", lambda e: e.memset(VX[:, 0:8, :], 1.0), [], [tKVQ])
        P.add("pool", lambda e: e.memset(VX[:, 8:16, :], 1.0), [], [tKVQ])

        def xT_for_tile(tt, slot, bank):
            cp("act", XB[:], X[:, tt, :], [tX[tt]], [tXB])
            for kc in range(8):
                tr(psb_bf(bank)[:, kc * 128:(kc + 1) * 128], XB[:, kc * 128:(kc + 1) * 128], [tXB], [tPS[bank]])
            cp("dve", xTt[slot][:].rearrange("p a b -> p (a b)"), psb_bf(bank)[:, 0:1024], [tPS[bank]], [tS[slot]["xT"]])

        def inproj(slot, g, bank):
            for kc in range(8):
                mm(psb(bank), xTt[slot][:, kc, :], WIN[:, kc, g * 512:(g + 1) * 512], kc == 0, kc == 7,
                   [tS[slot]["xT"], tWIN[g]], [tPS[bank]])

        def rms_rope_B(H, nh, so, slot, tt, g_b, out_ap_fn):
            t = tS[slot]
            n = nh * 64
            bank_t = H["t"]
            Hs = H["ap"]
            smq = sm[:, so:so + nh]
            act(TC[slot][:, 0:n], Hs, AF.Square, [bank_t], [t["TC"]])
            P.add("dve", lambda e: e.tensor_reduce(out=smq, in_=TC[slot][:, 0:n].rearrange("p (h d) -> p h d", h=nh), axis=AX.X, op=ALU.add),
                  [t["TC"]], [t["sm"]])
            rstd_chain(smq, smq, nh, 1.0 / 64, t["sm"])
            tt_("dve", TD[slot][:, 0:n].rearrange("p (h d) -> p h d", h=nh), Hs.rearrange("p (h d) -> p h d", h=nh),
                g_b[:].unsqueeze(1).broadcast_to([128, nh, 64]), ALU.mult, [bank_t, tPar], [t["TD"]])
            cBt = cosB[:, tt, :].unsqueeze(1).broadcast_to([128, nh, 64])
            tt_("pool", TE[slot][:, 0:n].rearrange("p (h d) -> p h d", h=nh), TD[slot][:, 0:n].rearrange("p (h d) -> p h d", h=nh), cBt, ALU.mult,
                [t["TD"], tTab], [t["TE"]])
            sBv = sinB[:, tt, :].rearrange("p (r h d) -> p r h d", r=2, h=2)
            for hf in range(2):
                o_ = TC[slot][:, 0:n].rearrange("p (a r h d) -> p a r h d", a=nh, r=2, h=2)[:, :, :, hf, :]
                i_ = TD[slot][:, 0:n].rearrange("p (a r h d) -> p a r h d", a=nh, r=2, h=2)[:, :, :, 1 - hf, :]
                s_ = sBv[:, :, hf, :].unsqueeze(1).broadcast_to([128, nh, 2, 16])
                tt_("dve", o_, i_, s_, ALU.mult, [t["TD"], tTab], [t["TC"]])
            tt_("pool", TE[slot][:, 0:n], TE[slot][:, 0:n], TC[slot][:, 0:n], ALU.add, [t["TE"], t["TC"]], [t["TE"]])
            out_ap_fn(TE[slot][:, 0:n], smq)

        def banks(tt):
            return (0, 1) if tt % 2 == 0 else (2, 3)

        def F0(tt):
            cv_step()
            slot = tt % 2
            b0, b1 = banks(tt)
            xT_for_tile(tt, slot, 4 if slot == 0 else 7)
            inproj(slot, 0, b0)
            inproj(slot, 1, b1)

        def M0(tt):
            slot = tt % 2
            t = tS[slot]
            b0, b1 = banks(tt)
            H0 = psb(b0)
            H0v = H0.rearrange("p (v h d) -> p v h d", v=16, h=2)
            cA = cosA[:, tt, :].unsqueeze(1).broadcast_to([128, 16, 32])
            sA0 = sinA[:, tt, 0:16].unsqueeze(1).broadcast_to([128, 16, 16])
            sA1 = sinA[:, tt, 16:32].unsqueeze(1).broadcast_to([128, 16, 16])
            TAv = TA[slot][:].rearrange("p (v d) -> p v d", v=16)
            TBv = TB[slot][:].rearrange("p (v h d) -> p v h d", v=16, h=2)
            tt_("dve", TAv, H0.rearrange("p (v d) -> p v d", v=16), cA, ALU.mult, [tPS[b0], tTab], [t["TA"]])
            tt_("dve", TBv[:, :, 0, :], H0v[:, :, 1, :], sA0, ALU.mult, [tPS[b0], tTab], [t["TB"]])
            tt_("dve", TBv[:, :, 1, :], H0v[:, :, 0, :], sA1, ALU.mult, [tPS[b0], tTab], [t["TB"]])
            tt_("pool", OBA[slot][:], TA[slot][:], TB[slot][:], ALU.add, [t["TA"], t["TB"]], [t["OBA"]])

            def outq(src, smq, slot=slot, t=t):
                o_ = OBB[slot][:, 0:512].rearrange("p (g j d) -> p j g d", g=4, j=2)
                tt_("pool", o_, src.rearrange("p (j g d) -> p j g d", j=2, g=4),
                    smq.rearrange("p (j g) -> p j g", j=2).unsqueeze(3).broadcast_to([128, 2, 4, 64]), ALU.mult,
                    [t["TE"], t["sm"]], [t["OBB"]])
            rms_rope_B(dict(ap=psb(b1), t=tPS[b1]), 8, 64 * slot, slot, tt, gq_b, outq)

        def E0(tt):
            slot = tt % 2
            t = tS[slot]
            tok = slice(tt * 128, (tt + 1) * 128)
            tb = 5 + slot
            for blk in range(4):
                tr(psb_bf(tb)[:, blk * 128:(blk + 1) * 128], OBA[slot][:, blk * 128:(blk + 1) * 128], [t["OBA"]], [tPS[tb]])
            for blk in range(4):
                tr(psb_bf(tb)[:, (4 + blk) * 128:(5 + blk) * 128], OBB[slot][:, blk * 128:(blk + 1) * 128], [t["OBB"]], [tPS[tb]])
            p5 = psb_bf(tb).rearrange("p (a b) -> p a b", a=8)
            cp("act", AqT[:, :, tok], p5[:, 0:2, :], [tPS[tb]], [tKVQ])
            cp("act", AkT[:, :, tok], p5[:, 2:4, :], [tPS[tb]], [tKVQ])
            cp("act", BqT[:, :, tok], p5[:, 4:8, :], [tPS[tb]], [tKVQ])

        for i in range(NT + 2):
            if i < NT:
                F0(i)
            if 0 <= i - 1 < NT:
                M0(i - 1)
            if 0 <= i - 2 < NT:
                E0(i - 2)

        load_win(l, 1)
        def F1(tt):
            cv_step()
            slot = tt % 2
            b0, b1 = banks(tt)
            xT_for_tile(tt, slot, 4 + slot)
            inproj(slot, 0, b0)
            inproj(slot, 1, b1)

        def M1(tt):
            slot = tt % 2
            t = tS[slot]
            b0, b1 = banks(tt)
            H2 = psb(b0)
            H3 = psb(b1)

            def outk(src, smq, slot=slot, t=t):
                tt_("pool", OBB[slot][:, 0:128].rearrange("p (h d) -> p h d", h=2), src.rearrange("p (h d) -> p h d", h=2),
                    smq.unsqueeze(2).broadcast_to([128, 2, 64]), ALU.mult, [t["TE"], t["sm"]], [t["OBB"]])
            rms_rope_B(dict(ap=H2[:, 0:128], t=tPS[b0]), 2, 64 * slot + 8, slot, tt, gk_b, outk)
            Hav = H2[:, 128:384].rearrange("p (a q d) -> p a q d", a=2, q=2)
            VXa = VX[:, tt, 0:512].rearrange("p (a c) -> p a c", a=2)
            cp("dve", VXa[:, :, 0:64], Hav[:, :, 0, :], [tPS[b0]], [tKVQ])
            cp("dve", VXa[:, :, 192:256], Hav[:, :, 1, :], [tPS[b0]], [tKVQ])
            VXb = VX[:, tt, 512:896].rearrange("p (j c) -> p j c", j=2)
            cp("dve", VXb[:, :, 64:128], H2[:, 384:512].rearrange("p (j d) -> p j d", j=2), [tPS[b0]], [tKVQ])
            UV = TA[slot]
            so = 64 * slot
            act(UV[:], H3, AF.Gelu_apprx_tanh, [tPS[b1]], [t["TA"]])
            P.add("dve", lambda e, so=so, UV=UV: e.bn_stats(out=sm[:, so + 16:so + 22], in_=UV[:, 256:512]), [t["TA"]], [t["sm"]])
            P.add("dve", lambda e, so=so: e.bn_aggr(out=sm[:, so + 22:so + 24], in_=sm[:, so + 16:so + 22].rearrange("p (a b) -> p a b", a=1)),
                  [t["sm"]], [t["sm"]])
            rstd_chain(sm[:, so + 24:so + 25], sm[:, so + 23:so + 24], 1, 1.0, t["sm"])
            ts_("dve", TB[slot][:, 0:256], UV[:, 256:512], sm[:, so + 22:so + 23], sm[:, so + 24:so + 25], ALU.subtract, ALU.mult,
                [t["TA"], t["sm"]], [t["TB"]])
            tt_("pool", TB[slot][:, 0:256], TB[slot][:, 0:256], cg_b[:], ALU.mult, [t["TB"], tPar], [t["TB"]])
            tt_("pool", VLN[slot][:], TB[slot][:, 0:256], cb_b[:], ALU.add, [t["TB"], tPar], [t["VLN"]])

        def E1(tt):
            slot = tt % 2
            t = tS[slot]
            tok = slice(tt * 128, (tt + 1) * 128)
            UV = TA[slot]
            tr(psb_bf(6)[:, 0:128], OBB[slot][:, 0:128], [t["OBB"]], [tPS[6]])
            for g in range(4):
                mm(psb(7)[:, g * 64:(g + 1) * 64], WsT[:, g, :], VLN[slot][:, g * 64:(g + 1) * 64], True, True, [tWs, t["VLN"]], [tPS[7]])
            for g in range(4):
                stt_("dve", CO[slot][:, g * 64:(g + 1) * 64], psb(7)[:, g * 64:(g + 1) * 64], bsT[:, g:g + 1], UV[:, g * 64:(g + 1) * 64],
                     ALU.add, ALU.mult, [tPS[7], tPar, t["TA"]], [t["CO"]])
            for blk in range(2):
                tr(psb_bf(6)[:, (1 + blk) * 128:(2 + blk) * 128], CO[slot][:, blk * 128:(blk + 1) * 128], [t["CO"]], [tPS[6]])
            p6 = psb_bf(6).rearrange("p (a b) -> p a b", a=8)
            cp("act", BkT[:, tok], p6[:, 0, :], [tPS[6]], [tKVQ])
            cp("act", mTC[:, :, tok], p6[:, 1:3, :], [tPS[6]], [tKVQ])

        for i in range(NT + 2):
            if i < NT:
                F1(i)
            if 0 <= i - 1 < NT:
                M1(i - 1)
            if 0 <= i - 2 < NT:
                E1(i - 2)

    def phaseB(l):
        tWO = T("wout")
        dma("pool", WOUT[:], w_out_d[l].rearrange("(ec p) d -> p ec d", p=128), "wout", [], [tWO])
        tR, tTt, tOP, tSQ, tY, tsm = T("R"), T("Tt"), T("OP"), T("SQ"), T("Y"), T("smB")
        tPT = [T("pt0"), T("pt1"), tY]
        tmT = [T("mT%d" % c) for c in range(6)]
        tQM = [T("qm0"), T("qm1")]
        neglam = sm[:, 40:41]
        cnt = {"s": 0, "m": 0, "tick": 0}
        pending = []

        def defer(delay, fn):
            pending.append([cnt["tick"] + delay, fn])

        def run_due(force=False):
            progressed = True
            while progressed:
                progressed = False
                for item in list(pending):
                    if force or item[0] <= cnt["tick"]:
                        pending.remove(item)
                        item[1]()
                        progressed = True

        def prep_qm(d, qs):
            rows = d["rows"]
            P.add("pool", lambda e: e.memset(QM[qs][:], 0.0), [], [tQM[qs]])
            if rows.start == 96:
                P.add("dve", lambda e: e.tensor_copy(out=QM[qs][64:128, :], in_=d["q_src64"]), [], [tQM[qs]])
                P.add("dve", lambda e: e.memset(QM[qs][64:96, :], 0.0), [], [tQM[qs]])
            else:
                P.add("dve", lambda e: e.tensor_copy(out=QM[qs][rows, :], in_=d["q_src"]), [], [tQM[qs]])

        def run_map(d, qs, acc):
            accb = (2 * acc, 2 * acc + 1)
            kT_fn, v_fn, scale = d["kT_fn"], d["v_fn"], d["scale"]
            pend = []
            for kb in range(NT):
                sp_ = cnt["s"] % 2
                pt_ = cnt["s"] % 3
                cnt["s"] += 1
                sb_ = (2 * sp_, 2 * sp_ + 1)
                for j in range(2):
                    mm(psb(sb_[j]), kT_fn(kb), QM[qs][:, j * 512:(j + 1) * 512], True, True, [tQM[qs]], [tPS[sb_[j]]])
                act(PT[pt_][:], PS[sp_][:], AF.Exp, [tPS[sb_[0]], tPS[sb_[1]]], [tPT[pt_]], scale=scale)
                pend.append((kb, pt_))
                if len(pend) > 2:
                    pk, ps_ = pend.pop(0)
                    for j in range(2):
                        mm(psb(accb[j]), v_fn(pk), PT[ps_][:, j * 512:(j + 1) * 512], pk == 0, pk == NT - 1, [tPT[ps_]], [tPS[accb[j]]])
                cnt["tick"] += 1
                run_due()
            for pk, ps_ in pend:
                for j in range(2):
                    mm(psb(accb[j]), v_fn(pk), PT[ps_][:, j * 512:(j + 1) * 512], pk == 0, pk == NT - 1, [tPT[ps_]], [tPS[accb[j]]])

        def tail_A(acc, hh, c, pp):
            dr = slice(64 * hh, 64 * hh + 64)
            nr = slice(64 * (1 - hh), 64 * (1 - hh) + 64)
            ta = [tPS[2 * acc], tPS[2 * acc + 1]]
            P.add("dve", lambda e: e.reciprocal(out=Rt[dr, :], in_=PS[acc][nr, :]), ta, [tR])
            if c == 0:
                tt_("dve", OP[dr, :], PS[acc][dr, :], Rt[dr, :], ALU.mult, ta + [tR], [tOP])
                return
            tt_("dve", Tt[dr, :], PS[acc][dr, :], Rt[dr, :], ALU.mult, ta + [tR], [tTt])
            stt_("dve", OP[dr, :], Tt[dr, :], neglam[dr, :], OP[dr, :], ALU.mult, ALU.add, [tTt, tOP], [tOP])
            if hh == 0:
                return
            tt_("pool", SQ[:], OP[:], OP[:], ALU.mult, [tOP], [tSQ])

            def st2():
                for j in range(2):
                    mm(psb(2 * acc + j), bdm[:], SQ[:, j * 512:(j + 1) * 512], True, True, [tSQ, tConst], [tPS[2 * acc + j]])
                defer(3, st3)

            def st3():
                act(Rt[:], PS[acc][:], AF.Sqrt, ta, [tR], bias=epsT[:, 0:1])
                defer(2, st4)

            def st4():
                P.add("dve", lambda e: e.reciprocal(out=Rt[:], in_=Rt[:]), [tR], [tR])
                stt_("dve", mTAB[:, pp, :], OP[:], gA2[:, 0:1], Rt[:], ALU.mult, ALU.mult, [tOP, tR, tPar], [tmT[pp]])
            defer(4, st2)

        def tail_B(acc, hh, cB):
            dr = slice(64 * hh, 64 * hh + 64)
            nr = slice(64 * (1 - hh), 64 * (1 - hh) + 64)
            ta = [tPS[2 * acc], tPS[2 * acc + 1]]
            P.add("dve", lambda e: e.reciprocal(out=Rt[dr, :], in_=PS[acc][nr, :]), ta, [tR])
            tt_("dve", mTAB[dr, 2 + cB, :], PS[acc][dr, :], Rt[dr, :], ALU.mult, ta + [tR], [tmT[2 + cB]])

        for qh in range(2):
            q0 = qh * 1024
            maps = []
            for pp in range(2):
                for hh in range(2):
                    h = 2 * pp + hh
                    for c in range(2):
                        r0 = (hh * 2 + c) * 32
                        maps.append(dict(
                            kT_fn=lambda kb, pp=pp: AkT[:, pp, kb * 128:(kb + 1) * 128],
                            q_src=AqT[r0:r0 + 32, pp, q0:q0 + 1024], rows=slice(r0, r0 + 32),
                            q_src64=AqT[64:128, pp, q0:q0 + 1024],
                            v_fn=lambda kb, h=h: VX[:, kb, h * 128:(h + 1) * 128],
                            scale=32 ** -0.5,
                            tail=lambda acc, hh=hh, c=c, pp=pp: tail_A(acc, hh, c, pp)))
            for cB in range(4):
                for hh in range(2):
                    hB = 2 * cB + hh
                    j_kv, g = hB // 4, hB % 4
                    rows = slice(64 * j_kv, 64 * j_kv + 64)
                    voff = 512 + j_kv * 192 + (64 if hh == 0 else 0)
                    maps.append(dict(
                        kT_fn=lambda kb: BkT[:, kb * 128:(kb + 1) * 128],
                        q_src=BqT[rows, g, q0:q0 + 1024], rows=rows, q_src64=None,
                        v_fn=lambda kb, voff=voff: VX[:, kb, voff:voff + 128],
                        scale=64 ** -0.5,
                        tail=lambda acc, hh=hh, cB=cB: tail_B(acc, hh, cB)))
            prep_qm(maps[0], cnt["m"] % 2)
            for i, d in enumerate(maps):
                qs = cnt["m"] % 2
                acc = 2 + (cnt["m"] % 2)
                cnt["m"] += 1
                if i + 1 < len(maps):
                    prep_qm(maps[i + 1], cnt["m"] % 2)
                cv_step()
                run_map(d, qs, acc)
                defer(3, lambda d=d, acc=acc: d["tail"](acc))
            run_due(force=True)
            for tl in range(8):
                tt = qh * 8 + tl
                acc = 2 + (tl % 2)
                for dg in range(2):
                    for ec in range(8):
                        if ec < 6:
                            lhs = mTAB[:, ec, tl * 128:(tl + 1) * 128]
                            rd = [tmT[ec], tWO]
                        else:
                            lhs = mTC[:, ec - 6, tt * 128:(tt + 1) * 128]
                            rd = [tWO]
                        mm(psb(2 * acc + dg), lhs, WOUT[:, ec, dg * 512:(dg + 1) * 512], ec == 0, ec == 7, rd, [tPS[2 * acc + dg]])
                ta = [tPS[2 * acc], tPS[2 * acc + 1]]
                Yx, tYx = (Yb, tY) if tl % 2 == 0 else (Tt, tTt)
                stt_("dve", Yx[:], X[:, tt, :], float(ALPHA), PS[acc][:], ALU.mult, ALU.add, ta + [tX[tt]], [tYx])
                layernorm_tail(Yx, tYx, tt, tsm, so=44 + 20 * (tl % 2))

    def layernorm_tail(Y, tY, tt, tsm, so=44):
        for j in range(2):
            P.add("dve", lambda e, j=j: e.bn_stats(out=sm[:, so + 6 * j:so + 6 + 6 * j], in_=Y[:, j * 512:(j + 1) * 512]), [tY], [tsm])
        P.add("dve", lambda e: e.bn_aggr(out=sm[:, so + 12:so + 14], in_=sm[:, so:so + 12].rearrange("p (a b) -> p a b", a=2)), [tsm], [tsm])
        rstd_chain(sm[:, so + 14:so + 15], sm[:, so + 13:so + 14], 1, 1.0, tsm)
        stt_("dve", sm[:, so + 15:so + 16], sm[:, so + 12:so + 13], -1.0, sm[:, so + 14:so + 15], ALU.mult, ALU.mult, [tsm], [tsm])
        act(X[:, tt, :], Y[:], AF.Identity, [tY, tsm], [tX[tt]], scale=sm[:, so + 14:so + 15], bias=sm[:, so + 15:so + 16])

    tG1T = T("g1T")

    def load_g1T(l):
        dma("sp", g1T[:], l1g_d[l].rearrange("(kc p) -> p kc", p=128), "g1t", [], [tG1T], slow=True)
        dma("sp", b1T[:], l1b_d[l].rearrange("(kc p) -> p kc", p=128), "g1t", [], [tG1T], slow=True)

    def phaseC(l, s, last, prefetch):
        cv_step(1000)
        tG1, tG2 = T("g1"), T("g2")
        tWG = [T("wg0"), T("wg1")]
        tWU = [T("wu0"), T("wu1")]
        tWD = [T("wd0"), T("wd1")]
        thid = [T("hid%d" % f) for f in range(NFC)]
        txTf = [T("xTf0"), T("xTf1")]
        tXB, tSG, tsm = T("XBc"), [T("sg0"), T("sg1")], T("smC")
        tYs = [T("Y0"), T("Y1"), T("Y2")]
        wgs = wg_d[l].rearrange("(kc p) f -> p kc f", p=128)
        wus = wu_d[l].rearrange("(kc p) f -> p kc f", p=128)
        wds = wd_d[l].rearrange("(fc p) d -> p fc d", p=128)
        seq = {"gu": 0, "d": 0}

        def load_gu(fb):
            sl = seq["gu"] % 2
            seq["gu"] += 1
            if USE_CV:
                dma("sp", WG[sl][:].rearrange("p a b -> p (a b)"), wg_bf[l, fb], "wg%d" % sl, [tCV[l]], [tWG[sl]])
                dma("sp", WU[sl][:].rearrange("p a b -> p (a b)"), wu_bf[l, fb], "wu%d" % sl, [tCV[l]], [tWU[sl]])
            else:
                dma("pool", WG[sl][:], wgs[:, :, fb * 256:(fb + 1) * 256], "wg%d" % sl, [], [tWG[sl]])
                dma("pool", WU[sl][:], wus[:, :, fb * 256:(fb + 1) * 256], "wu%d" % sl, [], [tWU[sl]])
            return sl

        def load_d(db):
            sl = seq["d"] % 2
            seq["d"] += 1
            n = 4 if db < 5 else 2
            if USE_CV:
                dma("sp", WD[sl][:, 0:n, :].rearrange("p a b -> p (a b)"), wd_bf[l, db, :, 0:n * 1024], "wd%d" % sl, [tCV[l]], [tWD[sl]])
            else:
                dma("pool", WD[sl][:, 0:n, :], wds[:, db * 4:db * 4 + n, :], "wd%d" % sl, [], [tWD[sl]])
            return sl

        def front(tg):
            xs = tg % 2
            for tl in range(4):
                tt = tg * 4 + tl
                cp("act", XBc[:], X[:, tt, :], [tX[tt]], [tXB])
                bank = 4 + tl
                for kc in range(8):
                    tr(psb_bf(bank)[:, kc * 128:(kc + 1) * 128], XBc[:, kc * 128:(kc + 1) * 128], [tXB], [tPS[bank]])
                for kc in range(8):
                    ts_("dve", xTf2[xs][:, kc, tl * 128:(tl + 1) * 128], psb_bf(bank)[:, kc * 128:(kc + 1) * 128],
                        g1T[:, kc:kc + 1], b1T[:, kc:kc + 1], ALU.mult, ALU.add, [tPS[bank], tG1T], [txTf[xs]])
                tt_("pool", X[:, tt, :], X[:, tt, :], g1_b[:], ALU.mult, [tX[tt], tG1], [tX[tt]])
                tt_("pool", X[:, tt, :], X[:, tt, :], b1_b[:], ALU.add, [tX[tt], tG1], [tX[tt]])

        def C1(tg, nxt):
            xs = tg % 2
            for fb in range(11):
                sl = nxt
                if fb + 1 < 11:
                    nxt = load_gu(fb + 1)
                for fci in range(2):
                    fc = 2 * fb + fci
                    gb = 0 + (fc % 2)
                    ub = 2 + (fc % 2)
                    for kc in range(8):
                        mm(psb(gb), WG[sl][:, kc, fci * 128:(fci + 1) * 128], xTf2[xs][:, kc, :], kc == 0, kc == 7, [tWG[sl], txTf[xs]], [tPS[gb]])
                    for kc in range(8):
                        mm(psb(ub), WU[sl][:, kc, fci * 128:(fci + 1) * 128], xTf2[xs][:, kc, :], kc == 0, kc == 7, [tWU[sl], txTf[xs]], [tPS[ub]])
                    act(SG[fc % 2][:], psb(gb), AF.Silu, [tPS[gb]], [tSG[fc % 2]])
                    tt_("dve", hidT[:, fc, :], psb(ub), SG[fc % 2][:], ALU.mult, [tPS[ub], tSG[fc % 2]], [thid[fc]])

        def C2(tg):
            nd = load_d(0)
            for db in range(6):
                sl = nd
                if db + 1 < 6:
                    nd = load_d(db + 1)
                n = 4 if db < 5 else 2
                for fci in range(n):
                    fc = db * 4 + fci
                    for tl in range(4):
                        for dg in range(2):
                            mm(psb(2 * tl + dg), hidT[:, fc, tl * 128:(tl + 1) * 128], WD[sl][:, fci, dg * 512:(dg + 1) * 512],
                               fc == 0, fc == NFC - 1, [thid[fc], tWD[sl]], [tPS[2 * tl + dg]])

        def LN2(tg):
            def evac(tl):
                tt = tg * 4 + tl
                ta = [tPS[2 * tl], tPS[2 * tl + 1]]
                stt_("dve", Ys[tl % 3][:], X[:, tt, :], float(ALPHA), PS[tl][:], ALU.mult, ALU.add, ta + [tX[tt]], [tYs[tl % 3]])

            def tail(tl):
                tt = tg * 4 + tl
                layernorm_tail(Ys[tl % 3], tYs[tl % 3], tt, tsm)
                tt_("pool", X[:, tt, :], X[:, tt, :], g2_b[:], ALU.mult, [tX[tt], tG2], [tX[tt]])
                tt_("pool", X[:, tt, :], X[:, tt, :], b2_b[:], ALU.add, [tX[tt], tG2], [tX[tt]])
                if last:
                    dma("sp", out_d[s, tt * 128:(tt + 1) * 128, :], X[:, tt, :], "out", [tX[tt]], [])
            evac(0)
            evac(1)
            evac(2)
            tail(0)
            evac(3)
            tail(1)
            tail(2)
            tail(3)

        nxt = load_gu(0)
        gq_ = "pool" if USE_CV else "sp"
        dma(gq_, g1_b[:], l1g_d[l:l + 1, :].broadcast_to([128, D]), "lng", [], [tG1])
        dma(gq_, b1_b[:], l1b_d[l:l + 1, :].broadcast_to([128, D]), "lng", [], [tG1])
        dma(gq_, g2_b[:], l2g_d[l:l + 1, :].broadcast_to([128, D]), "lng2", [], [tG2])
        dma(gq_, b2_b[:], l2b_d[l:l + 1, :].broadcast_to([128, D]), "lng2", [], [tG2])
        front(0)
        for tg in range(4):
            C1(tg, nxt)
            if tg == 0:
                prefetch()
            if tg + 1 < 4:
                nxt = load_gu(0)
                front(tg + 1)
            C2(tg)
            LN2(tg)

    load_win(0, 0)
    load_tables()
    load_params(0)
    if USE_CV:
        convert_layer(0)
    for s in range(nseq):
        if s > 0:
            P.barrier()
        for q in range(4):
            dma("sp", X[:, q * 4:(q + 1) * 4, :], x_d[s, q * 512:(q + 1) * 512, :].rearrange("(t p) d -> p t d", p=128), "x%d" % q,
                [], [tX[q * 4 + i] for i in range(4)])
        def dbg_store():
            P.barrier()
            for tt in range(NT):
                dma("sp", out_d[s, tt * 128:(tt + 1) * 128, :], X[:, tt, :], "out", [tX[tt]], [])
        for l in range(L):
            P.barrier()
            if stop == "load":
                dbg_store()
                break
            phaseA(l)
            P.barrier()
            if stop == "A":
                dbg_store()
                break
            if USE_CV and s == 0 and l + 1 < L:
                convert_layer(l + 1)
            load_g1T(l)
            phaseB(l)
            P.barrier()
            if stop == "B":
                dbg_store()
                break
            nl = l + 1 if l + 1 < L else (0 if s + 1 < nseq else None)

            def prefetch(nl=nl):
                if nl is not None:
                    load_win(nl, 0)
                    load_tables("pool")
                    load_params(nl, "pool")
            phaseC(l, s, l == L - 1, prefetch)
    P.barrier()
    P.finalize_and_emit(st)
    st.close()
    return nc, P


_CACHE = {}


def kernel(**inputs):
    n_cores = 8
    nseq = 32 // n_cores
    depth = 4
    if "nc" not in _CACHE:
        _CACHE["nc"] = build(nseq, depth)[0]
    nc = _CACHE["nc"]
    consts = host_consts()
    maps = []
    for c in range(n_cores):
        m = {k: np.ascontiguousarray(np.asarray(v, dtype=np.float32)) for k, v in inputs.items() if k != "x"}
        m["x"] = np.ascontiguousarray(np.asarray(inputs["x"], dtype=np.float32)[c * nseq:(c + 1) * nseq])
        m.update(consts)
        maps.append(m)
    res = run_bass_kernel_spmd(nc, maps, core_ids=list(range(n_cores)))
    return np.concatenate([np.asarray(r["out"]) for r in res.results], axis=0).astype(np.float32)
```
